# Optimizing a Trainium2 kernel written in Bass

```python
import math
import jax
import jax.numpy as jnp
from jax import lax
import numpy as np

D_MODEL = 2048
BATCH = 2
SEQ = 16384
DEPTH = 4

GRID_W = 64
CTX_LEN = 256
MIX_WIDTH = D_MODEL
ATTN_WIDTH = MIX_WIDTH // 2
HYENA_WIDTH = MIX_WIDTH // 4
POOL_WIDTH = MIX_WIDTH - ATTN_WIDTH - HYENA_WIDTH
HEAD_DIM = 128
N_HEADS = ATTN_WIDTH // HEAD_DIM
N_KV_HEADS = 2
KV_GROUP = N_HEADS // N_KV_HEADS
KV_WIDTH = N_KV_HEADS * HEAD_DIM
WINDOW = 128
BLOCK = 128
ROPE_BASE = 10000.0
HYENA_ORDER = 2
HYENA_SHORT = 3
HYENA_EMB_DIM = 33
HYENA_FILTER_HIDDEN = 64
HYENA_FAST_DECAY_PCT = 0.3
HYENA_SLOW_DECAY_PCT = 1.5
HYENA_DECAY_TARGET = 1e-2
POOL_WINDOWS = (2, 4, 8, 16)
POOL_GROUP = POOL_WIDTH // len(POOL_WINDOWS)
MLP_HIDDEN = 4 * D_MODEL
N_MOD = 6
EPS = 1e-6
NEG_INF = -1e30

Q_END = ATTN_WIDTH
K_END = Q_END + KV_WIDTH
V_END = K_END + KV_WIDTH
HY_END = V_END + (HYENA_ORDER + 1) * HYENA_WIDTH
IN_WIDTH = HY_END + POOL_WIDTH

kernel_name = "hybrid_dit_swa_hyena_pool_trunk"


def rmsnorm(x, g):
    xf = x.astype(jnp.float32)
    y = xf * lax.rsqrt(jnp.mean(xf * xf, axis=-1, keepdims=True) + EPS)
    return (y * g.astype(jnp.float32)).astype(x.dtype)


def modulate(h, shift, scale):
    return h * (1 + scale) + shift


def axial_rope(x, row, col):
    half = HEAD_DIM // 2
    quarter = half // 2
    inv_freq = ROPE_BASE ** (-jnp.arange(quarter, dtype=jnp.float32) / quarter)

    def rotate(xh, pos):
        ang = pos.astype(jnp.float32)[:, None] * inv_freq[None, :]
        cos = jnp.cos(ang)[None, :, None, :].astype(xh.dtype)
        sin = jnp.sin(ang)[None, :, None, :].astype(xh.dtype)
        x1, x2 = xh[..., :quarter], xh[..., quarter:]
        return jnp.concatenate([x1 * cos - x2 * sin, x2 * cos + x1 * sin], axis=-1)

    return jnp.concatenate([rotate(x[..., :half], row), rotate(x[..., half:], col)], axis=-1)


def sink_softmax(s, sink):
    m = jnp.maximum(jnp.max(s, axis=-1, keepdims=True), sink)
    p = jnp.exp(s - m)
    return p / (jnp.sum(p, axis=-1, keepdims=True) + jnp.exp(sink - m))


def context_attention(q, k, v, sink):
    b, n = q.shape[:2]
    qg = q.reshape(b, n, N_KV_HEADS, KV_GROUP, HEAD_DIM)
    s = jnp.einsum('bqkgd,bskd->bkgqs', qg, k).astype(jnp.float32) * HEAD_DIM ** -0.5
    p = sink_softmax(s, sink.astype(jnp.float32).reshape(1, N_KV_HEADS, KV_GROUP, 1, 1))
    o = jnp.einsum('bkgqs,bskd->bqkgd', p.astype(v.dtype), v)
    return o.reshape(b, n, ATTN_WIDTH)


def window_attention(q, k, v, k_ctx, v_ctx, sink):
    b, n = q.shape[:2]
    nb = n // BLOCK
    qb = q.reshape(b, nb, BLOCK, N_KV_HEADS, KV_GROUP, HEAD_DIM)

    def band(t):
        tp = jnp.pad(t, ((0, 0), (BLOCK, BLOCK), (0, 0), (0, 0))).reshape(b, nb + 2, BLOCK, N_KV_HEADS, HEAD_DIM)
        return jnp.concatenate([tp[:, :-2], tp[:, 1:-1], tp[:, 2:]], axis=2)

    kb, vb = band(k), band(v)
    scale = HEAD_DIM ** -0.5
    s_loc = jnp.einsum('bnqkgd,bnskd->bnkgqs', qb, kb).astype(jnp.float32) * scale
    blk = jnp.arange(nb)[:, None, None]
    qi = jnp.arange(BLOCK)[None, :, None]
    sj = jnp.arange(3 * BLOCK)[None, None, :]
    kpos = (blk - 1) * BLOCK + sj
    valid = (jnp.abs(sj - BLOCK - qi) <= WINDOW) & (kpos >= 0) & (kpos < n)
    s_loc = jnp.where(valid[None, :, None, None], s_loc, NEG_INF)
    s_ctx = jnp.einsum('bnqkgd,bskd->bnkgqs', qb, k_ctx).astype(jnp.float32) * scale
    p = sink_softmax(jnp.concatenate([s_loc, s_ctx], axis=-1),
                     sink.astype(jnp.float32).reshape(1, 1, N_KV_HEADS, KV_GROUP, 1, 1))
    p = p.astype(v.dtype)
    o = (jnp.einsum('bnkgqs,bnskd->bnqkgd', p[..., :3 * BLOCK], vb)
         + jnp.einsum('bnkgqs,bskd->bnqkgd', p[..., 3 * BLOCK:], v_ctx))
    return o.reshape(b, n, ATTN_WIDTH)


def short_conv(u, w, bias):
    n = u.shape[1]
    up = jnp.pad(u, ((0, 0), (1, 1), (0, 0)))
    return up[:, :n] * w[0] + up[:, 1:n + 1] * w[1] + up[:, 2:] * w[2] + bias


def hyena_filter(n, w1, b1, f1, w2, b2, f2, w3):
    t = jnp.linspace(0.0, 1.0, n, dtype=jnp.float32)[:, None]
    bands = (HYENA_EMB_DIM - 1) // 2
    omega = 2.0 * math.pi * jnp.arange(n, dtype=jnp.float32)[:, None] / n
    f = jnp.linspace(1e-4, bands - 1, bands, dtype=jnp.float32)[None, :]
    feats = jnp.concatenate([t, jnp.cos(f * omega), -jnp.sin(f * omega)], axis=-1).astype(w1.dtype)
    h = jnp.sin(f1 * (feats @ w1 + b1))
    h = jnp.sin(f2 * (h @ w2 + b2))
    h = (h @ w3).astype(jnp.float32).reshape(n, 2, HYENA_WIDTH)
    max_decay = math.log(HYENA_DECAY_TARGET) / HYENA_FAST_DECAY_PCT
    min_decay = math.log(HYENA_DECAY_TARGET) / HYENA_SLOW_DECAY_PCT
    deltas = jnp.abs(jnp.linspace(min_decay, max_decay, HYENA_WIDTH, dtype=jnp.float32))
    h = h * jnp.exp(-t[:, :, None] * deltas)
    h = h / jnp.sum(jnp.abs(h), axis=(0, 1), keepdims=True)
    h_fwd, h_bwd = h[:, 0], h[:, 1]
    k = jnp.concatenate([h_fwd, jnp.zeros((1, HYENA_WIDTH), jnp.float32), h_bwd[:0:-1]], axis=0)
    return k.at[0].add(h_bwd[0])


def fft_long_conv(u, k):
    n = u.shape[1]
    uf = jnp.fft.rfft(u.astype(jnp.float32), n=2 * n, axis=1)
    kf = jnp.fft.rfft(k, axis=0)
    y = jnp.fft.irfft(uf * kf[None], n=2 * n, axis=1)[:, :n]
    return y.astype(u.dtype)


def hyena_mixer(u, conv_w, conv_b, w1, b1, f1, w2, b2, f2, w3, bias):
    n = u.shape[1]
    z = short_conv(u, conv_w, conv_b)
    x0, x1, v = jnp.split(z, HYENA_ORDER + 1, axis=-1)
    k = hyena_filter(n, w1, b1, f1, w2, b2, f2, w3)
    v = v * x1
    y = fft_long_conv(v, k) + v * bias
    return y * x0


def pool_mixer(p, w_pool, scale):
    b, n, _ = p.shape
    pf = p.astype(jnp.float32)
    t = jnp.arange(n)
    outs = []
    for g, w in enumerate(POOL_WINDOWS):
        xg = pf[..., g * POOL_GROUP:(g + 1) * POOL_GROUP]
        h = w // 2
        cs = jnp.pad(jnp.cumsum(xg, axis=1), ((0, 0), (1, 0), (0, 0)))
        cs = jnp.pad(cs, ((0, 0), (h, h), (0, 0)), mode='edge')
        total = cs[:, 2 * h:2 * h + n] - cs[:, :n]
        count = (jnp.minimum(t + h, n) - jnp.maximum(t - h, 0)).astype(jnp.float32)
        outs.append(total / count[None, :, None] - xg)
    y = jnp.stack(outs, axis=2).astype(p.dtype)
    y = jnp.einsum('blgc,gcd->blgd', y, w_pool).reshape(b, n, POOL_WIDTH)
    return y * scale


def mix_branches(p, attn_out, hy_params, w_pool, pool_scale, g_branch, w_out):
    hy = hyena_mixer(p[..., V_END:HY_END], *hy_params)
    po = pool_mixer(p[..., HY_END:], w_pool, pool_scale)
    merged = jnp.concatenate([
        rmsnorm(attn_out, g_branch[:ATTN_WIDTH]),
        rmsnorm(hy, g_branch[ATTN_WIDTH:ATTN_WIDTH + HYENA_WIDTH]),
        rmsnorm(po, g_branch[ATTN_WIDTH + HYENA_WIDTH:]),
    ], axis=-1)
    return merged @ w_out


def sq_relu_mlp(h, w_up, w_down):
    return jnp.square(jax.nn.relu(h @ w_up)) @ w_down


def setup_inputs(seed: int = 0) -> dict:
    key = jax.random.key(seed)
    ks = jax.random.split(key, 32)
    f32 = jnp.float32

    def nrm(k, shape, s):
        return jax.random.normal(k, shape, f32) * s

    def gain(k, shape):
        return 1.0 + 0.05 * jax.random.normal(k, shape, f32)

    D, L = D_MODEL, DEPTH
    return {
        "x": nrm(ks[0], (BATCH, SEQ, D), 1.0),
        "c": nrm(ks[1], (BATCH, D), 1.0),
        "ctx": nrm(ks[2], (BATCH, CTX_LEN, D), 1.0),
        "c_ctx": nrm(ks[3], (D,), 1.0),
        "w_mod": nrm(ks[4], (L, D, N_MOD * D), D ** -0.5),
        "b_mod": nrm(ks[5], (L, N_MOD * D), 0.02),
        "g_pre_mix": gain(ks[6], (L, D)),
        "g_post_mix": gain(ks[7], (L, D)),
        "g_pre_mlp": gain(ks[8], (L, D)),
        "g_post_mlp": gain(ks[9], (L, D)),
        "w_in": nrm(ks[10], (L, D, IN_WIDTH), D ** -0.5),
        "w_out": nrm(ks[11], (L, MIX_WIDTH, D), MIX_WIDTH ** -0.5),
        "g_branch": gain(ks[12], (L, MIX_WIDTH)),
        "attn_sink": nrm(ks[13], (L, N_HEADS), 0.5),
        "hy_conv_w": nrm(ks[14], (L, HYENA_SHORT, (HYENA_ORDER + 1) * HYENA_WIDTH), HYENA_SHORT ** -0.5),
        "hy_conv_b": nrm(ks[15], (L, (HYENA_ORDER + 1) * HYENA_WIDTH), 0.02),
        "hy_w1": nrm(ks[16], (L, HYENA_EMB_DIM, HYENA_FILTER_HIDDEN), HYENA_EMB_DIM ** -0.5),
        "hy_b1": nrm(ks[17], (L, HYENA_FILTER_HIDDEN), 0.02),
        "hy_freq1": gain(ks[18], (L, HYENA_FILTER_HIDDEN)),
        "hy_w2": nrm(ks[19], (L, HYENA_FILTER_HIDDEN, HYENA_FILTER_HIDDEN), HYENA_FILTER_HIDDEN ** -0.5),
        "hy_b2": nrm(ks[20], (L, HYENA_FILTER_HIDDEN), 0.02),
        "hy_freq2": gain(ks[21], (L, HYENA_FILTER_HIDDEN)),
        "hy_w3": nrm(ks[22], (L, HYENA_FILTER_HIDDEN, 2 * (HYENA_ORDER - 1) * HYENA_WIDTH), HYENA_FILTER_HIDDEN ** -0.5),
        "hy_bias": nrm(ks[23], (L, HYENA_WIDTH), 1.0),
        "pool_w": nrm(ks[24], (L, len(POOL_WINDOWS), POOL_GROUP, POOL_GROUP), POOL_GROUP ** -0.5),
        "pool_scale": 1.0 + 0.1 * jax.random.normal(ks[25], (L, POOL_WIDTH), f32),
        "w_up": nrm(ks[26], (L, D, MLP_HIDDEN), D ** -0.5),
        "w_down": nrm(ks[27], (L, MLP_HIDDEN, D), MLP_HIDDEN ** -0.5),
    }


def reference(x, c, ctx, c_ctx, w_mod, b_mod, g_pre_mix, g_post_mix, g_pre_mlp, g_post_mlp,
              w_in, w_out, g_branch, attn_sink, hy_conv_w, hy_conv_b, hy_w1, hy_b1, hy_freq1,
              hy_w2, hy_b2, hy_freq2, hy_w3, hy_bias, pool_w, pool_scale, w_up, w_down):
    b, n, _ = x.shape
    rows = n // GRID_W
    row = jnp.repeat(jnp.arange(rows, dtype=jnp.int32), GRID_W)
    col = jnp.tile(jnp.arange(GRID_W, dtype=jnp.int32), rows)
    for i in range(DEPTH):
        last = i == DEPTH - 1
        hy_params = (hy_conv_w[i], hy_conv_b[i], hy_w1[i], hy_b1[i], hy_freq1[i],
                     hy_w2[i], hy_b2[i], hy_freq2[i], hy_w3[i], hy_bias[i])
        mod_x = jax.nn.silu(c) @ w_mod[i] + b_mod[i]
        mod_c = jax.nn.silu(c_ctx) @ w_mod[i] + b_mod[i]
        sh1, sc1, gt1, sh2, sc2, gt2 = jnp.split(mod_x[:, None, :], N_MOD, axis=-1)
        csh1, csc1, cgt1, csh2, csc2, cgt2 = jnp.split(mod_c, N_MOD, axis=-1)

        hx = modulate(rmsnorm(x, g_pre_mix[i]), sh1, sc1)
        hc = modulate(rmsnorm(ctx, g_pre_mix[i]), csh1, csc1)
        px = hx @ w_in[i]
        if last:
            kv_ctx = hc @ w_in[i][:, Q_END:V_END]
        else:
            pc = hc @ w_in[i]
            kv_ctx = pc[..., Q_END:V_END]
        k_ctx = kv_ctx[..., :KV_WIDTH].reshape(b, -1, N_KV_HEADS, HEAD_DIM)
        v_ctx = kv_ctx[..., KV_WIDTH:].reshape(b, -1, N_KV_HEADS, HEAD_DIM)

        q = axial_rope(px[..., :Q_END].reshape(b, n, N_HEADS, HEAD_DIM), row, col)
        k = axial_rope(px[..., Q_END:K_END].reshape(b, n, N_KV_HEADS, HEAD_DIM), row, col)
        v = px[..., K_END:V_END].reshape(b, n, N_KV_HEADS, HEAD_DIM)
        attn_x = window_attention(q, k, v, k_ctx, v_ctx, attn_sink[i])
        ox = mix_branches(px, attn_x, hy_params, pool_w[i], pool_scale[i], g_branch[i], w_out[i])
        x = x + gt1 * rmsnorm(ox, g_post_mix[i])

        hx2 = modulate(rmsnorm(x, g_pre_mlp[i]), sh2, sc2)
        x = x + gt2 * rmsnorm(sq_relu_mlp(hx2, w_up[i], w_down[i]), g_post_mlp[i])

        if not last:
            q_ctx = pc[..., :Q_END].reshape(b, -1, N_HEADS, HEAD_DIM)
            attn_c = context_attention(q_ctx, k_ctx, v_ctx, attn_sink[i])
            oc = mix_branches(pc, attn_c, hy_params, pool_w[i], pool_scale[i], g_branch[i], w_out[i])
            ctx = ctx + cgt1 * rmsnorm(oc, g_post_mix[i])
            hc2 = modulate(rmsnorm(ctx, g_pre_mlp[i]), csh2, csc2)
            ctx = ctx + cgt2 * rmsnorm(sq_relu_mlp(hc2, w_up[i], w_down[i]), g_post_mlp[i])
    return x
```

```python
import numpy as np
import ml_dtypes
import concourse.bass as bass
import concourse.mybir as mybir
from concourse.bass_utils import run_bass_kernel_spmd

F32 = mybir.dt.float32
BF16 = mybir.dt.bfloat16
ALU = mybir.AluOpType
AF = mybir.ActivationFunctionType
AX = mybir.AxisListType

EPOCH = 20000


class Buf:
    __slots__ = ("name", "w", "r", "parent", "children", "sem", "cnt")

    def __init__(self, name, parent=None):
        self.name = name
        self.w = None
        self.r = {}
        self.parent = parent
        self.children = {}
        self.sem = None
        self.cnt = 0

    def part(self, key):
        c = self.children.get(key)
        if c is None:
            c = Buf(f"{self.name}.{key}", parent=self)
            self.children[key] = c
        return c


class Op:
    __slots__ = ("eng", "fn", "deps", "is_dma", "sem", "val", "needs_inc", "idx", "owner")

    def __init__(self, eng, fn, is_dma, owner):
        self.eng = eng
        self.fn = fn
        self.deps = []
        self.is_dma = is_dma
        self.owner = owner
        self.sem = None
        self.val = 0
        self.needs_inc = False


class Prog:
    ENGS = ("sync", "scalar", "vector", "gpsimd", "tensor")
    UID = 0

    def __init__(self, nc):
        self.nc = nc
        self.ops = {e: [] for e in self.ENGS}
        self.nops = 0
        self.all_ops = []
        self.final = []
        Prog.UID += 1
        self.uid = Prog.UID

    def buf(self, name):
        return Buf(name)

    def _nodes(self, b):
        nodes = [b]
        if b.children:
            nodes += list(b.children.values())
        if b.parent is not None:
            nodes.append(b.parent)
        return nodes

    def op(self, eng, fn, reads=(), writes=(), dma=False, owner=None):
        o = Op(eng, fn, dma, owner)
        deps = {}

        def add(d):
            if d is None:
                return
            if d.eng == eng and eng == "tensor":
                return
            deps[id(d)] = d

        for b in reads:
            for n in self._nodes(b):
                add(n.w)
        for b in writes:
            for n in self._nodes(b):
                add(n.w)
                for rd in n.r.values():
                    for x in rd:
                        add(x)
        best = {}
        out = []
        for d in deps.values():
            if d.is_dma:
                out.append(d)
            else:
                cur = best.get(d.eng)
                if cur is None or d.idx > cur.idx:
                    best[d.eng] = d
        out += list(best.values())
        for d in out:
            d.needs_inc = True
        o.deps = out
        o.idx = len(self.ops[eng])
        self.ops[eng].append(o)
        self.all_ops.append(o)
        self.nops += 1
        for b in reads:
            lst = b.r.setdefault(eng, [])
            if dma:
                lst.append(o)
            else:
                lst[:] = [o]
        for b in writes:
            b.w = o
            b.r = {}
            for c in b.children.values():
                c.w = o
                c.r = {}
        if dma:
            assert owner is not None
            o.needs_inc = True
        return o

    def dma(self, eng, out, in_, reads, writes, owner, **kw):
        return self.op(eng, lambda e: e.dma_start(out=out, in_=in_, **kw), reads, writes, dma=True, owner=owner)

    def emit(self, final_wait_bufs=()):
        nc = self.nc
        import contextlib
        stack = contextlib.ExitStack()
        eng_sems = {}
        dsems = []
        for o in self.all_ops:
            if o.is_dma:
                b = o.owner
                if b.sem is None:
                    b.sem = nc.alloc_semaphore(name=f"d{self.uid}_{len(dsems)}")
                    dsems.append(b.sem)
                b.cnt += 16
                o.sem = b.sem
                o.val = b.cnt
        for e in self.ENGS:
            cnt = 0
            for o in self.ops[e]:
                if (not o.is_dma) and o.needs_inc:
                    ep = cnt // EPOCH
                    key = (e, ep)
                    if key not in eng_sems:
                        eng_sems[key] = nc.alloc_semaphore(name=f"e{self.uid}_{e}_{ep}")
                    o.sem = eng_sems[key]
                    o.val = cnt % EPOCH + 1
                    cnt += 1
        self.nsems = len(eng_sems)
        final = []
        for o in self.all_ops:
            if o.is_dma:
                final.append((o.sem, o.val))
        with stack, nc.Block() as block:
            def replay(ename, e, extra=()):
                waited = {}
                for o in self.ops[ename]:
                    for d in o.deps:
                        k = id(d.sem)
                        if waited.get(k, 0) >= d.val:
                            continue
                        waited[k] = d.val
                        e.wait_ge(d.sem, d.val)
                    ins = o.fn(e)
                    if o.needs_inc:
                        ins.then_inc(o.sem, 16 if o.is_dma else 1)
                fin = {}
                for (s, v) in extra:
                    if fin.get(id(s), (None, 0))[1] < v:
                        fin[id(s)] = (s, v)
                for (s, v) in fin.values():
                    e.wait_ge(s, v)

            @block.sync
            def _(e):
                replay("sync", e)

            @block.scalar
            def _(e):
                replay("scalar", e)

            @block.vector
            def _(e):
                replay("vector", e)

            @block.gpsimd
            def _(e):
                replay("gpsimd", e, extra=final)

            @block.tensor
            def _(e):
                replay("tensor", e)


def bf16_np(a):
    return np.asarray(a).astype(ml_dtypes.bfloat16)


D = 2048
SEQ = 16384
CTX = 256
GRID_W = 64
NCORE = 8
CHUNK = 4096
BF = ml_dtypes.bfloat16


def rope_tables(tok0, ntok_x=CHUNK, nctx=CTX):
    t = np.arange(tok0, tok0 + ntok_x)
    row = (t // GRID_W).astype(np.float32)
    col = (t % GRID_W).astype(np.float32)
    quarter = 32
    inv_freq = (10000.0 ** (-np.arange(quarter, dtype=np.float32) / quarter)).astype(np.float32)
    C = np.ones((128, ntok_x + nctx), np.float32)
    S = np.zeros((128, ntok_x + nctx), np.float32)
    for d in range(128):
        pos = row if d < 64 else col
        ang = (pos * inv_freq[d % 32]).astype(np.float32)
        C[d, :ntok_x] = np.cos(ang)
        s = np.sin(ang)
        S[d, :ntok_x] = -s if (d % 64) < 32 else s
    return C, S


def rot_perm():
    Pm = np.zeros((128, 128), np.float32)
    for m in range(128):
        partner = m + 32 if (m % 64) < 32 else m - 32
        Pm[partner, m] = 1.0
    return Pm.astype(BF)


def hy_col_perm():
    idx = np.zeros(1536, np.int64)
    for r in range(8):
        for part in range(3):
            for c in range(64):
                idx[r * 192 + part * 64 + c] = part * 512 + r * 64 + c
    return idx


def w_in_cols():
    hp = hy_col_perm()
    return np.concatenate([np.arange(0, 1536), 1536 + hp, np.arange(3072, 3584)])


def attn_masks(chunk_j, nchunks=4):
    s = np.arange(128)[:, None]
    q = np.arange(128)[None, :]
    NEG = -30000.0
    prev = np.where(s >= q, 0.0, NEG)
    nxt = np.where(s <= q, 0.0, NEG)
    allneg = np.full((128, 128), NEG)
    m = [prev, nxt, allneg if chunk_j == 0 else prev, allneg if chunk_j == nchunks - 1 else nxt]
    return np.stack([np.tile(a, (1, 4)) for a in m]).astype(BF)


def pool_inv_counts(tok0, ntok, n):
    t = np.arange(tok0, tok0 + ntok)
    out = np.zeros((4, ntok), np.float32)
    for g, w in enumerate((2, 4, 8, 16)):
        h = w // 2
        cnt = (np.minimum(t + h, n) - np.maximum(t - h, 0)).astype(np.float32)
        out[g] = 1.0 / cnt
    return out


def halo_cat(parts, j, axis, halo, zero_like):
    own = parts[j]
    def take(a, sl):
        idx = [slice(None)] * a.ndim
        idx[axis] = sl
        return a[tuple(idx)]
    zshape = list(own.shape); zshape[axis] = halo
    z = np.zeros(zshape, own.dtype)
    left = take(parts[j - 1], slice(parts[j - 1].shape[axis] - halo, None)) if j > 0 else z
    right = take(parts[j + 1], slice(0, halo)) if j < len(parts) - 1 else z
    return np.concatenate([left, own, right], axis=axis)


def hyena_tables(n, core, width=512):
    m = np.arange(2 * n)
    tau = np.where(m < n, np.where(m == 0, 0, n - m), m - n)
    t_all = np.linspace(0.0, 1.0, n, dtype=np.float32)
    bands = 16
    omega_all = (2.0 * np.pi * np.arange(n, dtype=np.float32) / n).astype(np.float32)
    f = np.linspace(1e-4, bands - 1, bands, dtype=np.float32)[None, :]
    fo = (f * omega_all[:, None]).astype(np.float32)
    feats_all = np.concatenate([t_all[:, None], np.cos(fo), -np.sin(fo)], axis=-1).astype(np.float32)
    ft = np.ascontiguousarray(feats_all[tau].T)
    max_decay = np.log(1e-2) / 0.3
    min_decay = np.log(1e-2) / 1.5
    deltas = np.abs(np.linspace(min_decay, max_decay, width, dtype=np.float32))[core * 64:(core + 1) * 64]
    dec = np.exp(-t_all[tau][None, :] * deltas[:, None]).astype(np.float32)
    return ft, dec


import contextlib, os

D = 2048
NTX = 32
NTC = 2
NT = NTX + NTC
NTOK = NT * 128
EPS = 1e-6


class K:
    def __init__(self, nc, drams, outs):
        self.nc = nc
        self.P = Prog(nc)
        self.stack = contextlib.ExitStack()
        self.drams = drams
        self.outnames = outs

    def d(self, name):
        ap = self.drams[name]
        return ap, self.P.buf(name)

    def sb(self, name, shape, dt):
        t = self.stack.enter_context(self.nc.sbuf_tensor(f"p{self.P.uid}_{name}", list(shape), dt))
        return t, self.P.buf(name)

    def ps(self, name, shape, dt):
        t = self.stack.enter_context(self.nc.psum_tensor(f"p{self.P.uid}_{name}", list(shape), dt))
        return t, self.P.buf(name)


class Program:
    def __init__(self):
        self.nc = bass.Bass("TRN2", target_bir_lowering=False)
        self.drams = {}
        self.outs = []
        self.nphase = 0

    def dram(self, name, shape, dt, kind=None):
        if kind:
            t = self.nc.dram_tensor(name, list(shape), dt, kind=kind)
        else:
            t = self.nc.dram_tensor(name, list(shape), dt)
        self.drams[name] = t.ap()
        if kind == "ExternalOutput":
            self.outs.append(name)
        return self.drams[name]

    def phase(self, fn, *args, **kw):
        nc = self.nc
        self.nphase += 1
        with nc.cleanup_on_exit():
            k = K(nc, self.drams, self.outs)
            outbufs = fn(k, *args, **kw) or []
            with k.stack:
                k.P.emit(final_wait_bufs=outbufs)
            nc.all_engine_barrier()


def load_fm(P, dst, ap1d, off, srcb, dstb, n=D, eng="sync"):
    for kk in range(n // 128):
        src = bass.AP(ap1d.tensor, ap1d.offset + off + kk * 128, [[1, 128], [1, 1]])
        P.dma(eng, dst[:, kk:kk + 1], src, [srcb], [dstb], dstb)


def vec_bc(ap1d, off, n=D):
    return bass.AP(ap1d.tensor, ap1d.offset + off, [[0, 128], [1, n]])


def rstd_from_ss(P, ss, ssb, rstd, rstdb, width, n=1):
    P.op("vector", lambda e: e.tensor_scalar(out=rstd, in0=ss, scalar1=1.0 / width, scalar2=EPS, op0=ALU.mult, op1=ALU.add),
         [ssb], [rstdb])
    P.op("scalar", lambda e: e.activation(out=rstd, in_=rstd, func=AF.Ln), [rstdb], [rstdb])
    P.op("scalar", lambda e: e.activation(out=rstd, in_=rstd, func=AF.Exp, scale=-0.5), [rstdb], [rstdb])


def pa_drams(pr, l):
    pr.dram("xin", [NTOK, D], F32, "ExternalInput")
    pr.dram("w32", [D, 3584], F32, "ExternalInput")
    pr.dram("w_bf", [D, 3584], BF16)
    pr.dram("modx", [6 * D], F32, "ExternalInput")
    pr.dram("modc", [6 * D], F32, "ExternalInput")
    pr.dram("gpre", [D], F32, "ExternalInput")
    pr.dram("ropeC", [128, NTOK], F32, "ExternalInput")
    pr.dram("ropeS", [128, NTOK], F32, "ExternalInput")
    pr.dram("prot", [128, 128], BF16, "ExternalInput")
    pr.dram("ident", [128, 128], F32, "ExternalInput")
    pr.dram("qT", [128, 8, NTOK], BF16, "ExternalOutput")
    pr.dram("kT", [128, 2, NTOK], BF16, "ExternalOutput")
    pr.dram("v", [NTOK, 256], BF16, "ExternalOutput")
    pr.dram("zT", [1536, NTOK], BF16, "ExternalOutput")
    pr.dram("pT", [512, NTOK], BF16, "ExternalOutput")


def build_pa():
    pr = Program()
    pa_drams(pr, 0)
    pr.phase(cast_phase, "w32", "w_bf", D, 3584)
    pr.phase(proj_phase)
    return pr.nc


def cast_phase(k, src, dst, rows, cols):
    P = k.P
    s, sb_ = k.d(src)
    d, db_ = k.d(dst)
    n = 8
    step = rows // n
    for c in range(n):
        P.dma("gpsimd", d[c * step:(c + 1) * step, :], s[c * step:(c + 1) * step, :], [sb_], [db_.part(c)], db_.part(c))


def proj_phase(k, xin="xin", w="w_bf", modx="modx", modc="modc", gpre="gpre", ropeC="ropeC", ropeS="ropeS",
               prot="prot", ident="ident", qT="qT", kT="kT", vo="v", zT="zT", pT="pT"):
    xin, xinb = k.d(xin); w, wb_ = k.d(w); modx, modxb = k.d(modx); modc, modcb = k.d(modc); gpre, gpreb = k.d(gpre)
    ropeC, ropeCb = k.d(ropeC); ropeS, ropeSb = k.d(ropeS); prot, protb = k.d(prot); ident, identb = k.d(ident)
    qT, qTb = k.d(qT); kT, kTb = k.d(kT); vo, vob = k.d(vo); zT, zTb = k.d(zT); pT, pTb = k.d(pT)
    nc, P = k.nc, k.P
    W, Wb = k.sb("W", [128, 16, 3584], BF16)
    wv = w.rearrange("(k p) n -> p k n", p=128)
    for c in range(4):
        P.dma("sync", W[:, 4 * c:4 * c + 4, :], wv[:, 4 * c:4 * c + 4, :], [wb_], [Wb.part(c)], Wb.part(c))
    idt, idtb = k.sb("idt", [128, 128], F32)
    P.dma("sync", idt[:], ident[:, :], [identb], [idtb], idtb)
    prt, prtb = k.sb("prt", [128, 128], BF16)
    P.dma("sync", prt[:], prot[:, :], [protb], [prtb], prtb)
    AB, ABb = k.sb("AB", [128, 2, 2, 16], F32)
    tmpv, tmpvb = k.sb("tmpv", [128, 3, 16], F32)
    load_fm(P, tmpv[:, 0, :], gpre, 0, gpreb, tmpvb.part(0))
    for s, (m, mb) in enumerate(((modx, modxb), (modc, modcb))):
        load_fm(P, tmpv[:, 1 + s, :], m, D, mb, tmpvb.part(1 + s))
        load_fm(P, AB[:, s, 1, :], m, 0, mb, ABb.part(s))
    for s in range(2):
        P.op("vector", lambda e, s=s: e.scalar_tensor_tensor(out=AB[:, s, 0, :], in0=tmpv[:, 1 + s, :], scalar=1.0, in1=tmpv[:, 0, :],
                                                              op0=ALU.add, op1=ALU.mult),
             [tmpvb], [ABb.part(("A", s))])

    NXB = 2
    xt = [k.sb(f"xt{i}", [128, D], F32) for i in range(NXB)]
    ss = [k.sb(f"ss{i}", [128, 2], F32) for i in range(NXB)]
    junk, junkb = k.sb("junk", [128, D], BF16)
    hxT = [k.sb(f"hxT{i}", [128, 16, 512], BF16) for i in range(1)]
    ptr = [k.ps(f"ptr{i}", [128, 4, 128], F32) for i in range(2)]
    pacc = [k.ps(f"pacc{i}", [128, 512], F32) for i in range(3)]
    prot_ps = [k.ps(f"prot{i}", [128, 512], F32) for i in range(2)]
    pv = [k.ps(f"pv{i}", [128, 256], F32) for i in range(1)]
    qb = [k.sb(f"qb{i}", [128, 512], BF16) for i in range(2)]
    rc = [k.sb(f"rc{i}", [128, 512], F32) for i in range(2)]
    rs = [k.sb(f"rs{i}", [128, 512], F32) for i in range(2)]
    t1 = [k.sb(f"t1{i}", [128, 512], F32) for i in range(2)]
    t2 = [k.sb(f"t2{i}", [128, 512], F32) for i in range(2)]
    pf = [k.sb(f"pf{i}", [128, 512], F32) for i in range(2)]
    prf = [k.sb(f"prf{i}", [128, 512], F32) for i in range(2)]
    qo = [k.sb(f"qo{i}", [128, 512], BF16) for i in range(3)]
    zo = [k.sb(f"zo{i}", [128, 512], BF16) for i in range(3)]
    vs = [k.sb(f"vs{i}", [128, 256], BF16) for i in range(2)]

    macro = [(4 * i, 4, 0) for i in range(NTX // 4)] + [(NTX, NTC, 1)]
    import os
    if os.environ.get('NMACRO'): macro = macro[:int(os.environ['NMACRO'])]
    cnt = dict(x=0, tr=0, acc=0, rot=0, q=0, z=0, v=0)
    for mi, (t0, nt, mset) in enumerate(macro):
        ntok = nt * 128
        tok0 = t0 * 128
        H, Hb = hxT[0]
        RC, RCb = rc[mi % 2]
        RS, RSb = rs[mi % 2]
        P.dma("sync", RC[:, :ntok], ropeC[:, tok0:tok0 + ntok], [ropeCb], [RCb], RCb)
        P.dma("sync", RS[:, :ntok], ropeS[:, tok0:tok0 + ntok], [ropeSb], [RSb], RSb)
        for j in range(nt):
            X, Xb = xt[cnt["x"] % NXB]
            S, Sb = ss[cnt["x"] % NXB]
            cnt["x"] += 1
            r0 = (t0 + j) * 128
            P.dma("sync", X[:], xin[r0:r0 + 128, :], [xinb], [Xb], Xb)
            P.op("scalar", lambda e, X=X, S=S: e.activation(out=junk[:], in_=X[:], func=AF.Square, accum_out=S[:, 0:1]),
                 [Xb], [junkb, Sb])
            rstd_from_ss(P, S[:, 0:1], Sb, S[:, 1:2], Sb, D)
            P.op("scalar", lambda e, X=X, S=S: e.activation(out=X[:], in_=X[:], func=AF.Copy, scale=S[:, 1:2]),
                 [Xb, Sb], [Xb])
            for kg in range(4):
                PT, PTb = ptr[cnt["tr"] % 2]
                cnt["tr"] += 1
                for k4 in range(4):
                    kk = kg * 4 + k4
                    P.op("tensor", lambda e, PT=PT, X=X, k4=k4, kk=kk: e.transpose(out=PT[:, k4, :], in_=X[:, kk * 128:(kk + 1) * 128], identity=idt[:]),
                         [Xb, idtb], [PTb])
                for k4 in range(4):
                    kk = kg * 4 + k4
                    P.op("scalar", lambda e, ntok=ntok, PT=PT, H=H, kk=kk, k4=k4, j=j, mset=mset: e.activation(
                        out=H[:, kk, j * 128:(j + 1) * 128], in_=PT[:, k4, :], func=AF.Identity,
                        scale=AB[:, mset, 0, kk:kk + 1], bias=AB[:, mset, 1, kk:kk + 1]),
                         [PTb, ABb], [Hb])
        STAGE = int(os.environ.get('PA_STAGE', '9'))
        chunks = [("q", i, i * 128) for i in range(8)] + [("k", i, 1024 + i * 128) for i in range(2)] + \
                 [("z", i, 1536 + i * 128) for i in range(12)] + [("p", i, 3072 + i * 128) for i in range(4)]
        if STAGE == 1:
            chunks = []
            for kk in range(12):
                P.dma('gpsimd', zT[kk * 128:(kk + 1) * 128, tok0:tok0 + ntok].bitcast(BF16)[:, :ntok], H[:, kk, :ntok], [Hb], [zTb.part(kk)], Hb)
        if STAGE == 2:
            chunks = [c for c in chunks if c[0] in ('z', 'p')]
        for (kind, ci, col) in chunks:
            PA_, PAb = pacc[cnt["acc"] % 3]
            cnt["acc"] += 1
            for kk in range(16):
                P.op("tensor", lambda e, ntok=ntok, PA_=PA_, kk=kk, col=col, H=H: e.matmul(PA_[:, :ntok], lhsT=W[:, kk, col:col + 128], rhs=H[:, kk, :ntok],
                                                                            start=(kk == 0), stop=(kk == 15)),
                     [Wb, Hb], [PAb])
            if kind in ("q", "k"):
                QB, QBb = qb[cnt["rot"] % 2]
                PR, PRb = prot_ps[cnt["rot"] % 2]
                T1, T1b = t1[cnt["rot"] % 2]
                cnt["rot"] += 1
                QO, QOb = qo[cnt["q"] % 3]
                cnt["q"] += 1
                PF, PFb = pf[(cnt["rot"] - 1) % 2]
                PRF, PRFb = prf[(cnt["rot"] - 1) % 2]
                T2, T2b = t2[(cnt["rot"] - 1) % 2]
                P.op("scalar", lambda e, ntok=ntok, PF=PF, PA_=PA_: e.copy(out=PF[:, :ntok], in_=PA_[:, :ntok]), [PAb], [PFb])
                P.op("gpsimd", lambda e, ntok=ntok, QB=QB, PF=PF: e.tensor_copy(out=QB[:, :ntok], in_=PF[:, :ntok]), [PFb], [QBb])
                P.op("tensor", lambda e, ntok=ntok, PR=PR, QB=QB: e.matmul(PR[:, :ntok], lhsT=prt[:], rhs=QB[:, :ntok], start=True, stop=True),
                     [prtb, QBb], [PRb])
                P.op("scalar", lambda e, ntok=ntok, PRF=PRF, PR=PR: e.copy(out=PRF[:, :ntok], in_=PR[:, :ntok]), [PRb], [PRFb])
                P.op("vector", lambda e, ntok=ntok, T1=T1, PF=PF, RC=RC: e.tensor_tensor(out=T1[:, :ntok], in0=PF[:, :ntok], in1=RC[:, :ntok], op=ALU.mult),
                     [PFb, RCb], [T1b])
                P.op("vector", lambda e, ntok=ntok, T2=T2, PRF=PRF, RS=RS: e.tensor_tensor(out=T2[:, :ntok], in0=PRF[:, :ntok], in1=RS[:, :ntok], op=ALU.mult),
                     [PRFb, RSb], [T2b])
                P.op("vector", lambda e, ntok=ntok, QO=QO, T1=T1, T2=T2: e.tensor_tensor(out=QO[:, :ntok], in0=T1[:, :ntok], in1=T2[:, :ntok], op=ALU.add),
                     [T1b, T2b], [QOb])
                dst = (qT if kind == "q" else kT)
                dstb = (qTb if kind == "q" else kTb)
                P.dma("gpsimd", dst[:, ci, tok0:tok0 + ntok], QO[:, :ntok], [QOb], [dstb.part((mi, ci))], QOb)
            else:
                ZO, ZOb = zo[cnt["z"] % 3]
                cnt["z"] += 1
                P.op("scalar", lambda e, ntok=ntok, ZO=ZO, PA_=PA_: e.copy(out=ZO[:, :ntok], in_=PA_[:, :ntok]), [PAb], [ZOb])
                dst, dstb = (zT, zTb) if kind == "z" else (pT, pTb)
                P.dma("gpsimd", dst[ci * 128:(ci + 1) * 128, tok0:tok0 + ntok], ZO[:, :ntok], [ZOb], [dstb.part((mi, ci))], ZOb)
        for j in range(nt if STAGE >= 4 else 0):
            PV, PVb = pv[0]
            for kk in range(16):
                P.op("tensor", lambda e, PV=PV, kk=kk, j=j, H=H: e.matmul(PV[:], lhsT=H[:, kk, j * 128:(j + 1) * 128], rhs=W[:, kk, 1280:1536],
                                                                      start=(kk == 0), stop=(kk == 15)),
                     [Wb, Hb], [PVb])
            VS, VSb = vs[cnt["v"] % 2]
            cnt["v"] += 1
            P.op("scalar", lambda e, VS=VS, PV=PV: e.copy(out=VS[:], in_=PV[:]), [PVb], [VSb])
            r0 = (t0 + j) * 128
            P.dma("gpsimd", vo[r0:r0 + 128, :], VS[:], [VSb], [vob.part((mi, j))], VSb)


NCOL = 1536


def build_pm():
    pr = Program()
    pr.dram("c3", [3, D], F32, "ExternalInput")
    pr.dram("wm", [4, D, NCOL], F32, "ExternalInput")
    pr.dram("bm", [4, NCOL], F32, "ExternalInput")
    pr.dram("mod", [4, 3, NCOL], F32, "ExternalOutput")
    pr.phase(mod_phase)
    return pr.nc


def mod_phase(k, c3="c3", wm="wm", bm="bm", mo="mod"):
    c3, c3b = k.d(c3); wm, wmb = k.d(wm); bm, bmb = k.d(bm); mo, mob = k.d(mo)
    nc, P = k.nc, k.P
    cT, cTb = k.sb("cT", [128, 16, 3], F32)
    sT, sTb = k.sb("sT", [128, 16, 3], F32)
    for row in range(3):
        for kk in range(16):
            src = bass.AP(c3.tensor, c3.offset + row * D + kk * 128, [[1, 128], [1, 1]])
            P.dma("sync", cT[:, kk, row:row + 1], src, [c3b], [cTb], cTb)
    P.op("scalar", lambda e: e.activation(out=sT[:], in_=cT[:], func=AF.Silu), [cTb], [sTb])
    wt = [k.sb(f"wt{i}", [128, 16, 512], F32) for i in range(2)]
    pm = [k.ps(f"pm{i}", [3, 512], F32) for i in range(2)]
    ms = [k.sb(f"ms{i}", [3, 512], F32) for i in range(2)]
    bs = [k.sb(f"bs{i}", [3, 512], F32) for i in range(2)]
    i = 0
    for l in range(4):
        wv = wm[l].rearrange("(k p) n -> p k n", p=128)
        for n in range(NCOL // 512):
            WT, WTb = wt[i % 2]
            PM, PMb = pm[i % 2]
            MS, MSb = ms[i % 2]
            BS, BSb = bs[i % 2]
            i += 1
            for h in range(2):
                P.dma("sync", WT[:, 8 * h:8 * h + 8, :], wv[:, 8 * h:8 * h + 8, n * 512:(n + 1) * 512], [wmb], [WTb.part(h)], WTb.part(h))
            bsrc = bass.AP(bm.tensor, bm.offset + l * NCOL + n * 512, [[0, 3], [1, 512]])
            P.dma("sync", BS[:], bsrc, [bmb], [BSb], BSb)
            for kk in range(16):
                P.op("tensor", lambda e, PM=PM, WT=WT, kk=kk: e.matmul(PM[:], lhsT=sT[:, kk, :], rhs=WT[:, kk, :], start=(kk == 0), stop=(kk == 15)),
                     [sTb, WTb], [PMb])
            P.op("scalar", lambda e, MS=MS, PM=PM: e.copy(out=MS[:], in_=PM[:]), [PMb], [MSb])
            P.op("vector", lambda e, MS=MS, BS=BS: e.tensor_tensor(out=MS[:], in0=MS[:], in1=BS[:], op=ALU.add), [MSb, BSb], [MSb])
            P.dma("gpsimd", mo[l, :, n * 512:(n + 1) * 512], MS[:], [MSb], [mob.part((l, n))], MSb)


HID = 4 * D


def rstd_multi(P, S, Sb, widths):
    for g, wd in enumerate(widths):
        P.op("vector", lambda e, g=g, wd=wd: e.tensor_scalar(out=S[:, 4 + g:5 + g], in0=S[:, g:g + 1], scalar1=1.0 / wd, scalar2=EPS,
                                                              op0=ALU.mult, op1=ALU.add), [Sb], [Sb])
    n = len(widths)
    P.op("scalar", lambda e: e.activation(out=S[:, 4:4 + n], in_=S[:, 4:4 + n], func=AF.Ln), [Sb], [Sb])
    P.op("scalar", lambda e: e.activation(out=S[:, 4:4 + n], in_=S[:, 4:4 + n], func=AF.Exp, scale=-0.5), [Sb], [Sb])


def norm_T(P, X, Xb, S, Sb, groups, junk, junkb, ptr, cnt, idt, idtb, scale_fn, bias_fn, vecb, H, Hb, hcol0):
    for g, (c0, c1) in enumerate(groups):
        P.op("scalar", lambda e, g=g, c0=c0, c1=c1: e.activation(out=junk[:, c0:c1], in_=X[:, c0:c1], func=AF.Square, accum_out=S[:, g:g + 1]),
             [Xb], [junkb, Sb])
    rstd_multi(P, S, Sb, [c1 - c0 for (c0, c1) in groups])
    for g, (c0, c1) in enumerate(groups):
        P.op("scalar", lambda e, g=g, c0=c0, c1=c1: e.activation(out=X[:, c0:c1], in_=X[:, c0:c1], func=AF.Copy, scale=S[:, 4 + g:5 + g]),
             [Xb, Sb], [Xb])
    for kg in range(4):
        PT, PTb = ptr[cnt["tr"] % len(ptr)]
        cnt["tr"] += 1
        for k4 in range(4):
            kk = kg * 4 + k4
            P.op("tensor", lambda e, PT=PT, k4=k4, kk=kk: e.transpose(out=PT[:, k4, :], in_=X[:, kk * 128:(kk + 1) * 128], identity=idt[:]),
                 [Xb, idtb], [PTb])
        for k4 in range(4):
            kk = kg * 4 + k4
            if bias_fn is None:
                P.op("scalar", lambda e, PT=PT, kk=kk, k4=k4: e.activation(out=H[:, kk, hcol0:hcol0 + 128], in_=PT[:, k4, :], func=AF.Copy,
                                                                        scale=scale_fn(kk)), [PTb, vecb], [Hb])
            else:
                P.op("scalar", lambda e, PT=PT, kk=kk, k4=k4: e.activation(out=H[:, kk, hcol0:hcol0 + 128], in_=PT[:, k4, :], func=AF.Identity,
                                                                        scale=scale_fn(kk), bias=bias_fn(kk)), [PTb, vecb], [Hb])


def load_bc(P, dst, dstb, ap1d, off, srcb, n=D, eng="sync"):
    P.dma(eng, dst, vec_bc(ap1d, off, n), [srcb], [dstb], dstb)


def outproj_phase(k, attn="attn", hy="hy", po="po", xin="xin", wo="wo_bf", gbr="gbr", gpost="gpost", modx="modx", modc="modc",
                  ident="ident", x1="x1"):
    nc, P = k.nc, k.P
    attn, attnb = k.d(attn); hy, hyb = k.d(hy); po, pob = k.d(po); xin, xinb = k.d(xin); wo, wob = k.d(wo)
    gbr, gbrb = k.d(gbr); gpost, gpostb = k.d(gpost); modx, modxb = k.d(modx); modc, modcb = k.d(modc)
    ident, identb = k.d(ident); x1, x1b = k.d(x1)
    W, Wb = k.sb("Wo", [128, 16, D], BF16)
    wv = wo.rearrange("(k p) n -> p k n", p=128)
    for c in range(4):
        P.dma("sync", W[:, 4 * c:4 * c + 4, :], wv[:, 4 * c:4 * c + 4, :], [wob], [Wb.part(c)], Wb.part(c))
    idt, idtb = k.sb("idt", [128, 128], F32)
    P.dma("sync", idt[:], ident[:, :], [identb], [idtb], idtb)
    gb, gbb = k.sb("gb", [128, 16], F32)
    load_fm(P, gb[:, :], gbr, 0, gbrb, gbb)
    G1 = [k.sb(f"G1_{s}", [128, D], F32) for s in range(2)]
    gtmp, gtmpb = k.sb("gtmp", [128, D], F32)
    for s, (m, mb) in enumerate(((modx, modxb), (modc, modcb))):
        load_bc(P, G1[s][0][:], G1[s][1], gpost, 0, gpostb)
        load_bc(P, gtmp[:], gtmpb, m, 2 * D, mb)
        P.op("vector", lambda e, s=s: e.tensor_tensor(out=G1[s][0][:], in0=G1[s][0][:], in1=gtmp[:], op=ALU.mult), [G1[s][1], gtmpb], [G1[s][1]])
    Ms = [k.sb(f"M{i}", [128, D], F32) for i in range(2)]
    Xs = [k.sb(f"X{i}", [128, D], F32) for i in range(2)]
    Os = [k.sb(f"O{i}", [128, D], F32) for i in range(2)]
    Ss = [k.sb(f"S{i}", [128, 8], F32) for i in range(2)]
    S2s = [k.sb(f"S2{i}", [128, 8], F32) for i in range(2)]
    mT = [k.sb(f"mT{i}", [128, 16, 128], BF16) for i in range(2)]
    junk, junkb = k.sb("junk", [128, D], BF16)
    ptr = [k.ps(f"ptr{i}", [128, 4, 128], F32) for i in range(2)]
    pout = [k.ps(f"pout{i}", [128, D], F32) for i in range(1)]
    cnt = dict(tr=0)
    groups = [(0, 1024), (1024, 1536), (1536, 2048)]
    for t in range(NT):
        mset = 0 if t < NTX else 1
        r0 = t * 128
        M, Mb = Ms[t % 2]; X, Xb = Xs[t % 2]; O, Ob = Os[t % 2]; S, Sb = Ss[t % 2]; S2, S2b = S2s[t % 2]
        H, Hb = mT[t % 2]
        P.dma("gpsimd", M[:, 0:1024], attn[r0:r0 + 128, :], [attnb], [Mb.part(0)], Mb.part(0))
        P.dma("gpsimd", M[:, 1024:1536], hy[r0:r0 + 128, :], [hyb], [Mb.part(1)], Mb.part(1))
        P.dma("gpsimd", M[:, 1536:2048], po[r0:r0 + 128, :], [pob], [Mb.part(2)], Mb.part(2))
        P.dma("sync", X[:], xin[r0:r0 + 128, :], [xinb], [Xb], Xb)
        norm_T(P, M, Mb, S, Sb, groups, junk, junkb, ptr, cnt, idt, idtb, lambda kk: gb[:, kk:kk + 1], None, gbb, H, Hb, 0)
        PO, POb = pout[0]
        for cg in range(4):
            for kk in range(16):
                P.op("tensor", lambda e, PO=PO, H=H, kk=kk, cg=cg: e.matmul(PO[:, cg * 512:(cg + 1) * 512], lhsT=H[:, kk, :], rhs=W[:, kk, cg * 512:(cg + 1) * 512],
                                                                        start=(kk == 0), stop=(kk == 15)), [Hb, Wb], [POb])
        P.op("scalar", lambda e, PO=PO, S2=S2: e.activation(out=junk[:], in_=PO[:], func=AF.Square, accum_out=S2[:, 0:1]), [POb], [junkb, S2b])
        rstd_multi(P, S2, S2b, [D])
        P.op("scalar", lambda e, PO=PO, O=O, S2=S2: e.activation(out=O[:], in_=PO[:], func=AF.Copy, scale=S2[:, 4:5]), [POb, S2b], [Ob])
        P.op("vector", lambda e, O=O, mset=mset: e.tensor_tensor(out=O[:], in0=O[:], in1=G1[mset][0][:], op=ALU.mult), [Ob, G1[mset][1]], [Ob])
        P.op("gpsimd", lambda e, O=O, X=X: e.tensor_tensor(out=O[:], in0=O[:], in1=X[:], op=ALU.add), [Ob, Xb], [Ob])
        P.dma("gpsimd", x1[r0:r0 + 128, :], O[:], [Ob], [x1b.part(t)], Ob)


def mlp_phase(k, x1="x1", wu="wu_bf", wd="wd_bf", gpre="gpre2", gpost="gpost2", modx="modx", modc="modc", ident="ident", x2="x2"):
    nc, P = k.nc, k.P
    x1, x1b = k.d(x1); wu, wub = k.d(wu); wd, wdb = k.d(wd); gpre, gpreb = k.d(gpre); gpost, gpostb = k.d(gpost)
    modx, modxb = k.d(modx); modc, modcb = k.d(modc); ident, identb = k.d(ident); x2, x2b = k.d(x2)
    idt, idtb = k.sb("idt", [128, 128], F32)
    P.dma("sync", idt[:], ident[:, :], [identb], [idtb], idtb)
    AB, ABb = k.sb("AB", [128, 2, 2, 16], F32)
    tmpv, tmpvb = k.sb("tmpv", [128, 3, 16], F32)
    load_fm(P, tmpv[:, 0, :], gpre, 0, gpreb, tmpvb.part(0))
    for s, (m, mb) in enumerate(((modx, modxb), (modc, modcb))):
        load_fm(P, tmpv[:, 1 + s, :], m, 4 * D, mb, tmpvb.part(1 + s))
        load_fm(P, AB[:, s, 1, :], m, 3 * D, mb, ABb.part(s))
    for s in range(2):
        P.op("vector", lambda e, s=s: e.scalar_tensor_tensor(out=AB[:, s, 0, :], in0=tmpv[:, 1 + s, :], scalar=1.0, in1=tmpv[:, 0, :],
                                                              op0=ALU.add, op1=ALU.mult), [tmpvb], [ABb.part(("A", s))])
    G2, G2b = k.sb("G2", [128, D], F32)
    Xs = [k.sb(f"X{i}", [128, D], F32) for i in range(2)]
    gtmp, gtmpb = Xs[1]
    Ss = [k.sb(f"S{i}", [128, 8], F32) for i in range(2)]
    S2, S2b = k.sb("S2", [128, 4, 8], F32)
    O2, O2b = k.sb("O2", [128, 4, D], F32)
    h2T, h2Tb = k.sb("h2T", [128, 16, 512], BF16)
    hidT, hidTb = k.sb("hidT", [128, 64, 512], BF16)
    ws = [k.sb(f"ws{i}", [128, 16, 512], BF16) for i in range(2)]
    rl = [k.sb(f"rl{i}", [128, 512], F32) for i in range(2)]
    junk2, junk2b = k.sb("junk2", [128, 512], BF16)
    ptr = [k.ps(f"ptr{i}", [128, 4, 128], F32) for i in range(2)]
    pup = [k.ps(f"pup{i}", [128, 512], F32) for i in range(2)]
    pdn = [k.ps(f"pdn{i}", [128, 512], F32) for i in range(4)]
    wuv = wu.rearrange("(k p) n -> p k n", p=128)
    wdv = wd.rearrange("(k p) n -> p k n", p=128)
    macro = [(4 * i, 4, 0) for i in range(NTX // 4)] + [(NTX, NTC, 1)]
    cnt = dict(tr=0, x=0, w=0, up=0)
    if os.environ.get('MLP_NMACRO'): macro = macro[:int(os.environ['MLP_NMACRO'])]
    cur_set = None
    for mi, (t0, nt, mset) in enumerate(macro):
        ntok = nt * 128
        if mset != cur_set:
            cur_set = mset
            m, mb = (modx, modxb) if mset == 0 else (modc, modcb)
            load_bc(P, G2[:], G2b, gpost, 0, gpostb)
            load_bc(P, gtmp[:], gtmpb, m, 5 * D, mb)
            P.op("vector", lambda e: e.tensor_tensor(out=G2[:], in0=G2[:], in1=gtmp[:], op=ALU.mult), [G2b, gtmpb], [G2b])
        junkv = O2[:, 0, :].bitcast(BF16)[:, 0:D]
        for j in range(nt):
            X, Xb = Xs[cnt["x"] % 2]; S, Sb = Ss[cnt["x"] % 2]
            cnt["x"] += 1
            r0 = (t0 + j) * 128
            P.dma("sync", X[:], x1[r0:r0 + 128, :], [x1b], [Xb], Xb)
            norm_T(P, X, Xb, S, Sb, [(0, D)], junkv, O2b, ptr, cnt, idt, idtb,
                   lambda kk, mset=mset: AB[:, mset, 0, kk:kk + 1], lambda kk, mset=mset: AB[:, mset, 1, kk:kk + 1], ABb, h2T, h2Tb, j * 128)
        for sl in range(16):
            WS, WSb = ws[cnt["w"] % 2]
            cnt["w"] += 1
            for h in range(2):
                P.dma("sync", WS[:, 8 * h:8 * h + 8, :], wuv[:, 8 * h:8 * h + 8, sl * 512:(sl + 1) * 512], [wub], [WSb.part(h)], WSb.part(h))
            for c4 in range(4):
                hc = sl * 4 + c4
                PU, PUb = pup[cnt["up"] % 2]; RL, RLb = rl[cnt["up"] % 2]
                cnt["up"] += 1
                for kk in range(16):
                    P.op("tensor", lambda e, ntok=ntok, PU=PU, WS=WS, kk=kk, c4=c4: e.matmul(PU[:, :ntok], lhsT=WS[:, kk, c4 * 128:(c4 + 1) * 128], rhs=h2T[:, kk, :ntok],
                                                                              start=(kk == 0), stop=(kk == 15)), [WSb, h2Tb], [PUb])
                P.op("scalar", lambda e, ntok=ntok, PU=PU, RL=RL: e.activation(out=RL[:, :ntok], in_=PU[:, :ntok], func=AF.Relu), [PUb], [RLb])
                eng = "vector" if hc % 2 == 0 else "gpsimd"
                P.op(eng, lambda e, ntok=ntok, RL=RL, hc=hc: e.tensor_tensor(out=hidT[:, hc, :ntok], in0=RL[:, :ntok], in1=RL[:, :ntok], op=ALU.mult),
                     [RLb], [hidTb.part(hc)])
        for cg in range(4):
            for q in range(4):
                WS, WSb = ws[cnt["w"] % 2]
                cnt["w"] += 1
                for h in range(2):
                    P.dma("sync", WS[:, 8 * h:8 * h + 8, :], wdv[:, q * 16 + 8 * h:q * 16 + 8 * h + 8, cg * 512:(cg + 1) * 512], [wdb], [WSb.part(h)], WSb.part(h))
                for j in range(nt):
                    PD, PDb = pdn[j]
                    for c in range(16):
                        hc = q * 16 + c
                        P.op("tensor", lambda e, PD=PD, WS=WS, c=c, hc=hc, j=j: e.matmul(PD[:], lhsT=hidT[:, hc, j * 128:(j + 1) * 128], rhs=WS[:, c, :],
                                                                                   start=(hc == 0), stop=(hc == 63)), [hidTb, WSb], [PDb])
            for j in range(nt):
                PD, PDb = pdn[j]
                P.op("scalar", lambda e, PD=PD, j=j, cg=cg: e.activation(out=junk2[:], in_=PD[:], func=AF.Square, accum_out=S2[:, j, cg:cg + 1]),
                     [PDb], [junk2b, S2b.part(j)])
                P.op("scalar", lambda e, PD=PD, j=j, cg=cg: e.copy(out=O2[:, j, cg * 512:(cg + 1) * 512], in_=PD[:]), [PDb], [O2b.part(j)])
        for j in range(nt):
            X, Xb = Xs[cnt["x"] % 2]
            cnt["x"] += 1
            r0 = (t0 + j) * 128
            P.dma("sync", X[:], x1[r0:r0 + 128, :], [x1b], [Xb], Xb)
            S2j = S2[:, j, :]
            P.op("vector", lambda e, S2j=S2j: e.tensor_tensor(out=S2j[:, 0:2], in0=S2j[:, 0:2], in1=S2j[:, 2:4], op=ALU.add), [S2b.part(j)], [S2b.part(j)])
            P.op("vector", lambda e, S2j=S2j: e.tensor_tensor(out=S2j[:, 0:1], in0=S2j[:, 0:1], in1=S2j[:, 1:2], op=ALU.add), [S2b.part(j)], [S2b.part(j)])
            rstd_multi(P, S2j, S2b.part(j), [D])
            P.op("vector", lambda e, j=j, S2j=S2j: e.scalar_tensor_tensor(out=O2[:, j, :], in0=O2[:, j, :], scalar=S2j[:, 4:5], in1=G2[:],
                                                                     op0=ALU.mult, op1=ALU.mult), [O2b.part(j), S2b.part(j), G2b], [O2b.part(j)])
            P.op("gpsimd", lambda e, j=j, X=X: e.tensor_tensor(out=O2[:, j, :], in0=O2[:, j, :], in1=X[:], op=ALU.add), [O2b.part(j), Xb], [O2b.part(j)])
            P.dma("gpsimd", x2[r0:r0 + 128, :], O2[:, j, :], [O2b.part(j)], [x2b.part((mi, j))], O2b.part(j))


def pc_drams(pr, x1_external=True):
    pr.dram("attn", [NTOK, 1024], BF16, "ExternalInput")
    pr.dram("hy", [NTOK, 512], BF16, "ExternalInput")
    pr.dram("po", [NTOK, 512], BF16, "ExternalInput")
    pr.dram("xin", [NTOK, D], F32, "ExternalInput")
    pr.dram("wo32", [D, D], F32, "ExternalInput")
    pr.dram("wu32", [D, HID], F32, "ExternalInput")
    pr.dram("wd32", [HID, D], F32, "ExternalInput")
    pr.dram("wo_bf", [D, D], BF16)
    pr.dram("wu_bf", [D, HID], BF16)
    pr.dram("wd_bf", [HID, D], BF16)
    for n in ("gbr", "gpost", "gpre2", "gpost2"):
        pr.dram(n, [D], F32, "ExternalInput")
    pr.dram("modx", [6 * D], F32, "ExternalInput")
    pr.dram("modc", [6 * D], F32, "ExternalInput")
    pr.dram("ident", [128, 128], F32, "ExternalInput")
    pr.dram("x1", [NTOK, D], F32, "ExternalOutput") if x1_external else pr.dram("x1", [NTOK, D], F32)
    pr.dram("x2", [NTOK, D], F32, "ExternalOutput")


def build_pc():
    pr = Program()
    pc_drams(pr)
    pr.phase(cast_phase, "wo32", "wo_bf", D, D)
    pr.phase(cast_phase, "wu32", "wu_bf", D, HID)
    pr.phase(cast_phase, "wd32", "wd_bf", HID, D)
    pr.phase(outproj_phase)
    pr.phase(mlp_phase)
    return pr.nc


NKX = NTX + 2
NKB = NKX + NTC
SCALE = 128 ** -0.5


def attn_phase(k, qT="qT", kTh="kTh", vh="vh", masks="masks", sink="sink", identb="identb", attn="attn"):
    nc, P = k.nc, k.P
    qT, qTb = k.d(qT); kTh, kThb = k.d(kTh); vh, vhb = k.d(vh); masks, masksb = k.d(masks); sink, sinkb = k.d(sink)
    identd, identdb = k.d(identb); attn, attnb = k.d(attn)
    KT, KTb = k.sb("KT", [128, 2, NKB * 128], BF16)
    for g in range(2):
        P.dma("sync", KT[:, g, :], kTh[:, g, :], [kThb], [KTb.part(g)], KTb.part(g))
    VA, VAb = k.sb("VA", [128, NKB, 2, 130], BF16)
    P.op("vector", lambda e: e.memset(VA[:], 1.0), [], [VAb])
    vhv = vh.rearrange("(b p) (g d) -> p b g d", p=128, g=2)
    for g in range(2):
        for h in range(4):
            b0, b1 = h * 9, min(NKB, (h + 1) * 9)
            P.dma("sync", VA[:, b0:b1, g, 0:128], vhv[:, b0:b1, g, :], [vhb], [VAb], VAb)
    MK, MKb = k.sb("MK", [128, 4, 512], BF16)
    P.dma("sync", MK[:], masks.rearrange("m p n -> p m n"), [masksb], [MKb], MKb)
    idb, idbb = k.sb("idb", [128, 128], BF16)
    P.dma("sync", idb[:], identd[:, :], [identdb], [idbb], idbb)
    es, esb = k.sb("es", [128, 8], F32)
    P.dma("sync", es[:], bass.AP(sink.tensor, sink.offset, [[0, 128], [1, 8]]), [sinkb], [esb], esb)
    P.op("scalar", lambda e: e.activation(out=es[:], in_=es[:], func=AF.Exp), [esb], [esb])
    Qs = [k.sb(f"Q{i}", [128, 8, 128], BF16) for i in range(2)]
    PTs = [k.sb(f"PT{i}", [128, 512], BF16) for i in range(10)]
    Ofs = [k.sb(f"Of{i}", [128, 132], F32) for i in range(3)]
    dn = [k.sb(f"dn{i}", [128, 2], F32) for i in range(3)]
    AT = [k.sb(f"AT{i}", [128, 1024], BF16) for i in range(2)]
    pst = [k.ps(f"pst{i}", [128, 512], F32) for i in range(4)]
    pso = [k.ps(f"pso{i}", [128, 512], F32) for i in range(3)]
    c = dict(st=0, pt=0, o=0)
    for t in range(NT):
        Q, Qb = Qs[t % 2]
        A, Ab = AT[t % 2]
        P.dma("sync", Q[:], qT[:, :, t * 128:(t + 1) * 128], [qTb], [Qb], Qb)
        if t < NTX:
            kbs = [(t, 0 if t > 0 else 2), (t + 1, None), (t + 2, 1 if t < NTX - 1 else 3), (NKX, None), (NKX + 1, None)]
        else:
            kbs = [(NKX, None), (NKX + 1, None)]
        for g in range(2):
            pts = []
            for (kb, mk) in kbs:
                ST, STb = pst[c["st"] % 4]; c["st"] += 1
                PT, PTb = PTs[c["pt"] % 10]; c["pt"] += 1
                P.op("tensor", lambda e, ST=ST, kb=kb, g=g, Q=Q, mk=mk: e.matmul(ST[:], lhsT=KT[:, g, kb * 128:(kb + 1) * 128], rhs=Q[:, 4 * g:4 * g + 4, :],
                                                                            start=True, stop=(mk is None)), [KTb, Qb], [STb])
                if mk is not None:
                    P.op("tensor", lambda e, ST=ST, mk=mk: e.matmul(ST[:], lhsT=idb[:], rhs=MK[:, mk, :], start=False, stop=True), [idbb, MKb], [STb])
                P.op("scalar", lambda e, ST=ST, PT=PT: e.activation(out=PT[:], in_=ST[:], func=AF.Exp, scale=SCALE), [STb], [PTb])
                pts.append((PT, PTb, kb))
            for h in range(4):
                head = 4 * g + h
                PO, POb = pso[c["o"] % 3]; OF, OFb = Ofs[c["o"] % 3]; DN, DNb = dn[c["o"] % 3]; c["o"] += 1
                for i, (PT, PTb, kb) in enumerate(pts):
                    P.op("tensor", lambda e, PO=PO, PT=PT, kb=kb, g=g, h=h, i=i, n=len(pts): e.matmul(PO[:, 0:129], lhsT=PT[:, h * 128:(h + 1) * 128], rhs=VA[:, kb, g, 0:129],
                                                                                              start=(i == 0), stop=(i == n - 1)), [PTb, VAb], [POb])
                P.op("scalar", lambda e, PO=PO, OF=OF: e.copy(out=OF[:, 0:129], in_=PO[:, 0:129]), [POb], [OFb])
                P.op("vector", lambda e, OF=OF, DN=DN, head=head: e.tensor_tensor(out=DN[:, 0:1], in0=OF[:, 128:129], in1=es[:, head:head + 1], op=ALU.add),
                     [OFb, esb], [DNb])
                P.op("vector", lambda e, DN=DN: e.reciprocal(out=DN[:, 1:2], in_=DN[:, 0:1]), [DNb], [DNb])
                P.op("vector", lambda e, OF=OF, DN=DN, A=A, head=head: e.tensor_scalar(out=A[:, head * 128:(head + 1) * 128], in0=OF[:, 0:128], scalar1=DN[:, 1:2],
                                                                                   scalar2=None, op0=ALU.mult), [OFb, DNb], [Ab])
        P.dma("gpsimd", attn[t * 128:(t + 1) * 128, :], A[:], [Ab], [attnb.part(t)], Ab)


POOLW = (2, 4, 8, 16)


def pool_phase(k, pTh="pTh", pTc="pTc", invx="invx", invc="invc", pw="pool_w", pscale="pool_scale", po="po"):
    nc, P = k.nc, k.P
    pTh, pThb = k.d(pTh); pTc, pTcb = k.d(pTc); invx, invxb = k.d(invx); invc, invcb = k.d(invc)
    pw, pwb = k.d(pw); pscale, pscaleb = k.d(pscale); po, pob = k.d(po)
    W32, W32b = k.sb("W32", [128, 4, 128], F32)
    P.dma("sync", W32[:], pw.rearrange("g c d -> c g d"), [pwb], [W32b], W32b)
    Wp, Wpb = k.sb("Wp", [128, 4, 128], BF16)
    P.op("vector", lambda e: e.tensor_copy(out=Wp[:], in_=W32[:]), [W32b], [Wpb])
    PS, PSb = k.sb("PS", [128, 512], F32)
    load_bc(P, PS[:], PSb, pscale, 0, pscaleb, n=512)
    Xp = [k.sb(f"Xp{i}", [128, 528], F32) for i in range(3)]
    Wa = [k.sb(f"Wa{i}", [128, 528], F32) for i in range(3)]
    Wb2 = [k.sb(f"Wb{i}", [128, 528], F32) for i in range(3)]
    IV = [k.sb(f"IV{i}", [128, 512], F32) for i in range(3)]
    Y = [k.sb(f"Y{i}", [128, 4, 512], BF16) for i in range(2)]
    OS = [k.sb(f"OS{i}", [128, 512], F32) for i in range(3)]
    OB = [k.sb(f"OB{i}", [128, 512], BF16) for i in range(3)]
    pp = [k.ps(f"pp{i}", [128, 512], F32) for i in range(2)]
    c = dict(x=0, o=0)
    chunks = [(pTh, pThb, invx, invxb, i * 512, 512, i * 512) for i in range(NTX // 4)] + [(pTc, pTcb, invc, invcb, 0, 256, NTX * 128)]
    for ci, (src, srcb, inv, invb, c0, n, orow) in enumerate(chunks):
        YT, YTb = Y[ci % 2]
        for g, w in enumerate(POOLW):
            h = w // 2
            X, Xb = Xp[c["x"] % 3]; A, Ab = Wa[c["x"] % 3]; B, Bb = Wb2[c["x"] % 3]; I, Ib = IV[c["x"] % 3]
            eng = "vector" if c["x"] % 2 == 0 else "gpsimd"
            c["x"] += 1
            P.dma("gpsimd", X[:, 0:n + 16], src[g * 128:(g + 1) * 128, c0:c0 + n + 16], [srcb], [Xb], Xb)
            P.dma("sync", I[:, 0:n], bass.AP(inv.tensor, inv.offset + g * inv.shape[1] + c0, [[0, 128], [1, n]]), [invb], [Ib], Ib)
            L = n + 16
            cur, curb = X, Xb
            step = 1
            bufs = [(A, Ab), (B, Bb)]
            bi = 0
            while step < w:
                L2 = L - step
                dst, dstb = bufs[bi]; bi ^= 1
                P.op(eng, lambda e, dst=dst, cur=cur, L2=L2, step=step: e.tensor_tensor(out=dst[:, 0:L2], in0=cur[:, 0:L2], in1=cur[:, step:step + L2], op=ALU.add),
                     [curb], [dstb])
                cur, curb = dst, dstb
                L = L2
                step *= 2
            dst, dstb = bufs[bi]
            P.op(eng, lambda e, dst=dst, cur=cur, I=I, h=h, n=n: e.tensor_tensor(out=dst[:, 0:n], in0=cur[:, 8 - h:8 - h + n], in1=I[:, 0:n], op=ALU.mult),
                 [curb, Ib], [dstb])
            P.op(eng, lambda e, dst=dst, X=X, YT=YT, g=g, n=n: e.tensor_tensor(out=YT[:, g, 0:n], in0=dst[:, 0:n], in1=X[:, 8:8 + n], op=ALU.subtract),
                 [dstb, Xb], [YTb.part(g)])
        for j in range(n // 128):
            PP, PPb = pp[c["o"] % 2]; O, Ob = OS[c["o"] % 3]; c["o"] += 1
            for g in range(4):
                P.op("tensor", lambda e, PP=PP, YT=YT, g=g, j=j: e.matmul(PP[:, g * 128:(g + 1) * 128], lhsT=YT[:, g, j * 128:(j + 1) * 128], rhs=Wp[:, g, :],
                                                                      start=True, stop=True), [YTb, Wpb], [PPb])
            P.op("scalar", lambda e, PP=PP, O=O: e.copy(out=O[:], in_=PP[:]), [PPb], [Ob])
            OBt, OBb = OB[(c["o"] - 1) % 3]
            P.op("vector", lambda e, O=O, OBt=OBt: e.tensor_tensor(out=OBt[:], in0=O[:], in1=PS[:], op=ALU.mult), [Ob, PSb], [OBb])
            r0 = orow + j * 128
            P.dma("gpsimd", po[r0:r0 + 128, :], OBt[:], [OBb], [pob.part((ci, j))], OBb)


def pb_drams(pr):
    pr.dram("qT", [128, 8, NTOK], BF16, "ExternalInput")
    pr.dram("kTh", [128, 2, NKB * 128], BF16, "ExternalInput")
    pr.dram("vh", [NKB * 128, 256], BF16, "ExternalInput")
    pr.dram("masks", [4, 128, 512], BF16, "ExternalInput")
    pr.dram("sink", [8], F32, "ExternalInput")
    pr.dram("identb", [128, 128], BF16, "ExternalInput")
    pr.dram("attn", [NTOK, 1024], BF16, "ExternalOutput")
    pr.dram("pTh", [512, 8 + NTX * 128 + 8], BF16, "ExternalInput")
    pr.dram("pTc", [512, 8 + NTC * 128 + 8], BF16, "ExternalInput")
    pr.dram("invx", [4, NTX * 128], F32, "ExternalInput")
    pr.dram("invc", [4, NTC * 128], F32, "ExternalInput")
    pr.dram("pool_w", [4, 128, 128], F32, "ExternalInput")
    pr.dram("pool_scale", [512], F32, "ExternalInput")
    pr.dram("po", [NTOK, 512], BF16, "ExternalOutput")


def build_pb():
    pr = Program()
    pb_drams(pr)
    pr.phase(attn_phase)
    pr.phase(pool_phase)
    return pr.nc


I32 = mybir.dt.int32
TWO_PI = float(2 * np.pi)


def col_ap(ap1d, off, n):
    return bass.AP(ap1d.tensor, ap1d.offset + off, [[1, n], [1, 1]])


def sin_layer(P, ps, psb, f, fb, vb, a, ab, tf, tfb, ti, tib, h, hb, N):
    P.op("scalar", lambda e: e.activation(out=a[:, :N], in_=ps[:, :N], func=AF.Identity, scale=f, bias=fb), [psb, vb], [ab])
    P.op("vector", lambda e: e.tensor_scalar(out=a[:, :N], in0=a[:, :N], scalar1=1.0 / TWO_PI, scalar2=None, op0=ALU.mult), [ab], [ab])
    P.op("vector", lambda e: e.tensor_copy(out=ti[:, :N], in_=a[:, :N]), [ab], [tib])
    P.op("vector", lambda e: e.tensor_copy(out=tf[:, :N], in_=ti[:, :N]), [tib], [tfb])
    P.op("vector", lambda e: e.tensor_tensor(out=a[:, :N], in0=a[:, :N], in1=tf[:, :N], op=ALU.subtract), [ab, tfb], [ab])
    P.op("vector", lambda e: e.tensor_scalar(out=tf[:, :N], in0=a[:, :N], scalar1=0.5, scalar2=None, op0=ALU.is_gt), [ab], [tfb])
    P.op("vector", lambda e: e.tensor_tensor(out=a[:, :N], in0=a[:, :N], in1=tf[:, :N], op=ALU.subtract), [ab, tfb], [ab])
    P.op("vector", lambda e: e.tensor_scalar(out=tf[:, :N], in0=a[:, :N], scalar1=-0.5, scalar2=None, op0=ALU.is_lt), [ab], [tfb])
    P.op("vector", lambda e: e.tensor_tensor(out=a[:, :N], in0=a[:, :N], in1=tf[:, :N], op=ALU.add), [ab, tfb], [ab])
    P.op("scalar", lambda e: e.activation(out=h[:, :N], in_=a[:, :N], func=AF.Sin, scale=TWO_PI), [ab], [hb])


def filter_phase(k, n, ft="ft", dec="dec", w1="hw1", b1="hb1", f1="hf1", w2="hw2", b2="hb2", f2="hf2", w3b="hw3b", w3f="hw3f",
                 bias="hbias", Hd="Hd", hsum="hsum"):
    nc, P = k.nc, k.P
    ft, ftb = k.d(ft); dec, decb = k.d(dec); w1, w1b = k.d(w1); b1, b1b = k.d(b1); f1, f1b = k.d(f1); w2, w2b = k.d(w2)
    b2, b2b = k.d(b2); f2, f2b = k.d(f2); w3b, w3bb = k.d(w3b); w3f, w3fb = k.d(w3f); bias, biasb = k.d(bias); Hd, Hdb = k.d(Hd); hsum, hsumb = k.d(hsum)
    W1, W1b = k.sb("W1", [33, 64], F32); W2, W2b = k.sb("W2", [64, 64], F32); W3, W3b = k.sb("W3", [64, 2, 64], F32)
    P.dma("sync", W1[:], w1[:, :], [w1b], [W1b], W1b)
    P.dma("sync", W2[:], w2[:, :], [w2b], [W2b], W2b)
    P.dma("sync", W3[:, 0, :], w3b[:, :], [w3bb], [W3b.part(0)], W3b.part(0))
    P.dma("sync", W3[:, 1, :], w3f[:, :], [w3fb], [W3b.part(1)], W3b.part(1))
    V, Vb = k.sb("V", [64, 8], F32)
    for i, (a, ab) in enumerate(((f1, f1b), (b1, b1b), (f2, f2b), (b2, b2b), (bias, biasb))):
        P.dma("sync", V[:, i:i + 1], col_ap(a, 0, 64), [ab], [Vb.part(i)], Vb.part(i))
    P.op("vector", lambda e: e.tensor_tensor(out=V[:, 5:6], in0=V[:, 0:1], in1=V[:, 1:2], op=ALU.mult), [Vb], [Vb])
    P.op("vector", lambda e: e.tensor_tensor(out=V[:, 6:7], in0=V[:, 2:3], in1=V[:, 3:4], op=ALU.mult), [Vb], [Vb])
    nch = (2 * n + 511) // 512
    SA, SAb = k.sb("SA", [64, nch + 4], F32)
    T0, T0b = k.sb("T0", [64, 4], F32)
    FT = [k.sb(f"FT{i}", [33, 512], F32) for i in range(2)]
    DC = [k.sb(f"DC{i}", [64, 512], F32) for i in range(2)]
    A = [k.sb(f"A{i}", [64, 512], F32) for i in range(2)]
    TF = [k.sb(f"TF{i}", [64, 512], F32) for i in range(2)]
    TI = [k.sb(f"TI{i}", [64, 512], I32) for i in range(2)]
    H1 = [k.sb(f"H1{i}", [64, 512], F32) for i in range(2)]
    H2 = [k.sb(f"H2{i}", [64, 512], F32) for i in range(2)]
    R = [k.sb(f"R{i}", [64, 512], F32) for i in range(2)]
    J = [k.sb(f"J{i}", [64, 512], F32) for i in range(2)]
    HB = [k.sb(f"HB{i}", [64, 512], BF16) for i in range(2)]
    p1 = [k.ps(f"p1{i}", [64, 512], F32) for i in range(2)]
    p2 = [k.ps(f"p2{i}", [64, 512], F32) for i in range(2)]
    p3 = [k.ps(f"p3{i}", [64, 512], F32) for i in range(2)]
    for ch in range(nch):
        c0 = ch * 512
        N = min(512, 2 * n - c0)
        i = ch % 2
        F_, Fb_ = FT[i]; D_, Db_ = DC[i]
        P.dma("sync", F_[:, :N], ft[:, c0:c0 + N], [ftb], [Fb_], Fb_)
        P.dma("sync", D_[:, :N], dec[:, c0:c0 + N], [decb], [Db_], Db_)
        P.op("tensor", lambda e, i=i, N=N, F_=F_: e.matmul(p1[i][0][:, :N], lhsT=W1[:], rhs=F_[:, :N], start=True, stop=True), [W1b, Fb_], [p1[i][1]])
        sin_layer(P, p1[i][0], p1[i][1], V[:, 0:1], V[:, 5:6], Vb, A[i][0], A[i][1], TF[i][0], TF[i][1], TI[i][0], TI[i][1], H1[i][0], H1[i][1], N)
        P.op("tensor", lambda e, i=i, N=N: e.matmul(p2[i][0][:, :N], lhsT=W2[:], rhs=H1[i][0][:, :N], start=True, stop=True), [W2b, H1[i][1]], [p2[i][1]])
        sin_layer(P, p2[i][0], p2[i][1], V[:, 2:3], V[:, 6:7], Vb, A[i][0], A[i][1], TF[i][0], TF[i][1], TI[i][0], TI[i][1], H2[i][0], H2[i][1], N)
        nb = max(0, min(N, n - c0))
        if nb > 0:
            P.op("tensor", lambda e, i=i, nb=nb: e.matmul(p3[i][0][:, 0:nb], lhsT=W3[:, 0, :], rhs=H2[i][0][:, 0:nb], start=True, stop=True), [W3b, H2[i][1]], [p3[i][1]])
        if nb < N:
            P.op("tensor", lambda e, i=i, nb=nb, N=N: e.matmul(p3[i][0][:, nb:N], lhsT=W3[:, 1, :], rhs=H2[i][0][:, nb:N], start=True, stop=True), [W3b, H2[i][1]], [p3[i][1]])
        R_, Rb_ = R[i]
        P.op("scalar", lambda e, i=i, N=N, R_=R_: e.copy(out=R_[:, :N], in_=p3[i][0][:, :N]), [p3[i][1]], [Rb_])
        P.op("vector", lambda e, N=N, R_=R_, D_=D_: e.tensor_tensor(out=R_[:, :N], in0=R_[:, :N], in1=D_[:, :N], op=ALU.mult), [Rb_, Db_], [Rb_])
        P.op("scalar", lambda e, i=i, N=N, R_=R_, ch=ch: e.activation(out=J[i][0][:, :N], in_=R_[:, :N], func=AF.Abs, accum_out=SA[:, ch:ch + 1]), [Rb_], [J[i][1], SAb.part(ch)])
        P.op("vector", lambda e, i=i, N=N, R_=R_: e.tensor_copy(out=HB[i][0][:, :N], in_=R_[:, :N]), [Rb_], [HB[i][1]])
        if c0 == 0:
            P.op("vector", lambda e, R_=R_: e.tensor_copy(out=T0[:, 0:1], in_=R_[:, 0:1]), [Rb_], [T0b.part(0)])
        if c0 <= n < c0 + N:
            P.op("vector", lambda e, R_=R_, o=n - c0: e.tensor_copy(out=T0[:, 1:2], in_=R_[:, o:o + 1]), [Rb_], [T0b.part(1)])
        P.dma("gpsimd", Hd[:, c0:c0 + N], HB[i][0][:, :N], [HB[i][1]], [Hdb.part(ch)], HB[i][1])
    P.op("vector", lambda e: e.tensor_reduce(out=SA[:, nch:nch + 1], in_=SA[:, 0:nch], axis=AX.X, op=ALU.add), [SAb], [SAb])
    P.op("vector", lambda e: e.tensor_tensor(out=T0[:, 2:3], in0=T0[:, 0:1], in1=T0[:, 1:2], op=ALU.add), [T0b], [T0b])
    P.op("vector", lambda e: e.scalar_tensor_tensor(out=T0[:, 2:3], in0=SA[:, nch:nch + 1], scalar=V[:, 4:5], in1=T0[:, 2:3], op0=ALU.mult, op1=ALU.add),
         [SAb, Vb, T0b], [T0b])
    tb, tbb = k.sb("tb", [64, 2], BF16)
    P.op("vector", lambda e: e.tensor_copy(out=tb[:, 0:1], in_=T0[:, 2:3]), [T0b], [tbb])
    P.dma("gpsimd", bass.AP(Hd.tensor, Hd.offset + n, [[2 * n, 64], [1, 1]]), tb[:, 0:1], [tbb], [Hdb], tbb, allow_slow_non_contiguous=True)
    P.dma("gpsimd", col_ap(hsum, 0, 64), SA[:, nch:nch + 1], [SAb], [hsumb], SAb)


def hyconv_a_phase(k, n, zc="zc", cw="hcw", cb="hcb", hsum="hsum", identf="ident", identb="identb", Vt="Vt_s", X0t="X0t_s"):
    nc, P = k.nc, k.P
    zc, zcb = k.d(zc); cw, cwb = k.d(cw); cb, cbb = k.d(cb); hsum, hsumb = k.d(hsum); identb, identbb = k.d(identb); Vt, Vtb = k.d(Vt); X0t, X0tb = k.d(X0t)
    NB = n // 128
    CW, CWb = k.sb("CW", [128, 3, 4], F32)
    for b in range(2):
        for part in range(3):
            for tap in range(3):
                P.dma("sync", CW[b * 64:(b + 1) * 64, part, tap:tap + 1], col_ap(cw, (tap * 3 + part) * 64, 64), [cwb], [CWb], CWb)
            P.dma("sync", CW[b * 64:(b + 1) * 64, part, 3:4], col_ap(cb, part * 64, 64), [cbb], [CWb], CWb)
    rS, rSb = k.sb("rS", [128, 2], F32)
    for b in range(2):
        P.dma("sync", rS[b * 64:(b + 1) * 64, 0:1], col_ap(hsum, 0, 64), [hsumb], [rSb], rSb)
    P.op("vector", lambda e: e.reciprocal(out=rS[:, 1:2], in_=rS[:, 0:1]), [rSb], [rSb])
    P.op("vector", lambda e: e.tensor_scalar(out=CW[:, 0, :], in0=CW[:, 0, :], scalar1=rS[:, 1:2], scalar2=None, op0=ALU.mult), [CWb, rSb], [CWb])
    idb, idbb = k.sb("idb", [128, 128], BF16)
    P.dma("sync", idb[:], identb[:, :], [identbb], [idbb], idbb)
    TC = min(1024, n)
    Z = [[k.sb(f"Z{i}_{p}", [128, TC + 2], F32) for p in range(3)] for i in range(2)]
    C = [k.sb(f"C{p}", [128, TC], F32) for p in range(3)]
    VX, VXb = k.sb("VX", [128, TC], BF16)
    X0, X0b = k.sb("X0", [128, TC], BF16)
    VT, VTb = k.sb("VT", [128, 128, NB], BF16)
    XT, XTb = k.sb("XT", [128, 128, NB], BF16)
    pt = [k.ps(f"pt{i}", [128, 4, 128], BF16) for i in range(4)]
    cnt = 0
    for ci in range(n // TC):
        t0 = ci * TC
        for p in range(3):
            Zp, Zpb = Z[ci % 2][p]
            P.dma("gpsimd", Zp[:], zc[p, :, t0:t0 + TC + 2], [zcb], [Zpb], Zpb)
            Cp, Cpb = C[p]
            eng = "vector"
            P.op(eng, lambda e, Cp=Cp, Zp=Zp, p=p: e.tensor_scalar(out=Cp[:], in0=Zp[:, 1:TC + 1], scalar1=CW[:, p, 1:2], scalar2=CW[:, p, 3:4], op0=ALU.mult, op1=ALU.add),
                 [Zpb, CWb], [Cpb])
            P.op(eng, lambda e, Cp=Cp, Zp=Zp, p=p: e.scalar_tensor_tensor(out=Cp[:], in0=Zp[:, 0:TC], scalar=CW[:, p, 0:1], in1=Cp[:], op0=ALU.mult, op1=ALU.add),
                 [Zpb, CWb, Cpb], [Cpb])
            P.op(eng, lambda e, Cp=Cp, Zp=Zp, p=p: e.scalar_tensor_tensor(out=Cp[:], in0=Zp[:, 2:TC + 2], scalar=CW[:, p, 2:3], in1=Cp[:], op0=ALU.mult, op1=ALU.add),
                 [Zpb, CWb, Cpb], [Cpb])
        nbk = TC // 128
        P.op("vector", lambda e, nbk=nbk: e.tensor_tensor(out=VX[:].rearrange("p (k j) -> p k j", k=nbk)[:, :, ::-1], in0=C[2][0][:].rearrange("p (k j) -> p k j", k=nbk),
                                                        in1=C[1][0][:].rearrange("p (k j) -> p k j", k=nbk), op=ALU.mult), [C[2][1], C[1][1]], [VXb])
        P.op("gpsimd", lambda e: e.tensor_copy(out=X0[:], in_=C[0][0][:]), [C[0][1]], [X0b])
        for (src, srcb, dst, dstb) in ((VX, VXb, VT, VTb), (X0, X0b, XT, XTb)):
            for b4 in range(0, nbk, 4):
                nb4 = min(4, nbk - b4)
                PT, PTb = pt[cnt % 4]; cnt += 1
                for q in range(nb4):
                    P.op("tensor", lambda e, PT=PT, src=src, q=q, b4=b4: e.transpose(out=PT[:, q, :], in_=src[:, (b4 + q) * 128:(b4 + q + 1) * 128], identity=idb[:]),
                         [srcb, idbb], [PTb])
                blk0 = t0 // 128 + b4
                P.op("scalar", lambda e, PT=PT, dst=dst, blk0=blk0, nb4=nb4: e.copy(out=dst[:, :, blk0:blk0 + nb4], in_=PT[:, 0:nb4, :].rearrange("p k c -> p c k")),
                     [PTb], [dstb])
    P.dma("gpsimd", Vt[:, :, :], VT[:], [VTb], [Vtb], VTb)
    P.dma("gpsimd", X0t[:, :, :], XT[:], [XTb], [X0tb], XTb)


def hyconv_b_phase(k, n, Hd="Hd", Vt="Vt_s", X0t="X0t_s", hyo="hyo"):
    nc, P = k.nc, k.P
    Hd, Hdb = k.d(Hd); Vt, Vtb = k.d(Vt); X0t, X0tb = k.d(X0t); hyo, hyob = k.d(hyo)
    NB = n // 128
    VT, VTb = k.sb("VT", [128, 128, NB], BF16)
    XT, XTb = k.sb("XT", [128, 128, NB], BF16)
    P.dma("sync", VT[:], Vt[:, :, :], [Vtb], [VTb], VTb)
    P.dma("sync", XT[:], X0t[:, :, :], [X0tb], [XTb], XTb)
    OUT, OUTb = k.sb("OUT", [128, 2 * NB, 64], BF16)
    ds = list(range(-(NB - 1), NB))
    GD = 64
    groups = [ds[i:i + GD] for i in range(0, len(ds), GD)]
    gz = [g for g in groups if 0 in g][0]
    groups = [gz] + [g for g in groups if g is not gz]
    GW = GD * 128 + 128
    G = [k.sb(f"G{i}", [128, GW], BF16) for i in range(4)]
    YS = [k.sb(f"YS{i}", [128, 2, NB], F32) for i in range(2)]
    py = [k.ps(f"py{i}", [128, 2, NB], F32) for i in range(2)]
    gi = 0
    for c in range(64):
        PY, PYb = py[c % 2]
        nmm = len(ds)
        done = 0
        for grp in groups:
            Gt, Gtb = G[gi % 4]; gi += 1
            u0 = n + 128 * grp[0] - 127
            width = 128 * (len(grp) - 1) + 128
            src = bass.AP(Hd.tensor, Hd.offset + c * 2 * n + u0, [[1, 128], [1, width]])
            P.dma("sync", Gt[:, 0:width], src, [Hdb], [Gtb], Gtb)
            order = ([0] + [d for d in grp if d != 0]) if 0 in grp else grp
            for d in order:
                a_lo, a_hi = max(0, d), min(NB, NB + d)
                off = 128 * (d - grp[0])
                done += 1
                P.op("tensor", lambda e, PY=PY, Gt=Gt, off=off, a_lo=a_lo, a_hi=a_hi, d=d, c=c, first=(done == 1), last=(done == nmm):
                     e.matmul(PY[:, :, a_lo:a_hi], lhsT=Gt[:, off:off + 128], rhs=VT[:, c:c + 65:64, a_lo - d:a_hi - d], start=first, stop=last),
                     [Gtb, VTb], [PYb])
        Y, Yb = YS[c % 2]
        P.op("scalar", lambda e, Y=Y, PY=PY: e.copy(out=Y[:], in_=PY[:]), [PYb], [Yb])
        eng = "vector" if c % 2 == 0 else "gpsimd"
        P.op(eng, lambda e, Y=Y, c=c: e.tensor_tensor(out=OUT[:, :, c].rearrange("p (b a) -> p b a", b=2), in0=Y[:], in1=XT[:, c:c + 65:64, :], op=ALU.mult),
             [Yb, XTb], [OUTb])
    AB_ = max(1, min(4, NB))
    for b in range(2):
        for a0 in range(0, NB, AB_):
            dst = bass.AP(hyo.tensor, hyo.offset + b * n * 64 + a0 * 128 * 64, [[64, 128], [128 * 64, AB_], [1, 64]])
            P.dma("gpsimd", dst, OUT[:, b * NB + a0:b * NB + a0 + AB_, :], [OUTb], [hyob.part((b, a0))], OUTb)


N_SEQ = 16384
N_CTX = 256


def build_A():
    pr = Program()
    pa_drams(pr, 0)
    pr.phase(cast_phase, "w32", "w_bf", D, 3584)
    pr.phase(proj_phase)
    return pr.nc


def hy_drams(pr, n, sfx):
    pr.dram("ft" + sfx, [33, 2 * n], F32, "ExternalInput")
    pr.dram("dec" + sfx, [64, 2 * n], F32, "ExternalInput")
    pr.dram("zc" + sfx, [3, 128, n + 2], BF16, "ExternalInput")
    pr.dram("Hd" + sfx, [64, 2 * n], BF16)
    pr.dram("hsum" + sfx, [64], F32)
    pr.dram("Vt_s" + sfx, [128, 128, n // 128], BF16)
    pr.dram("X0t_s" + sfx, [128, 128, n // 128], BF16)
    pr.dram("hyo" + sfx, [2, n, 64], BF16, "ExternalOutput")


def build_B():
    pr = Program()
    pb_drams(pr)
    pr.dram("hw1", [33, 64], F32, "ExternalInput"); pr.dram("hw2", [64, 64], F32, "ExternalInput")
    pr.dram("hw3b", [64, 64], F32, "ExternalInput"); pr.dram("hw3f", [64, 64], F32, "ExternalInput")
    for nm in ("hb1", "hf1", "hb2", "hf2", "hbias"):
        pr.dram(nm, [64], F32, "ExternalInput")
    pr.dram("hcw", [3, 3, 64], F32, "ExternalInput"); pr.dram("hcb", [3, 64], F32, "ExternalInput")
    hy_drams(pr, N_SEQ, "")
    hy_drams(pr, N_CTX, "c")
    pr.phase(attn_phase)
    pr.phase(pool_phase)
    for n, s in ((N_SEQ, ""), (N_CTX, "c")):
        pr.phase(filter_phase, n, ft="ft" + s, dec="dec" + s, Hd="Hd" + s, hsum="hsum" + s)
        pr.phase(hyconv_a_phase, n, zc="zc" + s, hsum="hsum" + s, Vt="Vt_s" + s, X0t="X0t_s" + s)
        pr.phase(hyconv_b_phase, n, Hd="Hd" + s, Vt="Vt_s" + s, X0t="X0t_s" + s, hyo="hyo" + s)
    return pr.nc


def build_C():
    pr = Program()
    pc_drams(pr, x1_external=False)
    pr.phase(cast_phase, "wo32", "wo_bf", D, D)
    pr.phase(cast_phase, "wu32", "wu_bf", D, HID)
    pr.phase(cast_phase, "wd32", "wd_bf", HID, D)
    pr.phase(outproj_phase)
    pr.phase(mlp_phase)
    return pr.nc


def build_CA():
    pr = Program()
    pc_drams(pr, x1_external=False)
    pr.dram("w32", [D, 3584], F32, "ExternalInput")
    pr.dram("w_bf", [D, 3584], BF16)
    for nm in ("modx_n", "modc_n"):
        pr.dram(nm, [6 * D], F32, "ExternalInput")
    pr.dram("gpre", [D], F32, "ExternalInput")
    pr.dram("ropeC", [128, NTOK], F32, "ExternalInput")
    pr.dram("ropeS", [128, NTOK], F32, "ExternalInput")
    pr.dram("prot", [128, 128], BF16, "ExternalInput")
    pr.dram("qT", [128, 8, NTOK], BF16, "ExternalOutput")
    pr.dram("kT", [128, 2, NTOK], BF16, "ExternalOutput")
    pr.dram("v", [NTOK, 256], BF16, "ExternalOutput")
    pr.dram("zT", [1536, NTOK], BF16, "ExternalOutput")
    pr.dram("pT", [512, NTOK], BF16, "ExternalOutput")
    pr.phase(cast_phase, "wo32", "wo_bf", D, D)
    pr.phase(cast_phase, "wu32", "wu_bf", D, HID)
    pr.phase(cast_phase, "wd32", "wd_bf", HID, D)
    pr.phase(cast_phase, "w32", "w_bf", D, 3584)
    pr.phase(outproj_phase)
    pr.phase(mlp_phase)
    pr.phase(proj_phase, xin="x2", modx="modx_n", modc="modc_n")
    return pr.nc


def kernel(**inputs):
    f32 = lambda name: np.ascontiguousarray(np.asarray(inputs[name], np.float32))
    x = f32("x").copy()
    ctx = f32("ctx").copy()
    c3 = np.concatenate([f32("c"), f32("c_ctx")[None]], 0)
    w_mod, b_mod = f32("w_mod"), f32("b_mod")
    cores = list(range(8))
    res = run_bass_kernel_spmd(build_pm(), [dict(c3=c3, wm=np.ascontiguousarray(w_mod[:, :, r * NCOL:(r + 1) * NCOL]),
                                                 bm=np.ascontiguousarray(b_mod[:, r * NCOL:(r + 1) * NCOL])) for r in cores], core_ids=cores)
    mod = np.concatenate([res.results[r]["mod"] for r in cores], axis=2)
    del w_mod
    ncA, ncB, ncC, ncCA = build_A(), build_B(), build_C(), build_CA()
    cols = w_in_cols()
    ident = np.eye(128, dtype=np.float32)
    identb = np.eye(128).astype(BF)
    prot = rot_perm()
    ropes = [rope_tables(j * CHUNK) for j in range(4)]
    masks = [attn_masks(j) for j in range(4)]
    invx = [pool_inv_counts(j * CHUNK, CHUNK, N_SEQ) for j in range(4)]
    invc = pool_inv_counts(0, N_CTX, N_CTX)
    tabs = [hyena_tables(N_SEQ, r) for r in cores]
    tabc = [hyena_tables(N_CTX, r) for r in cores]
    A = None
    for l in range(4):
        g = lambda name, l=l: f32(name)[l]
        xins = [np.concatenate([x[r // 4, (r % 4) * CHUNK:(r % 4 + 1) * CHUNK], ctx[r // 4]], 0) for r in cores]
        if A is None:
            w32 = np.ascontiguousarray(g("w_in")[:, cols])
            ims = [dict(xin=xins[r], w32=w32, modx=mod[l, r // 4], modc=mod[l, 2], gpre=g("g_pre_mix"), ropeC=ropes[r % 4][0], ropeS=ropes[r % 4][1],
                        prot=prot, ident=ident) for r in cores]
            A = run_bass_kernel_spmd(ncA, ims, core_ids=cores).results
            del ims
        hw = g("hy_conv_w"); hb = g("hy_conv_b"); w3 = g("hy_w3"); hbias = g("hy_bias")
        ims = []
        for r in cores:
            b, j = r // 4, r % 4
            grp = [4 * b + i for i in range(4)]
            kTh = np.concatenate([halo_cat([A[q]["kT"][:, :, :CHUNK] for q in grp], j, 2, 128, None), A[r]["kT"][:, :, CHUNK:]], 2)
            vh = np.concatenate([halo_cat([A[q]["v"][:CHUNK] for q in grp], j, 0, 128, None), A[r]["v"][CHUNK:]], 0)
            pTh = halo_cat([A[q]["pT"][:, :CHUNK] for q in grp], j, 1, 8, None)
            pTc = np.concatenate([np.zeros((512, 8), BF), A[r]["pT"][:, CHUNK:], np.zeros((512, 8), BF)], 1)
            zc = np.zeros((3, 128, N_SEQ + 2), BF)
            zcc = np.zeros((3, 128, N_CTX + 2), BF)
            for part in range(3):
                rows = slice(192 * r + part * 64, 192 * r + part * 64 + 64)
                for bb in range(2):
                    for jj in range(4):
                        zc[part, bb * 64:(bb + 1) * 64, 1 + jj * CHUNK:1 + (jj + 1) * CHUNK] = A[4 * bb + jj]["zT"][rows, :CHUNK]
                    zcc[part, bb * 64:(bb + 1) * 64, 1:1 + N_CTX] = A[4 * bb]["zT"][rows, CHUNK:]
            cw = np.stack([hw[:, part * 512 + r * 64: part * 512 + (r + 1) * 64] for part in range(3)], 1)
            cb = np.stack([hb[part * 512 + r * 64: part * 512 + (r + 1) * 64] for part in range(3)], 0)
            ims.append(dict(qT=A[r]["qT"], kTh=np.ascontiguousarray(kTh), vh=np.ascontiguousarray(vh), masks=masks[j], sink=g("attn_sink"), identb=identb,
                            pTh=np.ascontiguousarray(pTh), pTc=pTc, invx=invx[j], invc=invc, pool_w=g("pool_w"), pool_scale=g("pool_scale"),
                            hw1=g("hy_w1"), hw2=g("hy_w2"), hw3f=np.ascontiguousarray(w3[:, r * 64:(r + 1) * 64]),
                            hw3b=np.ascontiguousarray(w3[:, 512 + r * 64:512 + (r + 1) * 64]), hb1=g("hy_b1"), hf1=g("hy_freq1"), hb2=g("hy_b2"), hf2=g("hy_freq2"),
                            hbias=np.ascontiguousarray(hbias[r * 64:(r + 1) * 64]), hcw=np.ascontiguousarray(cw), hcb=np.ascontiguousarray(cb),
                            ft=tabs[r][0], dec=tabs[r][1], zc=zc, ftc=tabc[r][0], decc=tabc[r][1], zcc=zcc))
        A = None
        Bo = run_bass_kernel_spmd(ncB, ims, core_ids=cores).results
        del ims
        ims = []
        for r in cores:
            b, j = r // 4, r % 4
            hy = np.concatenate([np.concatenate([Bo[q]["hyo"][b, j * CHUNK:(j + 1) * CHUNK] for q in cores], 1),
                                 np.concatenate([Bo[q]["hyoc"][b] for q in cores], 1)], 0)
            d = dict(attn=Bo[r]["attn"], hy=np.ascontiguousarray(hy), po=Bo[r]["po"], xin=xins[r], wo32=g("w_out"), wu32=g("w_up"), wd32=g("w_down"),
                     gbr=g("g_branch"), gpost=g("g_post_mix"), gpre2=g("g_pre_mlp"), gpost2=g("g_post_mlp"), modx=mod[l, b], modc=mod[l, 2], ident=ident)
            if l < 3:
                d.update(w32=np.ascontiguousarray(f32("w_in")[l + 1][:, cols]), modx_n=mod[l + 1, b], modc_n=mod[l + 1, 2], gpre=f32("g_pre_mix")[l + 1],
                         ropeC=ropes[j][0], ropeS=ropes[j][1], prot=prot)
            ims.append(d)
        del Bo
        Co = run_bass_kernel_spmd(ncCA if l < 3 else ncC, ims, core_ids=cores).results
        del ims
        for r in cores:
            b, j = r // 4, r % 4
            x[b, j * CHUNK:(j + 1) * CHUNK] = Co[r]["x2"][:CHUNK]
            if j == 0:
                ctx[b] = Co[r]["x2"][CHUNK:]
        A = Co if l < 3 else None
    return x
```

```python
import numpy as np
import ml_dtypes
import concourse.bass as bass
import concourse.mybir as mybir
from concourse.bass_utils import run_bass_kernel_spmd

F32 = mybir.dt.float32
BF16 = mybir.dt.bfloat16
ALU = mybir.AluOpType
AF = mybir.ActivationFunctionType
AX = mybir.AxisListType

EPOCH = 20000


class Buf:
    __slots__ = ("name", "w", "r", "parent", "children", "sem", "cnt")

    def __init__(self, name, parent=None):
        self.name = name
        self.w = None
        self.r = {}
        self.parent = parent
        self.children = {}
        self.sem = None
        self.cnt = 0

    def part(self, key):
        c = self.children.get(key)
        if c is None:
            c = Buf(f"{self.name}.{key}", parent=self)
            self.children[key] = c
        return c


class Op:
    __slots__ = ("eng", "fn", "deps", "is_dma", "sem", "val", "needs_inc", "idx", "owner")

    def __init__(self, eng, fn, is_dma, owner):
        self.eng = eng
        self.fn = fn
        self.deps = []
        self.is_dma = is_dma
        self.owner = owner
        self.sem = None
        self.val = 0
        self.needs_inc = False


class Prog:
    ENGS = ("sync", "scalar", "vector", "gpsimd", "tensor")
    UID = 0

    def __init__(self, nc):
        self.nc = nc
        self.ops = {e: [] for e in self.ENGS}
        self.nops = 0
        self.all_ops = []
        self.final = []
        Prog.UID += 1
        self.uid = Prog.UID

    def buf(self, name):
        return Buf(name)

    def _nodes(self, b):
        nodes = [b]
        if b.children:
            nodes += list(b.children.values())
        if b.parent is not None:
            nodes.append(b.parent)
        return nodes

    def op(self, eng, fn, reads=(), writes=(), dma=False, owner=None):
        o = Op(eng, fn, dma, owner)
        deps = {}

        def add(d):
            if d is None:
                return
            if d.eng == eng and eng == "tensor":
                return
            deps[id(d)] = d

        for b in reads:
            for n in self._nodes(b):
                add(n.w)
        for b in writes:
            for n in self._nodes(b):
                add(n.w)
                for rd in n.r.values():
                    for x in rd:
                        add(x)
        best = {}
        out = []
        for d in deps.values():
            if d.is_dma:
                out.append(d)
            else:
                cur = best.get(d.eng)
                if cur is None or d.idx > cur.idx:
                    best[d.eng] = d
        out += list(best.values())
        for d in out:
            d.needs_inc = True
        o.deps = out
        o.idx = len(self.ops[eng])
        self.ops[eng].append(o)
        self.all_ops.append(o)
        self.nops += 1
        for b in reads:
            lst = b.r.setdefault(eng, [])
            if dma:
                lst.append(o)
            else:
                lst[:] = [o]
        for b in writes:
            b.w = o
            b.r = {}
            for c in b.children.values():
                c.w = o
                c.r = {}
        if dma:
            assert owner is not None
            o.needs_inc = True
        return o

    def dma(self, eng, out, in_, reads, writes, owner, **kw):
        return self.op(eng, lambda e: e.dma_start(out=out, in_=in_, **kw), reads, writes, dma=True, owner=owner)

    def emit(self, final_wait_bufs=()):
        nc = self.nc
        import contextlib
        stack = contextlib.ExitStack()
        eng_sems = {}
        dsems = []
        for o in self.all_ops:
            if o.is_dma:
                b = o.owner
                if b.sem is None:
                    b.sem = nc.alloc_semaphore(name=f"d{self.uid}_{len(dsems)}")
                    dsems.append(b.sem)
                b.cnt += 16
                o.sem = b.sem
                o.val = b.cnt
        for e in self.ENGS:
            cnt = 0
            for o in self.ops[e]:
                if (not o.is_dma) and o.needs_inc:
                    ep = cnt // EPOCH
                    key = (e, ep)
                    if key not in eng_sems:
                        eng_sems[key] = nc.alloc_semaphore(name=f"e{self.uid}_{e}_{ep}")
                    o.sem = eng_sems[key]
                    o.val = cnt % EPOCH + 1
                    cnt += 1
        self.nsems = len(eng_sems)
        final = []
        for o in self.all_ops:
            if o.is_dma:
                final.append((o.sem, o.val))
        with stack, nc.Block() as block:
            def replay(ename, e, extra=()):
                waited = {}
                for o in self.ops[ename]:
                    for d in o.deps:
                        k = id(d.sem)
                        if waited.get(k, 0) >= d.val:
                            continue
                        waited[k] = d.val
                        e.wait_ge(d.sem, d.val)
                    ins = o.fn(e)
                    if o.needs_inc:
                        ins.then_inc(o.sem, 16 if o.is_dma else 1)
                fin = {}
                for (s, v) in extra:
                    if fin.get(id(s), (None, 0))[1] < v:
                        fin[id(s)] = (s, v)
                for (s, v) in fin.values():
                    e.wait_ge(s, v)

            @block.sync
            def _(e):
                replay("sync", e)

            @block.scalar
            def _(e):
                replay("scalar", e)

            @block.vector
            def _(e):
                replay("vector", e)

            @block.gpsimd
            def _(e):
                replay("gpsimd", e, extra=final)

            @block.tensor
            def _(e):
                replay("tensor", e)


def bf16_np(a):
    return np.asarray(a).astype(ml_dtypes.bfloat16)


D = 2048
SEQ = 16384
CTX = 256
GRID_W = 64
NCORE = 8
CHUNK = 4096
BF = ml_dtypes.bfloat16


def rope_tables(tok0, ntok_x=CHUNK, nctx=CTX):
    t = np.arange(tok0, tok0 + ntok_x)
    row = (t // GRID_W).astype(np.float32)
    col = (t % GRID_W).astype(np.float32)
    quarter = 32
    inv_freq = (10000.0 ** (-np.arange(quarter, dtype=np.float32) / quarter)).astype(np.float32)
    C = np.ones((128, ntok_x + nctx), np.float32)
    S = np.zeros((128, ntok_x + nctx), np.float32)
    for d in range(128):
        pos = row if d < 64 else col
        ang = (pos * inv_freq[d % 32]).astype(np.float32)
        C[d, :ntok_x] = np.cos(ang)
        s = np.sin(ang)
        S[d, :ntok_x] = -s if (d % 64) < 32 else s
    return C, S


def rot_perm():
    Pm = np.zeros((128, 128), np.float32)
    for m in range(128):
        partner = m + 32 if (m % 64) < 32 else m - 32
        Pm[partner, m] = 1.0
    return Pm.astype(BF)


def hy_col_perm():
    idx = np.zeros(1536, np.int64)
    for r in range(8):
        for part in range(3):
            for c in range(64):
                idx[r * 192 + part * 64 + c] = part * 512 + r * 64 + c
    return idx


def w_in_cols():
    hp = hy_col_perm()
    return np.concatenate([np.arange(0, 1536), 1536 + hp, np.arange(3072, 3584)])


def attn_masks(chunk_j, nchunks=4):
    s = np.arange(128)[:, None]
    q = np.arange(128)[None, :]
    NEG = -30000.0
    prev = np.where(s >= q, 0.0, NEG)
    nxt = np.where(s <= q, 0.0, NEG)
    allneg = np.full((128, 128), NEG)
    m = [prev, nxt, allneg if chunk_j == 0 else prev, allneg if chunk_j == nchunks - 1 else nxt]
    return np.stack([np.tile(a, (1, 4)) for a in m]).astype(BF)


def pool_inv_counts(tok0, ntok, n):
    t = np.arange(tok0, tok0 + ntok)
    out = np.zeros((4, ntok), np.float32)
    for g, w in enumerate((2, 4, 8, 16)):
        h = w // 2
        cnt = (np.minimum(t + h, n) - np.maximum(t - h, 0)).astype(np.float32)
        out[g] = 1.0 / cnt
    return out


def halo_cat(parts, j, axis, halo, zero_like):
    own = parts[j]
    def take(a, sl):
        idx = [slice(None)] * a.ndim
        idx[axis] = sl
        return a[tuple(idx)]
    zshape = list(own.shape); zshape[axis] = halo
    z = np.zeros(zshape, own.dtype)
    left = take(parts[j - 1], slice(parts[j - 1].shape[axis] - halo, None)) if j > 0 else z
    right = take(parts[j + 1], slice(0, halo)) if j < len(parts) - 1 else z
    return np.concatenate([left, own, right], axis=axis)


def hyena_tables(n, core, width=512):
    m = np.arange(2 * n)
    tau = np.where(m < n, np.where(m == 0, 0, n - m), m - n)
    t_all = np.linspace(0.0, 1.0, n, dtype=np.float32)
    bands = 16
    omega_all = (2.0 * np.pi * np.arange(n, dtype=np.float32) / n).astype(np.float32)
    f = np.linspace(1e-4, bands - 1, bands, dtype=np.float32)[None, :]
    fo = (f * omega_all[:, None]).astype(np.float32)
    feats_all = np.concatenate([t_all[:, None], np.cos(fo), -np.sin(fo)], axis=-1).astype(np.float32)
    ft = np.ascontiguousarray(feats_all[tau].T)
    max_decay = np.log(1e-2) / 0.3
    min_decay = np.log(1e-2) / 1.5
    deltas = np.abs(np.linspace(min_decay, max_decay, width, dtype=np.float32))[core * 64:(core + 1) * 64]
    dec = np.exp(-t_all[tau][None, :] * deltas[:, None]).astype(np.float32)
    return ft, dec


import contextlib, os

D = 2048
NTX = 32
NTC = 2
NT = NTX + NTC
NTOK = NT * 128
EPS = 1e-6


class K:
    def __init__(self, nc, drams, outs):
        self.nc = nc
        self.P = Prog(nc)
        self.stack = contextlib.ExitStack()
        self.drams = drams
        self.outnames = outs

    def d(self, name):
        ap = self.drams[name]
        return ap, self.P.buf(name)

    def sb(self, name, shape, dt):
        t = self.stack.enter_context(self.nc.sbuf_tensor(f"p{self.P.uid}_{name}", list(shape), dt))
        return t, self.P.buf(name)

    def ps(self, name, shape, dt):
        t = self.stack.enter_context(self.nc.psum_tensor(f"p{self.P.uid}_{name}", list(shape), dt))
        return t, self.P.buf(name)


class Program:
    def __init__(self):
        self.nc = bass.Bass("TRN2", target_bir_lowering=False)
        self.drams = {}
        self.outs = []
        self.nphase = 0

    def dram(self, name, shape, dt, kind=None):
        if kind:
            t = self.nc.dram_tensor(name, list(shape), dt, kind=kind)
        else:
            t = self.nc.dram_tensor(name, list(shape), dt)
        self.drams[name] = t.ap()
        if kind == "ExternalOutput":
            self.outs.append(name)
        return self.drams[name]

    def phase(self, fn, *args, **kw):
        nc = self.nc
        self.nphase += 1
        with nc.cleanup_on_exit():
            k = K(nc, self.drams, self.outs)
            outbufs = fn(k, *args, **kw) or []
            with k.stack:
                k.P.emit(final_wait_bufs=outbufs)
            nc.all_engine_barrier()


def load_fm(P, dst, ap1d, off, srcb, dstb, n=D, eng="sync"):
    for kk in range(n // 128):
        src = bass.AP(ap1d.tensor, ap1d.offset + off + kk * 128, [[1, 128], [1, 1]])
        P.dma(eng, dst[:, kk:kk + 1], src, [srcb], [dstb], dstb)


def vec_bc(ap1d, off, n=D):
    return bass.AP(ap1d.tensor, ap1d.offset + off, [[0, 128], [1, n]])


def rstd_from_ss(P, ss, ssb, rstd, rstdb, width, n=1):
    P.op("vector", lambda e: e.tensor_scalar(out=rstd, in0=ss, scalar1=1.0 / width, scalar2=EPS, op0=ALU.mult, op1=ALU.add),
         [ssb], [rstdb])
    P.op("scalar", lambda e: e.activation(out=rstd, in_=rstd, func=AF.Ln), [rstdb], [rstdb])
    P.op("scalar", lambda e: e.activation(out=rstd, in_=rstd, func=AF.Exp, scale=-0.5), [rstdb], [rstdb])


def pa_drams(pr, l):
    pr.dram("xin", [NTOK, D], F32, "ExternalInput")
    pr.dram("w32", [D, 3584], F32, "ExternalInput")
    pr.dram("w_bf", [D, 3584], BF16)
    pr.dram("modx", [6 * D], F32, "ExternalInput")
    pr.dram("modc", [6 * D], F32, "ExternalInput")
    pr.dram("gpre", [D], F32, "ExternalInput")
    pr.dram("ropeC", [128, NTOK], F32, "ExternalInput")
    pr.dram("ropeS", [128, NTOK], F32, "ExternalInput")
    pr.dram("prot", [128, 128], BF16, "ExternalInput")
    pr.dram("ident", [128, 128], F32, "ExternalInput")
    pr.dram("qT", [128, 8, NTOK], BF16, "ExternalOutput")
    pr.dram("kT", [128, 2, NTOK], BF16, "ExternalOutput")
    pr.dram("v", [NTOK, 256], BF16, "ExternalOutput")
    pr.dram("zT", [1536, NTOK], BF16, "ExternalOutput")
    pr.dram("pT", [512, NTOK], BF16, "ExternalOutput")


def build_pa():
    pr = Program()
    pa_drams(pr, 0)
    pr.phase(cast_phase, "w32", "w_bf", D, 3584)
    pr.phase(proj_phase)
    return pr.nc


def cast_phase(k, src, dst, rows, cols):
    P = k.P
    s, sb_ = k.d(src)
    d, db_ = k.d(dst)
    n = 8
    step = rows // n
    for c in range(n):
        P.dma("gpsimd", d[c * step:(c + 1) * step, :], s[c * step:(c + 1) * step, :], [sb_], [db_.part(c)], db_.part(c))


def proj_phase(k, xin="xin", w="w_bf", modx="modx", modc="modc", gpre="gpre", ropeC="ropeC", ropeS="ropeS",
               prot="prot", ident="ident", qT="qT", kT="kT", vo="v", zT="zT", pT="pT"):
    xin, xinb = k.d(xin); w, wb_ = k.d(w); modx, modxb = k.d(modx); modc, modcb = k.d(modc); gpre, gpreb = k.d(gpre)
    ropeC, ropeCb = k.d(ropeC); ropeS, ropeSb = k.d(ropeS); prot, protb = k.d(prot); ident, identb = k.d(ident)
    qT, qTb = k.d(qT); kT, kTb = k.d(kT); vo, vob = k.d(vo); zT, zTb = k.d(zT); pT, pTb = k.d(pT)
    nc, P = k.nc, k.P
    W, Wb = k.sb("W", [128, 16, 3584], BF16)
    wv = w.rearrange("(k p) n -> p k n", p=128)
    for c in range(4):
        P.dma("sync", W[:, 4 * c:4 * c + 4, :], wv[:, 4 * c:4 * c + 4, :], [wb_], [Wb.part(c)], Wb.part(c))
    idt, idtb = k.sb("idt", [128, 128], F32)
    P.dma("sync", idt[:], ident[:, :], [identb], [idtb], idtb)
    prt, prtb = k.sb("prt", [128, 128], BF16)
    P.dma("sync", prt[:], prot[:, :], [protb], [prtb], prtb)
    AB, ABb = k.sb("AB", [128, 2, 2, 16], F32)
    tmpv, tmpvb = k.sb("tmpv", [128, 3, 16], F32)
    load_fm(P, tmpv[:, 0, :], gpre, 0, gpreb, tmpvb.part(0))
    for s, (m, mb) in enumerate(((modx, modxb), (modc, modcb))):
        load_fm(P, tmpv[:, 1 + s, :], m, D, mb, tmpvb.part(1 + s))
        load_fm(P, AB[:, s, 1, :], m, 0, mb, ABb.part(s))
    for s in range(2):
        P.op("vector", lambda e, s=s: e.scalar_tensor_tensor(out=AB[:, s, 0, :], in0=tmpv[:, 1 + s, :], scalar=1.0, in1=tmpv[:, 0, :],
                                                              op0=ALU.add, op1=ALU.mult),
             [tmpvb], [ABb.part(("A", s))])

    NXB = 2
    xt = [k.sb(f"xt{i}", [128, D], F32) for i in range(NXB)]
    ss = [k.sb(f"ss{i}", [128, 2], F32) for i in range(NXB)]
    junk, junkb = k.sb("junk", [128, D], BF16)
    hxT = [k.sb(f"hxT{i}", [128, 16, 512], BF16) for i in range(1)]
    ptr = [k.ps(f"ptr{i}", [128, 4, 128], F32) for i in range(2)]
    pacc = [k.ps(f"pacc{i}", [128, 512], F32) for i in range(3)]
    prot_ps = [k.ps(f"prot{i}", [128, 512], F32) for i in range(2)]
    pv = [k.ps(f"pv{i}", [128, 256], F32) for i in range(1)]
    qb = [k.sb(f"qb{i}", [128, 512], BF16) for i in range(2)]
    rc = [k.sb(f"rc{i}", [128, 512], F32) for i in range(2)]
    rs = [k.sb(f"rs{i}", [128, 512], F32) for i in range(2)]
    t1 = [k.sb(f"t1{i}", [128, 512], F32) for i in range(2)]
    t2 = [k.sb(f"t2{i}", [128, 512], F32) for i in range(2)]
    pf = [k.sb(f"pf{i}", [128, 512], F32) for i in range(2)]
    prf = [k.sb(f"prf{i}", [128, 512], F32) for i in range(2)]
    qo = [k.sb(f"qo{i}", [128, 512], BF16) for i in range(3)]
    zo = [k.sb(f"zo{i}", [128, 512], BF16) for i in range(3)]
    vs = [k.sb(f"vs{i}", [128, 256], BF16) for i in range(2)]

    macro = [(4 * i, 4, 0) for i in range(NTX // 4)] + [(NTX, NTC, 1)]
    import os
    if os.environ.get('NMACRO'): macro = macro[:int(os.environ['NMACRO'])]
    cnt = dict(x=0, tr=0, acc=0, rot=0, q=0, z=0, v=0)
    for mi, (t0, nt, mset) in enumerate(macro):
        ntok = nt * 128
        tok0 = t0 * 128
        H, Hb = hxT[0]
        RC, RCb = rc[mi % 2]
        RS, RSb = rs[mi % 2]
        P.dma("sync", RC[:, :ntok], ropeC[:, tok0:tok0 + ntok], [ropeCb], [RCb], RCb)
        P.dma("sync", RS[:, :ntok], ropeS[:, tok0:tok0 + ntok], [ropeSb], [RSb], RSb)
        for j in range(nt):
            X, Xb = xt[cnt["x"] % NXB]
            S, Sb = ss[cnt["x"] % NXB]
            cnt["x"] += 1
            r0 = (t0 + j) * 128
            P.dma("sync", X[:], xin[r0:r0 + 128, :], [xinb], [Xb], Xb)
            P.op("scalar", lambda e, X=X, S=S: e.activation(out=junk[:], in_=X[:], func=AF.Square, accum_out=S[:, 0:1]),
                 [Xb], [junkb, Sb])
            rstd_from_ss(P, S[:, 0:1], Sb, S[:, 1:2], Sb, D)
            P.op("scalar", lambda e, X=X, S=S: e.activation(out=X[:], in_=X[:], func=AF.Copy, scale=S[:, 1:2]),
                 [Xb, Sb], [Xb])
            for kg in range(4):
                PT, PTb = ptr[cnt["tr"] % 2]
                cnt["tr"] += 1
                for k4 in range(4):
                    kk = kg * 4 + k4
                    P.op("tensor", lambda e, PT=PT, X=X, k4=k4, kk=kk: e.transpose(out=PT[:, k4, :], in_=X[:, kk * 128:(kk + 1) * 128], identity=idt[:]),
                         [Xb, idtb], [PTb])
                for k4 in range(4):
                    kk = kg * 4 + k4
                    P.op("scalar", lambda e, ntok=ntok, PT=PT, H=H, kk=kk, k4=k4, j=j, mset=mset: e.activation(
                        out=H[:, kk, j * 128:(j + 1) * 128], in_=PT[:, k4, :], func=AF.Identity,
                        scale=AB[:, mset, 0, kk:kk + 1], bias=AB[:, mset, 1, kk:kk + 1]),
                         [PTb, ABb], [Hb])
        STAGE = int(os.environ.get('PA_STAGE', '9'))
        chunks = [("q", i, i * 128) for i in range(8)] + [("k", i, 1024 + i * 128) for i in range(2)] + \
                 [("z", i, 1536 + i * 128) for i in range(12)] + [("p", i, 3072 + i * 128) for i in range(4)]
        if STAGE == 1:
            chunks = []
            for kk in range(12):
                P.dma('gpsimd', zT[kk * 128:(kk + 1) * 128, tok0:tok0 + ntok].bitcast(BF16)[:, :ntok], H[:, kk, :ntok], [Hb], [zTb.part(kk)], Hb)
        if STAGE == 2:
            chunks = [c for c in chunks if c[0] in ('z', 'p')]
        for (kind, ci, col) in chunks:
            PA_, PAb = pacc[cnt["acc"] % 3]
            cnt["acc"] += 1
            for kk in range(16):
                P.op("tensor", lambda e, ntok=ntok, PA_=PA_, kk=kk, col=col, H=H: e.matmul(PA_[:, :ntok], lhsT=W[:, kk, col:col + 128], rhs=H[:, kk, :ntok],
                                                                            start=(kk == 0), stop=(kk == 15)),
                     [Wb, Hb], [PAb])
            if kind in ("q", "k"):
                QB, QBb = qb[cnt["rot"] % 2]
                PR, PRb = prot_ps[cnt["rot"] % 2]
                T1, T1b = t1[cnt["rot"] % 2]
                cnt["rot"] += 1
                QO, QOb = qo[cnt["q"] % 3]
                cnt["q"] += 1
                PF, PFb = pf[(cnt["rot"] - 1) % 2]
                PRF, PRFb = prf[(cnt["rot"] - 1) % 2]
                T2, T2b = t2[(cnt["rot"] - 1) % 2]
                P.op("scalar", lambda e, ntok=ntok, PF=PF, PA_=PA_: e.copy(out=PF[:, :ntok], in_=PA_[:, :ntok]), [PAb], [PFb])
                P.op("gpsimd", lambda e, ntok=ntok, QB=QB, PF=PF: e.tensor_copy(out=QB[:, :ntok], in_=PF[:, :ntok]), [PFb], [QBb])
                P.op("tensor", lambda e, ntok=ntok, PR=PR, QB=QB: e.matmul(PR[:, :ntok], lhsT=prt[:], rhs=QB[:, :ntok], start=True, stop=True),
                     [prtb, QBb], [PRb])
                P.op("scalar", lambda e, ntok=ntok, PRF=PRF, PR=PR: e.copy(out=PRF[:, :ntok], in_=PR[:, :ntok]), [PRb], [PRFb])
                P.op("vector", lambda e, ntok=ntok, T1=T1, PF=PF, RC=RC: e.tensor_tensor(out=T1[:, :ntok], in0=PF[:, :ntok], in1=RC[:, :ntok], op=ALU.mult),
                     [PFb, RCb], [T1b])
                P.op("vector", lambda e, ntok=ntok, T2=T2, PRF=PRF, RS=RS: e.tensor_tensor(out=T2[:, :ntok], in0=PRF[:, :ntok], in1=RS[:, :ntok], op=ALU.mult),
                     [PRFb, RSb], [T2b])
                P.op("vector", lambda e, ntok=ntok, QO=QO, T1=T1, T2=T2: e.tensor_tensor(out=QO[:, :ntok], in0=T1[:, :ntok], in1=T2[:, :ntok], op=ALU.add),
                     [T1b, T2b], [QOb])
                dst = (qT if kind == "q" else kT)
                dstb = (qTb if kind == "q" else kTb)
                P.dma("gpsimd", dst[:, ci, tok0:tok0 + ntok], QO[:, :ntok], [QOb], [dstb.part((mi, ci))], QOb)
            else:
                ZO, ZOb = zo[cnt["z"] % 3]
                cnt["z"] += 1
                P.op("scalar", lambda e, ntok=ntok, ZO=ZO, PA_=PA_: e.copy(out=ZO[:, :ntok], in_=PA_[:, :ntok]), [PAb], [ZOb])
                dst, dstb = (zT, zTb) if kind == "z" else (pT, pTb)
                P.dma("gpsimd", dst[ci * 128:(ci + 1) * 128, tok0:tok0 + ntok], ZO[:, :ntok], [ZOb], [dstb.part((mi, ci))], ZOb)
        for j in range(nt if STAGE >= 4 else 0):
            PV, PVb = pv[0]
            for kk in range(16):
                P.op("tensor", lambda e, PV=PV, kk=kk, j=j, H=H: e.matmul(PV[:], lhsT=H[:, kk, j * 128:(j + 1) * 128], rhs=W[:, kk, 1280:1536],
                                                                      start=(kk == 0), stop=(kk == 15)),
                     [Wb, Hb], [PVb])
            VS, VSb = vs[cnt["v"] % 2]
            cnt["v"] += 1
            P.op("scalar", lambda e, VS=VS, PV=PV: e.copy(out=VS[:], in_=PV[:]), [PVb], [VSb])
            r0 = (t0 + j) * 128
            P.dma("gpsimd", vo[r0:r0 + 128, :], VS[:], [VSb], [vob.part((mi, j))], VSb)


NCOL = 1536


def build_pm():
    pr = Program()
    pr.dram("c3", [3, D], F32, "ExternalInput")
    pr.dram("wm", [4, D, NCOL], F32, "ExternalInput")
    pr.dram("bm", [4, NCOL], F32, "ExternalInput")
    pr.dram("mod", [4, 3, NCOL], F32, "ExternalOutput")
    pr.phase(mod_phase)
    return pr.nc


def mod_phase(k, c3="c3", wm="wm", bm="bm", mo="mod"):
    c3, c3b = k.d(c3); wm, wmb = k.d(wm); bm, bmb = k.d(bm); mo, mob = k.d(mo)
    nc, P = k.nc, k.P
    cT, cTb = k.sb("cT", [128, 16, 3], F32)
    sT, sTb = k.sb("sT", [128, 16, 3], F32)
    for row in range(3):
        for kk in range(16):
            src = bass.AP(c3.tensor, c3.offset + row * D + kk * 128, [[1, 128], [1, 1]])
            P.dma("sync", cT[:, kk, row:row + 1], src, [c3b], [cTb], cTb)
    P.op("scalar", lambda e: e.activation(out=sT[:], in_=cT[:], func=AF.Silu), [cTb], [sTb])
    wt = [k.sb(f"wt{i}", [128, 16, 512], F32) for i in range(2)]
    pm = [k.ps(f"pm{i}", [3, 512], F32) for i in range(2)]
    ms = [k.sb(f"ms{i}", [3, 512], F32) for i in range(2)]
    bs = [k.sb(f"bs{i}", [3, 512], F32) for i in range(2)]
    i = 0
    for l in range(4):
        wv = wm[l].rearrange("(k p) n -> p k n", p=128)
        for n in range(NCOL // 512):
            WT, WTb = wt[i % 2]
            PM, PMb = pm[i % 2]
            MS, MSb = ms[i % 2]
            BS, BSb = bs[i % 2]
            i += 1
            for h in range(2):
                P.dma("sync", WT[:, 8 * h:8 * h + 8, :], wv[:, 8 * h:8 * h + 8, n * 512:(n + 1) * 512], [wmb], [WTb.part(h)], WTb.part(h))
            bsrc = bass.AP(bm.tensor, bm.offset + l * NCOL + n * 512, [[0, 3], [1, 512]])
            P.dma("sync", BS[:], bsrc, [bmb], [BSb], BSb)
            for kk in range(16):
                P.op("tensor", lambda e, PM=PM, WT=WT, kk=kk: e.matmul(PM[:], lhsT=sT[:, kk, :], rhs=WT[:, kk, :], start=(kk == 0), stop=(kk == 15)),
                     [sTb, WTb], [PMb])
            P.op("scalar", lambda e, MS=MS, PM=PM: e.copy(out=MS[:], in_=PM[:]), [PMb], [MSb])
            P.op("vector", lambda e, MS=MS, BS=BS: e.tensor_tensor(out=MS[:], in0=MS[:], in1=BS[:], op=ALU.add), [MSb, BSb], [MSb])
            P.dma("gpsimd", mo[l, :, n * 512:(n + 1) * 512], MS[:], [MSb], [mob.part((l, n))], MSb)


HID = 4 * D


def rstd_multi(P, S, Sb, widths):
    for g, wd in enumerate(widths):
        P.op("vector", lambda e, g=g, wd=wd: e.tensor_scalar(out=S[:, 4 + g:5 + g], in0=S[:, g:g + 1], scalar1=1.0 / wd, scalar2=EPS,
                                                              op0=ALU.mult, op1=ALU.add), [Sb], [Sb])
    n = len(widths)
    P.op("scalar", lambda e: e.activation(out=S[:, 4:4 + n], in_=S[:, 4:4 + n], func=AF.Ln), [Sb], [Sb])
    P.op("scalar", lambda e: e.activation(out=S[:, 4:4 + n], in_=S[:, 4:4 + n], func=AF.Exp, scale=-0.5), [Sb], [Sb])


def norm_T(P, X, Xb, S, Sb, groups, junk, junkb, ptr, cnt, idt, idtb, scale_fn, bias_fn, vecb, H, Hb, hcol0):
    for g, (c0, c1) in enumerate(groups):
        P.op("scalar", lambda e, g=g, c0=c0, c1=c1: e.activation(out=junk[:, c0:c1], in_=X[:, c0:c1], func=AF.Square, accum_out=S[:, g:g + 1]),
             [Xb], [junkb, Sb])
    rstd_multi(P, S, Sb, [c1 - c0 for (c0, c1) in groups])
    for g, (c0, c1) in enumerate(groups):
        P.op("scalar", lambda e, g=g, c0=c0, c1=c1: e.activation(out=X[:, c0:c1], in_=X[:, c0:c1], func=AF.Copy, scale=S[:, 4 + g:5 + g]),
             [Xb, Sb], [Xb])
    for kg in range(4):
        PT, PTb = ptr[cnt["tr"] % len(ptr)]
        cnt["tr"] += 1
        for k4 in range(4):
            kk = kg * 4 + k4
            P.op("tensor", lambda e, PT=PT, k4=k4, kk=kk: e.transpose(out=PT[:, k4, :], in_=X[:, kk * 128:(kk + 1) * 128], identity=idt[:]),
                 [Xb, idtb], [PTb])
        for k4 in range(4):
            kk = kg * 4 + k4
            if bias_fn is None:
                P.op("scalar", lambda e, PT=PT, kk=kk, k4=k4: e.activation(out=H[:, kk, hcol0:hcol0 + 128], in_=PT[:, k4, :], func=AF.Copy,
                                                                        scale=scale_fn(kk)), [PTb, vecb], [Hb])
            else:
                P.op("scalar", lambda e, PT=PT, kk=kk, k4=k4: e.activation(out=H[:, kk, hcol0:hcol0 + 128], in_=PT[:, k4, :], func=AF.Identity,
                                                                        scale=scale_fn(kk), bias=bias_fn(kk)), [PTb, vecb], [Hb])


def load_bc(P, dst, dstb, ap1d, off, srcb, n=D, eng="sync"):
    P.dma(eng, dst, vec_bc(ap1d, off, n), [srcb], [dstb], dstb)


def outproj_phase(k, attn="attn", hy="hy", po="po", xin="xin", wo="wo_bf", gbr="gbr", gpost="gpost", modx="modx", modc="modc",
                  ident="ident", x1="x1"):
    nc, P = k.nc, k.P
    attn, attnb = k.d(attn); hy, hyb = k.d(hy); po, pob = k.d(po); xin, xinb = k.d(xin); wo, wob = k.d(wo)
    gbr, gbrb = k.d(gbr); gpost, gpostb = k.d(gpost); modx, modxb = k.d(modx); modc, modcb = k.d(modc)
    ident, identb = k.d(ident); x1, x1b = k.d(x1)
    W, Wb = k.sb("Wo", [128, 16, D], BF16)
    wv = wo.rearrange("(k p) n -> p k n", p=128)
    for c in range(4):
        P.dma("sync", W[:, 4 * c:4 * c + 4, :], wv[:, 4 * c:4 * c + 4, :], [wob], [Wb.part(c)], Wb.part(c))
    idt, idtb = k.sb("idt", [128, 128], F32)
    P.dma("sync", idt[:], ident[:, :], [identb], [idtb], idtb)
    gb, gbb = k.sb("gb", [128, 16], F32)
    load_fm(P, gb[:, :], gbr, 0, gbrb, gbb)
    G1 = [k.sb(f"G1_{s}", [128, D], F32) for s in range(2)]
    gtmp, gtmpb = k.sb("gtmp", [128, D], F32)
    for s, (m, mb) in enumerate(((modx, modxb), (modc, modcb))):
        load_bc(P, G1[s][0][:], G1[s][1], gpost, 0, gpostb)
        load_bc(P, gtmp[:], gtmpb, m, 2 * D, mb)
        P.op("vector", lambda e, s=s: e.tensor_tensor(out=G1[s][0][:], in0=G1[s][0][:], in1=gtmp[:], op=ALU.mult), [G1[s][1], gtmpb], [G1[s][1]])
    Ms = [k.sb(f"M{i}", [128, D], F32) for i in range(2)]
    Xs = [k.sb(f"X{i}", [128, D], F32) for i in range(2)]
    Os = [k.sb(f"O{i}", [128, D], F32) for i in range(2)]
    Ss = [k.sb(f"S{i}", [128, 8], F32) for i in range(2)]
    S2s = [k.sb(f"S2{i}", [128, 8], F32) for i in range(2)]
    mT = [k.sb(f"mT{i}", [128, 16, 128], BF16) for i in range(2)]
    junk, junkb = k.sb("junk", [128, D], BF16)
    ptr = [k.ps(f"ptr{i}", [128, 4, 128], F32) for i in range(2)]
    pout = [k.ps(f"pout{i}", [128, D], F32) for i in range(1)]
    cnt = dict(tr=0)
    groups = [(0, 1024), (1024, 1536), (1536, 2048)]
    for t in range(NT):
        mset = 0 if t < NTX else 1
        r0 = t * 128
        M, Mb = Ms[t % 2]; X, Xb = Xs[t % 2]; O, Ob = Os[t % 2]; S, Sb = Ss[t % 2]; S2, S2b = S2s[t % 2]
        H, Hb = mT[t % 2]
        P.dma("gpsimd", M[:, 0:1024], attn[r0:r0 + 128, :], [attnb], [Mb.part(0)], Mb.part(0))
        P.dma("gpsimd", M[:, 1024:1536], hy[r0:r0 + 128, :], [hyb], [Mb.part(1)], Mb.part(1))
        P.dma("gpsimd", M[:, 1536:2048], po[r0:r0 + 128, :], [pob], [Mb.part(2)], Mb.part(2))
        P.dma("sync", X[:], xin[r0:r0 + 128, :], [xinb], [Xb], Xb)
        norm_T(P, M, Mb, S, Sb, groups, junk, junkb, ptr, cnt, idt, idtb, lambda kk: gb[:, kk:kk + 1], None, gbb, H, Hb, 0)
        PO, POb = pout[0]
        for cg in range(4):
            for kk in range(16):
                P.op("tensor", lambda e, PO=PO, H=H, kk=kk, cg=cg: e.matmul(PO[:, cg * 512:(cg + 1) * 512], lhsT=H[:, kk, :], rhs=W[:, kk, cg * 512:(cg + 1) * 512],
                                                                        start=(kk == 0), stop=(kk == 15)), [Hb, Wb], [POb])
        P.op("scalar", lambda e, PO=PO, S2=S2: e.activation(out=junk[:], in_=PO[:], func=AF.Square, accum_out=S2[:, 0:1]), [POb], [junkb, S2b])
        rstd_multi(P, S2, S2b, [D])
        P.op("scalar", lambda e, PO=PO, O=O, S2=S2: e.activation(out=O[:], in_=PO[:], func=AF.Copy, scale=S2[:, 4:5]), [POb, S2b], [Ob])
        P.op("vector", lambda e, O=O, mset=mset: e.tensor_tensor(out=O[:], in0=O[:], in1=G1[mset][0][:], op=ALU.mult), [Ob, G1[mset][1]], [Ob])
        P.op("gpsimd", lambda e, O=O, X=X: e.tensor_tensor(out=O[:], in0=O[:], in1=X[:], op=ALU.add), [Ob, Xb], [Ob])
        P.dma("gpsimd", x1[r0:r0 + 128, :], O[:], [Ob], [x1b.part(t)], Ob)


def mlp_phase(k, x1="x1", wu="wu_bf", wd="wd_bf", gpre="gpre2", gpost="gpost2", modx="modx", modc="modc", ident="ident", x2="x2"):
    nc, P = k.nc, k.P
    x1, x1b = k.d(x1); wu, wub = k.d(wu); wd, wdb = k.d(wd); gpre, gpreb = k.d(gpre); gpost, gpostb = k.d(gpost)
    modx, modxb = k.d(modx); modc, modcb = k.d(modc); ident, identb = k.d(ident); x2, x2b = k.d(x2)
    idt, idtb = k.sb("idt", [128, 128], F32)
    P.dma("sync", idt[:], ident[:, :], [identb], [idtb], idtb)
    AB, ABb = k.sb("AB", [128, 2, 2, 16], F32)
    tmpv, tmpvb = k.sb("tmpv", [128, 3, 16], F32)
    load_fm(P, tmpv[:, 0, :], gpre, 0, gpreb, tmpvb.part(0))
    for s, (m, mb) in enumerate(((modx, modxb), (modc, modcb))):
        load_fm(P, tmpv[:, 1 + s, :], m, 4 * D, mb, tmpvb.part(1 + s))
        load_fm(P, AB[:, s, 1, :], m, 3 * D, mb, ABb.part(s))
    for s in range(2):
        P.op("vector", lambda e, s=s: e.scalar_tensor_tensor(out=AB[:, s, 0, :], in0=tmpv[:, 1 + s, :], scalar=1.0, in1=tmpv[:, 0, :],
                                                              op0=ALU.add, op1=ALU.mult), [tmpvb], [ABb.part(("A", s))])
    G2, G2b = k.sb("G2", [128, D], F32)
    Xs = [k.sb(f"X{i}", [128, D], F32) for i in range(2)]
    gtmp, gtmpb = Xs[1]
    Ss = [k.sb(f"S{i}", [128, 8], F32) for i in range(2)]
    S2, S2b = k.sb("S2", [128, 4, 8], F32)
    O2, O2b = k.sb("O2", [128, 4, D], F32)
    h2T, h2Tb = k.sb("h2T", [128, 16, 512], BF16)
    hidT, hidTb = k.sb("hidT", [128, 64, 512], BF16)
    ws = [k.sb(f"ws{i}", [128, 16, 512], BF16) for i in range(2)]
    rl = [k.sb(f"rl{i}", [128, 512], F32) for i in range(2)]
    junk2, junk2b = k.sb("junk2", [128, 512], BF16)
    ptr = [k.ps(f"ptr{i}", [128, 4, 128], F32) for i in range(2)]
    pup = [k.ps(f"pup{i}", [128, 512], F32) for i in range(2)]
    pdn = [k.ps(f"pdn{i}", [128, 512], F32) for i in range(4)]
    wuv = wu.rearrange("(k p) n -> p k n", p=128)
    wdv = wd.rearrange("(k p) n -> p k n", p=128)
    macro = [(4 * i, 4, 0) for i in range(NTX // 4)] + [(NTX, NTC, 1)]
    cnt = dict(tr=0, x=0, w=0, up=0)
    if os.environ.get('MLP_NMACRO'): macro = macro[:int(os.environ['MLP_NMACRO'])]
    cur_set = None
    for mi, (t0, nt, mset) in enumerate(macro):
        ntok = nt * 128
        if mset != cur_set:
            cur_set = mset
            m, mb = (modx, modxb) if mset == 0 else (modc, modcb)
            load_bc(P, G2[:], G2b, gpost, 0, gpostb)
            load_bc(P, gtmp[:], gtmpb, m, 5 * D, mb)
            P.op("vector", lambda e: e.tensor_tensor(out=G2[:], in0=G2[:], in1=gtmp[:], op=ALU.mult), [G2b, gtmpb], [G2b])
        junkv = O2[:, 0, :].bitcast(BF16)[:, 0:D]
        for j in range(nt):
            X, Xb = Xs[cnt["x"] % 2]; S, Sb = Ss[cnt["x"] % 2]
            cnt["x"] += 1
            r0 = (t0 + j) * 128
            P.dma("sync", X[:], x1[r0:r0 + 128, :], [x1b], [Xb], Xb)
            norm_T(P, X, Xb, S, Sb, [(0, D)], junkv, O2b, ptr, cnt, idt, idtb,
                   lambda kk, mset=mset: AB[:, mset, 0, kk:kk + 1], lambda kk, mset=mset: AB[:, mset, 1, kk:kk + 1], ABb, h2T, h2Tb, j * 128)
        for sl in range(16):
            WS, WSb = ws[cnt["w"] % 2]
            cnt["w"] += 1
            for h in range(2):
                P.dma("sync", WS[:, 8 * h:8 * h + 8, :], wuv[:, 8 * h:8 * h + 8, sl * 512:(sl + 1) * 512], [wub], [WSb.part(h)], WSb.part(h))
            for c4 in range(4):
                hc = sl * 4 + c4
                PU, PUb = pup[cnt["up"] % 2]; RL, RLb = rl[cnt["up"] % 2]
                cnt["up"] += 1
                for kk in range(16):
                    P.op("tensor", lambda e, ntok=ntok, PU=PU, WS=WS, kk=kk, c4=c4: e.matmul(PU[:, :ntok], lhsT=WS[:, kk, c4 * 128:(c4 + 1) * 128], rhs=h2T[:, kk, :ntok],
                                                                              start=(kk == 0), stop=(kk == 15)), [WSb, h2Tb], [PUb])
                P.op("scalar", lambda e, ntok=ntok, PU=PU, RL=RL: e.activation(out=RL[:, :ntok], in_=PU[:, :ntok], func=AF.Relu), [PUb], [RLb])
                eng = "vector" if hc % 2 == 0 else "gpsimd"
                P.op(eng, lambda e, ntok=ntok, RL=RL, hc=hc: e.tensor_tensor(out=hidT[:, hc, :ntok], in0=RL[:, :ntok], in1=RL[:, :ntok], op=ALU.mult),
                     [RLb], [hidTb.part(hc)])
        for cg in range(4):
            for q in range(4):
                WS, WSb = ws[cnt["w"] % 2]
                cnt["w"] += 1
                for h in range(2):
                        P.dma("sync", WS[:, 8 * h:8 * h + 8, :], wdv[:, q * 16 + 8 * h:q * 16 + 8 * h + 8, cg * 512:(cg + 1) * 512], [wdb], [WSb.part(h)], WSb.part(h))
                for j in range(nt):
                    PD, PDb = pdn[j]
                    for c in range(16):
                        hc = q * 16 + c
                        P.op("tensor", lambda e, PD=PD, WS=WS, c=c, hc=hc, j=j: e.matmul(PD[:], lhsT=hidT[:, hc, j * 128:(j + 1) * 128], rhs=WS[:, c, :],
                                                                                   start=(hc == 0), stop=(hc == 63)), [hidTb, WSb], [PDb])
            for j in range(nt):
                PD, PDb = pdn[j]
                P.op("scalar", lambda e, PD=PD, j=j, cg=cg: e.activation(out=junk2[:], in_=PD[:], func=AF.Square, accum_out=S2[:, j, cg:cg + 1]),
                     [PDb], [junk2b, S2b.part(j)])
                P.op("scalar", lambda e, PD=PD, j=j, cg=cg: e.copy(out=O2[:, j, cg * 512:(cg + 1) * 512], in_=PD[:]), [PDb], [O2b.part(j)])
        for j in range(nt):
            X, Xb = Xs[cnt["x"] % 2]
            cnt["x"] += 1
            r0 = (t0 + j) * 128
            P.dma("sync", X[:], x1[r0:r0 + 128, :], [x1b], [Xb], Xb)
            S2j = S2[:, j, :]
            P.op("vector", lambda e, S2j=S2j: e.tensor_tensor(out=S2j[:, 0:2], in0=S2j[:, 0:2], in1=S2j[:, 2:4], op=ALU.add), [S2b.part(j)], [S2b.part(j)])
            P.op("vector", lambda e, S2j=S2j: e.tensor_tensor(out=S2j[:, 0:1], in0=S2j[:, 0:1], in1=S2j[:, 1:2], op=ALU.add), [S2b.part(j)], [S2b.part(j)])
            rstd_multi(P, S2j, S2b.part(j), [D])
            P.op("vector", lambda e, j=j, S2j=S2j: e.scalar_tensor_tensor(out=O2[:, j, :], in0=O2[:, j, :], scalar=S2j[:, 4:5], in1=G2[:],
                                                                     op0=ALU.mult, op1=ALU.mult), [O2b.part(j), S2b.part(j), G2b], [O2b.part(j)])
            P.op("gpsimd", lambda e, j=j, X=X: e.tensor_tensor(out=O2[:, j, :], in0=O2[:, j, :], in1=X[:], op=ALU.add), [O2b.part(j), Xb], [O2b.part(j)])
            P.dma("gpsimd", x2[r0:r0 + 128, :], O2[:, j, :], [O2b.part(j)], [x2b.part((mi, j))], O2b.part(j))


def pc_drams(pr, x1_external=True):
    pr.dram("attn", [NTOK, 1024], BF16, "ExternalInput")
    pr.dram("hy", [NTOK, 512], BF16, "ExternalInput")
    pr.dram("po", [NTOK, 512], BF16, "ExternalInput")
    pr.dram("xin", [NTOK, D], F32, "ExternalInput")
    pr.dram("wo32", [D, D], F32, "ExternalInput")
    pr.dram("wu32", [D, HID], F32, "ExternalInput")
    pr.dram("wd32", [HID, D], F32, "ExternalInput")
    pr.dram("wo_bf", [D, D], BF16)
    pr.dram("wu_bf", [D, HID], BF16)
    pr.dram("wd_bf", [HID, D], BF16)
    for n in ("gbr", "gpost", "gpre2", "gpost2"):
        pr.dram(n, [D], F32, "ExternalInput")
    pr.dram("modx", [6 * D], F32, "ExternalInput")
    pr.dram("modc", [6 * D], F32, "ExternalInput")
    pr.dram("ident", [128, 128], F32, "ExternalInput")
    pr.dram("x1", [NTOK, D], F32, "ExternalOutput") if x1_external else pr.dram("x1", [NTOK, D], F32)
    pr.dram("x2", [NTOK, D], F32, "ExternalOutput")


def build_pc():
    pr = Program()
    pc_drams(pr)
    pr.phase(cast_phase, "wo32", "wo_bf", D, D)
    pr.phase(cast_phase, "wu32", "wu_bf", D, HID)
    pr.phase(cast_phase, "wd32", "wd_bf", HID, D)
    pr.phase(outproj_phase)
    pr.phase(mlp_phase)
    return pr.nc


NKX = NTX + 2
NKB = NKX + NTC
SCALE = 128 ** -0.5


def attn_phase(k, **kw):
    for _ in attn_gen(k, **kw):
        pass


def attn_gen(k, qT="qT", kTh="kTh", vh="vh", masks="masks", sink="sink", identb="identb", attn="attn", npst=4, npso=3):
    nc, P = k.nc, k.P
    qT, qTb = k.d(qT); kTh, kThb = k.d(kTh); vh, vhb = k.d(vh); masks, masksb = k.d(masks); sink, sinkb = k.d(sink)
    identd, identdb = k.d(identb); attn, attnb = k.d(attn)
    KT, KTb = k.sb("KT", [128, 2, NKB * 128], BF16)
    for g in range(2):
        P.dma("sync", KT[:, g, :], kTh[:, g, :], [kThb], [KTb.part(g)], KTb.part(g))
    VA, VAb = k.sb("VA", [128, NKB, 2, 130], BF16)
    P.op("vector", lambda e: e.memset(VA[:], 1.0), [], [VAb])
    vhv = vh.rearrange("(b p) (g d) -> p b g d", p=128, g=2)
    for g in range(2):
        for h in range(4):
            b0, b1 = h * 9, min(NKB, (h + 1) * 9)
            P.dma("sync", VA[:, b0:b1, g, 0:128], vhv[:, b0:b1, g, :], [vhb], [VAb], VAb)
    MK, MKb = k.sb("MK", [128, 4, 512], BF16)
    P.dma("sync", MK[:], masks.rearrange("m p n -> p m n"), [masksb], [MKb], MKb)
    idb, idbb = k.sb("idb", [128, 128], BF16)
    P.dma("sync", idb[:], identd[:, :], [identdb], [idbb], idbb)
    es, esb = k.sb("es", [128, 8], F32)
    P.dma("sync", es[:], bass.AP(sink.tensor, sink.offset, [[0, 128], [1, 8]]), [sinkb], [esb], esb)
    P.op("scalar", lambda e: e.activation(out=es[:], in_=es[:], func=AF.Exp), [esb], [esb])
    Qs = [k.sb(f"Q{i}", [128, 8, 128], BF16) for i in range(2)]
    PTs = [k.sb(f"PT{i}", [128, 512], BF16) for i in range(10)]
    Ofs = [k.sb(f"Of{i}", [128, 132], F32) for i in range(3)]
    dn = [k.sb(f"dn{i}", [128, 2], F32) for i in range(3)]
    AT = [k.sb(f"AT{i}", [128, 1024], BF16) for i in range(2)]
    pst = [k.ps(f"pst{i}", [128, 512], F32) for i in range(npst)]
    pso = [k.ps(f"pso{i}", [128, 512], F32) for i in range(npso)]
    c = dict(st=0, pt=0, o=0)
    for t in range(NT):
        Q, Qb = Qs[t % 2]
        A, Ab = AT[t % 2]
        P.dma("sync", Q[:], qT[:, :, t * 128:(t + 1) * 128], [qTb], [Qb], Qb)
        if t < NTX:
            kbs = [(t, 0 if t > 0 else 2), (t + 1, None), (t + 2, 1 if t < NTX - 1 else 3), (NKX, None), (NKX + 1, None)]
        else:
            kbs = [(NKX, None), (NKX + 1, None)]
        for g in range(2):
            pts = []
            for (kb, mk) in kbs:
                ST, STb = pst[c["st"] % npst]; c["st"] += 1
                PT, PTb = PTs[c["pt"] % 10]; c["pt"] += 1
                P.op("tensor", lambda e, ST=ST, kb=kb, g=g, Q=Q, mk=mk: e.matmul(ST[:], lhsT=KT[:, g, kb * 128:(kb + 1) * 128], rhs=Q[:, 4 * g:4 * g + 4, :],
                                                                            start=True, stop=(mk is None)), [KTb, Qb], [STb])
                if mk is not None:
                    P.op("tensor", lambda e, ST=ST, mk=mk: e.matmul(ST[:], lhsT=idb[:], rhs=MK[:, mk, :], start=False, stop=True), [idbb, MKb], [STb])
                P.op("scalar", lambda e, ST=ST, PT=PT: e.activation(out=PT[:], in_=ST[:], func=AF.Exp, scale=SCALE), [STb], [PTb])
                pts.append((PT, PTb, kb))
            for h in range(4):
                head = 4 * g + h
                PO, POb = pso[c["o"] % npso]; OF, OFb = Ofs[c["o"] % 3]; DN, DNb = dn[c["o"] % 3]; c["o"] += 1
                for i, (PT, PTb, kb) in enumerate(pts):
                    P.op("tensor", lambda e, PO=PO, PT=PT, kb=kb, g=g, h=h, i=i, n=len(pts): e.matmul(PO[:, 0:129], lhsT=PT[:, h * 128:(h + 1) * 128], rhs=VA[:, kb, g, 0:129],
                                                                                              start=(i == 0), stop=(i == n - 1)), [PTb, VAb], [POb])
                P.op("scalar", lambda e, PO=PO, OF=OF: e.copy(out=OF[:, 0:129], in_=PO[:, 0:129]), [POb], [OFb])
                P.op("vector", lambda e, OF=OF, DN=DN, head=head: e.tensor_tensor(out=DN[:, 0:1], in0=OF[:, 128:129], in1=es[:, head:head + 1], op=ALU.add),
                     [OFb, esb], [DNb])
                P.op("vector", lambda e, DN=DN: e.reciprocal(out=DN[:, 1:2], in_=DN[:, 0:1]), [DNb], [DNb])
                P.op("vector", lambda e, OF=OF, DN=DN, A=A, head=head: e.tensor_scalar(out=A[:, head * 128:(head + 1) * 128], in0=OF[:, 0:128], scalar1=DN[:, 1:2],
                                                                                   scalar2=None, op0=ALU.mult), [OFb, DNb], [Ab])
        P.dma("gpsimd", attn[t * 128:(t + 1) * 128, :], A[:], [Ab], [attnb.part(t)], Ab)
        yield


POOLW = (2, 4, 8, 16)


def pool_phase(k, pTh="pTh", pTc="pTc", invx="invx", invc="invc", pw="pool_w", pscale="pool_scale", po="po"):
    nc, P = k.nc, k.P
    pTh, pThb = k.d(pTh); pTc, pTcb = k.d(pTc); invx, invxb = k.d(invx); invc, invcb = k.d(invc)
    pw, pwb = k.d(pw); pscale, pscaleb = k.d(pscale); po, pob = k.d(po)
    W32, W32b = k.sb("W32", [128, 4, 128], F32)
    P.dma("sync", W32[:], pw.rearrange("g c d -> c g d"), [pwb], [W32b], W32b)
    Wp, Wpb = k.sb("Wp", [128, 4, 128], BF16)
    P.op("vector", lambda e: e.tensor_copy(out=Wp[:], in_=W32[:]), [W32b], [Wpb])
    PS, PSb = k.sb("PS", [128, 512], F32)
    load_bc(P, PS[:], PSb, pscale, 0, pscaleb, n=512)
    Xp = [k.sb(f"Xp{i}", [128, 528], F32) for i in range(3)]
    Wa = [k.sb(f"Wa{i}", [128, 528], F32) for i in range(3)]
    Wb2 = [k.sb(f"Wb{i}", [128, 528], F32) for i in range(3)]
    IV = [k.sb(f"IV{i}", [128, 512], F32) for i in range(3)]
    Y = [k.sb(f"Y{i}", [128, 4, 512], BF16) for i in range(2)]
    OS = [k.sb(f"OS{i}", [128, 512], F32) for i in range(3)]
    OB = [k.sb(f"OB{i}", [128, 512], BF16) for i in range(3)]
    pp = [k.ps(f"pp{i}", [128, 512], F32) for i in range(2)]
    c = dict(x=0, o=0)
    chunks = [(pTh, pThb, invx, invxb, i * 512, 512, i * 512) for i in range(NTX // 4)] + [(pTc, pTcb, invc, invcb, 0, 256, NTX * 128)]
    for ci, (src, srcb, inv, invb, c0, n, orow) in enumerate(chunks):
        YT, YTb = Y[ci % 2]
        for g, w in enumerate(POOLW):
            h = w // 2
            X, Xb = Xp[c["x"] % 3]; A, Ab = Wa[c["x"] % 3]; B, Bb = Wb2[c["x"] % 3]; I, Ib = IV[c["x"] % 3]
            eng = "vector" if c["x"] % 2 == 0 else "gpsimd"
            c["x"] += 1
            P.dma("gpsimd", X[:, 0:n + 16], src[g * 128:(g + 1) * 128, c0:c0 + n + 16], [srcb], [Xb], Xb)
            P.dma("sync", I[:, 0:n], bass.AP(inv.tensor, inv.offset + g * inv.shape[1] + c0, [[0, 128], [1, n]]), [invb], [Ib], Ib)
            L = n + 16
            cur, curb = X, Xb
            step = 1
            bufs = [(A, Ab), (B, Bb)]
            bi = 0
            while step < w:
                L2 = L - step
                dst, dstb = bufs[bi]; bi ^= 1
                P.op(eng, lambda e, dst=dst, cur=cur, L2=L2, step=step: e.tensor_tensor(out=dst[:, 0:L2], in0=cur[:, 0:L2], in1=cur[:, step:step + L2], op=ALU.add),
                     [curb], [dstb])
                cur, curb = dst, dstb
                L = L2
                step *= 2
            dst, dstb = bufs[bi]
            P.op(eng, lambda e, dst=dst, cur=cur, I=I, h=h, n=n: e.tensor_tensor(out=dst[:, 0:n], in0=cur[:, 8 - h:8 - h + n], in1=I[:, 0:n], op=ALU.mult),
                 [curb, Ib], [dstb])
            P.op(eng, lambda e, dst=dst, X=X, YT=YT, g=g, n=n: e.tensor_tensor(out=YT[:, g, 0:n], in0=dst[:, 0:n], in1=X[:, 8:8 + n], op=ALU.subtract),
                 [dstb, Xb], [YTb.part(g)])
        for j in range(n // 128):
            PP, PPb = pp[c["o"] % 2]; O, Ob = OS[c["o"] % 3]; c["o"] += 1
            for g in range(4):
                P.op("tensor", lambda e, PP=PP, YT=YT, g=g, j=j: e.matmul(PP[:, g * 128:(g + 1) * 128], lhsT=YT[:, g, j * 128:(j + 1) * 128], rhs=Wp[:, g, :],
                                                                      start=True, stop=True), [YTb, Wpb], [PPb])
            P.op("scalar", lambda e, PP=PP, O=O: e.copy(out=O[:], in_=PP[:]), [PPb], [Ob])
            OBt, OBb = OB[(c["o"] - 1) % 3]
            P.op("vector", lambda e, O=O, OBt=OBt: e.tensor_tensor(out=OBt[:], in0=O[:], in1=PS[:], op=ALU.mult), [Ob, PSb], [OBb])
            r0 = orow + j * 128
            P.dma("gpsimd", po[r0:r0 + 128, :], OBt[:], [OBb], [pob.part((ci, j))], OBb)


def pb_drams(pr):
    pr.dram("qT", [128, 8, NTOK], BF16, "ExternalInput")
    pr.dram("kTh", [128, 2, NKB * 128], BF16, "ExternalInput")
    pr.dram("vh", [NKB * 128, 256], BF16, "ExternalInput")
    pr.dram("masks", [4, 128, 512], BF16, "ExternalInput")
    pr.dram("sink", [8], F32, "ExternalInput")
    pr.dram("identb", [128, 128], BF16, "ExternalInput")
    pr.dram("attn", [NTOK, 1024], BF16, "ExternalOutput")
    pr.dram("pTh", [512, 8 + NTX * 128 + 8], BF16, "ExternalInput")
    pr.dram("pTc", [512, 8 + NTC * 128 + 8], BF16, "ExternalInput")
    pr.dram("invx", [4, NTX * 128], F32, "ExternalInput")
    pr.dram("invc", [4, NTC * 128], F32, "ExternalInput")
    pr.dram("pool_w", [4, 128, 128], F32, "ExternalInput")
    pr.dram("pool_scale", [512], F32, "ExternalInput")
    pr.dram("po", [NTOK, 512], BF16, "ExternalOutput")


def build_pb():
    pr = Program()
    pb_drams(pr)
    pr.phase(attn_phase)
    pr.phase(pool_phase)
    return pr.nc


I32 = mybir.dt.int32
TWO_PI = float(2 * np.pi)


def col_ap(ap1d, off, n):
    return bass.AP(ap1d.tensor, ap1d.offset + off, [[1, n], [1, 1]])


def sin_layer(P, ps, psb, fs, fbs, vb, a, ab, tf, tfb, ti, tib, h, hb, N):
    P.op("scalar", lambda e: e.activation(out=a[:, :N], in_=ps[:, :N], func=AF.Identity, scale=fs, bias=fbs), [psb, vb], [ab])
    P.op("vector", lambda e: e.tensor_copy(out=ti[:, :N], in_=a[:, :N]), [ab], [tib])
    P.op("vector", lambda e: e.tensor_copy(out=tf[:, :N], in_=ti[:, :N]), [tib], [tfb])
    P.op("vector", lambda e: e.tensor_tensor(out=a[:, :N], in0=a[:, :N], in1=tf[:, :N], op=ALU.subtract), [ab, tfb], [ab])
    P.op("scalar", lambda e: e.activation(out=h[:, :N], in_=a[:, :N], func=AF.Sin, scale=TWO_PI), [ab], [hb])


def filter_phase(k, n, **kw):
    for _ in filter_gen(k, n, **kw):
        pass


def filter_gen(k, n, ft="ft", dec="dec", w1="hw1", b1="hb1", f1="hf1", w2="hw2", b2="hb2", f2="hf2", w3b="hw3b", w3f="hw3f",
               bias="hbias", Hd="Hd", hsum="hsum", tag="", nps=2):
    nc, P = k.nc, k.P
    ft, ftb = k.d(ft); dec, decb = k.d(dec); w1, w1b = k.d(w1); b1, b1b = k.d(b1); f1, f1b = k.d(f1); w2, w2b = k.d(w2)
    b2, b2b = k.d(b2); f2, f2b = k.d(f2); w3b, w3bb = k.d(w3b); w3f, w3fb = k.d(w3f); bias, biasb = k.d(bias); Hd, Hdb = k.d(Hd); hsum, hsumb = k.d(hsum)
    W1, W1b = k.sb("W1" + tag, [33, 64], F32); W2, W2b = k.sb("W2" + tag, [64, 64], F32); W3, W3b = k.sb("W3" + tag, [64, 2, 64], F32)
    P.dma("sync", W1[:], w1[:, :], [w1b], [W1b], W1b)
    P.dma("sync", W2[:], w2[:, :], [w2b], [W2b], W2b)
    P.dma("sync", W3[:, 0, :], w3b[:, :], [w3bb], [W3b.part(0)], W3b.part(0))
    P.dma("sync", W3[:, 1, :], w3f[:, :], [w3fb], [W3b.part(1)], W3b.part(1))
    V, Vb = k.sb("V" + tag, [64, 10], F32)
    for i, (a, ab) in enumerate(((f1, f1b), (b1, b1b), (f2, f2b), (b2, b2b), (bias, biasb))):
        P.dma("sync", V[:, i:i + 1], col_ap(a, 0, 64), [ab], [Vb.part(i)], Vb.part(i))
    P.op("vector", lambda e: e.tensor_tensor(out=V[:, 5:6], in0=V[:, 0:1], in1=V[:, 1:2], op=ALU.mult), [Vb], [Vb])
    P.op("vector", lambda e: e.tensor_tensor(out=V[:, 6:7], in0=V[:, 2:3], in1=V[:, 3:4], op=ALU.mult), [Vb], [Vb])
    P.op("vector", lambda e: e.tensor_scalar(out=V[:, 5:7], in0=V[:, 5:7], scalar1=1.0 / TWO_PI, scalar2=None, op0=ALU.mult), [Vb], [Vb])
    P.op("vector", lambda e: e.tensor_scalar(out=V[:, 7:8], in0=V[:, 0:1], scalar1=1.0 / TWO_PI, scalar2=None, op0=ALU.mult), [Vb], [Vb])
    P.op("vector", lambda e: e.tensor_scalar(out=V[:, 8:9], in0=V[:, 2:3], scalar1=1.0 / TWO_PI, scalar2=None, op0=ALU.mult), [Vb], [Vb])
    nch = (2 * n + 511) // 512
    SA, SAb = k.sb("SA" + tag, [64, nch + 4], F32)
    T0, T0b = k.sb("T0" + tag, [64, 4], F32)
    FT = [k.sb(f"FT{i}" + tag, [33, 512], F32) for i in range(2)]
    DC = [k.sb(f"DC{i}" + tag, [64, 512], F32) for i in range(2)]
    A = [k.sb(f"A{i}" + tag, [64, 512], F32) for i in range(2)]
    TF = [k.sb(f"TF{i}" + tag, [64, 512], F32) for i in range(2)]
    TI = [k.sb(f"TI{i}" + tag, [64, 512], I32) for i in range(2)]
    H1 = [k.sb(f"H1{i}" + tag, [64, 512], F32) for i in range(2)]
    H2 = [k.sb(f"H2{i}" + tag, [64, 512], F32) for i in range(2)]
    R = [k.sb(f"R{i}" + tag, [64, 512], F32) for i in range(2)]
    J = [k.sb(f"J{i}" + tag, [64, 512], F32) for i in range(2)]
    HB = [k.sb(f"HB{i}" + tag, [64, 512], BF16) for i in range(2)]
    p1 = [k.ps(f"p1{i}" + tag, [64, 512], F32) for i in range(nps)]
    p2 = [k.ps(f"p2{i}" + tag, [64, 512], F32) for i in range(nps)]
    p3 = [k.ps(f"p3{i}" + tag, [64, 512], F32) for i in range(nps)]
    for ch in range(nch):
        c0 = ch * 512
        N = min(512, 2 * n - c0)
        i = ch % 2
        ip = ch % nps
        F_, Fb_ = FT[i]; D_, Db_ = DC[i]
        P.dma("sync", F_[:, :N], ft[:, c0:c0 + N], [ftb], [Fb_], Fb_)
        P.dma("sync", D_[:, :N], dec[:, c0:c0 + N], [decb], [Db_], Db_)
        P.op("tensor", lambda e, ip=ip, N=N, F_=F_: e.matmul(p1[ip][0][:, :N], lhsT=W1[:], rhs=F_[:, :N], start=True, stop=True), [W1b, Fb_], [p1[ip][1]])
        sin_layer(P, p1[ip][0], p1[ip][1], V[:, 7:8], V[:, 5:6], Vb, A[i][0], A[i][1], TF[i][0], TF[i][1], TI[i][0], TI[i][1], H1[i][0], H1[i][1], N)
        P.op("tensor", lambda e, i=i, ip=ip, N=N: e.matmul(p2[ip][0][:, :N], lhsT=W2[:], rhs=H1[i][0][:, :N], start=True, stop=True), [W2b, H1[i][1]], [p2[ip][1]])
        sin_layer(P, p2[ip][0], p2[ip][1], V[:, 8:9], V[:, 6:7], Vb, A[i][0], A[i][1], TF[i][0], TF[i][1], TI[i][0], TI[i][1], H2[i][0], H2[i][1], N)
        nb = max(0, min(N, n - c0))
        if nb > 0:
            P.op("tensor", lambda e, i=i, ip=ip, nb=nb: e.matmul(p3[ip][0][:, 0:nb], lhsT=W3[:, 0, :], rhs=H2[i][0][:, 0:nb], start=True, stop=True), [W3b, H2[i][1]], [p3[ip][1]])
        if nb < N:
            P.op("tensor", lambda e, i=i, ip=ip, nb=nb, N=N: e.matmul(p3[ip][0][:, nb:N], lhsT=W3[:, 1, :], rhs=H2[i][0][:, nb:N], start=True, stop=True), [W3b, H2[i][1]], [p3[ip][1]])
        R_, Rb_ = R[i]
        P.op("scalar", lambda e, ip=ip, N=N, R_=R_: e.copy(out=R_[:, :N], in_=p3[ip][0][:, :N]), [p3[ip][1]], [Rb_])
        P.op("vector", lambda e, N=N, R_=R_, D_=D_: e.tensor_tensor(out=R_[:, :N], in0=R_[:, :N], in1=D_[:, :N], op=ALU.mult), [Rb_, Db_], [Rb_])
        P.op("scalar", lambda e, i=i, N=N, R_=R_, ch=ch: e.activation(out=J[i][0][:, :N], in_=R_[:, :N], func=AF.Abs, accum_out=SA[:, ch:ch + 1]), [Rb_], [J[i][1], SAb.part(ch)])
        P.op("vector", lambda e, i=i, N=N, R_=R_: e.tensor_copy(out=HB[i][0][:, :N], in_=R_[:, :N]), [Rb_], [HB[i][1]])
        if c0 == 0:
            P.op("vector", lambda e, R_=R_: e.tensor_copy(out=T0[:, 0:1], in_=R_[:, 0:1]), [Rb_], [T0b.part(0)])
        if c0 <= n < c0 + N:
            P.op("vector", lambda e, R_=R_, o=n - c0: e.tensor_copy(out=T0[:, 1:2], in_=R_[:, o:o + 1]), [Rb_], [T0b.part(1)])
        P.dma("gpsimd", Hd[:, c0:c0 + N], HB[i][0][:, :N], [HB[i][1]], [Hdb.part(ch)], HB[i][1])
        yield
    P.op("vector", lambda e: e.tensor_reduce(out=SA[:, nch:nch + 1], in_=SA[:, 0:nch], axis=AX.X, op=ALU.add), [SAb], [SAb])
    P.op("vector", lambda e: e.tensor_tensor(out=T0[:, 2:3], in0=T0[:, 0:1], in1=T0[:, 1:2], op=ALU.add), [T0b], [T0b])
    P.op("vector", lambda e: e.scalar_tensor_tensor(out=T0[:, 2:3], in0=SA[:, nch:nch + 1], scalar=V[:, 4:5], in1=T0[:, 2:3], op0=ALU.mult, op1=ALU.add),
         [SAb, Vb, T0b], [T0b])
    tb, tbb = k.sb("tb" + tag, [64, 2], BF16)
    P.op("vector", lambda e: e.tensor_copy(out=tb[:, 0:1], in_=T0[:, 2:3]), [T0b], [tbb])
    P.dma("gpsimd", bass.AP(Hd.tensor, Hd.offset + n, [[2 * n, 64], [1, 1]]), tb[:, 0:1], [tbb], [Hdb], tbb, allow_slow_non_contiguous=True)
    P.dma("gpsimd", col_ap(hsum, 0, 64), SA[:, nch:nch + 1], [SAb], [hsumb], SAb)


def hyconv_a_phase(k, n, zc="zc", cw="hcw", cb="hcb", hsum="hsum", identf="ident", identb="identb", Vt="Vt_s", X0t="X0t_s"):
    nc, P = k.nc, k.P
    zc, zcb = k.d(zc); cw, cwb = k.d(cw); cb, cbb = k.d(cb); hsum, hsumb = k.d(hsum); identb, identbb = k.d(identb); Vt, Vtb = k.d(Vt); X0t, X0tb = k.d(X0t)
    NB = n // 128
    CW, CWb = k.sb("CW", [128, 3, 4], F32)
    for b in range(2):
        for part in range(3):
            for tap in range(3):
                P.dma("sync", CW[b * 64:(b + 1) * 64, part, tap:tap + 1], col_ap(cw, (tap * 3 + part) * 64, 64), [cwb], [CWb], CWb)
            P.dma("sync", CW[b * 64:(b + 1) * 64, part, 3:4], col_ap(cb, part * 64, 64), [cbb], [CWb], CWb)
    rS, rSb = k.sb("rS", [128, 2], F32)
    for b in range(2):
        P.dma("sync", rS[b * 64:(b + 1) * 64, 0:1], col_ap(hsum, 0, 64), [hsumb], [rSb], rSb)
    P.op("vector", lambda e: e.reciprocal(out=rS[:, 1:2], in_=rS[:, 0:1]), [rSb], [rSb])
    P.op("vector", lambda e: e.tensor_scalar(out=CW[:, 0, :], in0=CW[:, 0, :], scalar1=rS[:, 1:2], scalar2=None, op0=ALU.mult), [CWb, rSb], [CWb])
    idb, idbb = k.sb("idb", [128, 128], BF16)
    P.dma("sync", idb[:], identb[:, :], [identbb], [idbb], idbb)
    TC = min(1024, n)
    Z = [[k.sb(f"Z{i}_{p}", [128, TC + 2], F32) for p in range(3)] for i in range(2)]
    C = [k.sb(f"C{p}", [128, TC], F32) for p in range(3)]
    VX, VXb = k.sb("VX", [128, TC], BF16)
    X0, X0b = k.sb("X0", [128, TC], BF16)
    VT, VTb = k.sb("VT", [128, 128, NB], BF16)
    XT, XTb = k.sb("XT", [128, 128, NB], BF16)
    pt = [k.ps(f"pt{i}", [128, 4, 128], BF16) for i in range(4)]
    cnt = 0
    for ci in range(n // TC):
        t0 = ci * TC
        for p in range(3):
            Zp, Zpb = Z[ci % 2][p]
            P.dma("gpsimd", Zp[:], zc[p, :, t0:t0 + TC + 2], [zcb], [Zpb], Zpb)
            Cp, Cpb = C[p]
            eng = "vector"
            P.op(eng, lambda e, Cp=Cp, Zp=Zp, p=p: e.tensor_scalar(out=Cp[:], in0=Zp[:, 1:TC + 1], scalar1=CW[:, p, 1:2], scalar2=CW[:, p, 3:4], op0=ALU.mult, op1=ALU.add),
                 [Zpb, CWb], [Cpb])
            P.op(eng, lambda e, Cp=Cp, Zp=Zp, p=p: e.scalar_tensor_tensor(out=Cp[:], in0=Zp[:, 0:TC], scalar=CW[:, p, 0:1], in1=Cp[:], op0=ALU.mult, op1=ALU.add),
                 [Zpb, CWb, Cpb], [Cpb])
            P.op(eng, lambda e, Cp=Cp, Zp=Zp, p=p: e.scalar_tensor_tensor(out=Cp[:], in0=Zp[:, 2:TC + 2], scalar=CW[:, p, 2:3], in1=Cp[:], op0=ALU.mult, op1=ALU.add),
                 [Zpb, CWb, Cpb], [Cpb])
        nbk = TC // 128
        P.op("vector", lambda e, nbk=nbk: e.tensor_tensor(out=VX[:].rearrange("p (k j) -> p k j", k=nbk)[:, :, ::-1], in0=C[2][0][:].rearrange("p (k j) -> p k j", k=nbk),
                                                        in1=C[1][0][:].rearrange("p (k j) -> p k j", k=nbk), op=ALU.mult), [C[2][1], C[1][1]], [VXb])
        P.op("gpsimd", lambda e: e.tensor_copy(out=X0[:], in_=C[0][0][:]), [C[0][1]], [X0b])
        for (src, srcb, dst, dstb) in ((VX, VXb, VT, VTb), (X0, X0b, XT, XTb)):
            for b4 in range(0, nbk, 4):
                nb4 = min(4, nbk - b4)
                PT, PTb = pt[cnt % 4]; cnt += 1
                for q in range(nb4):
                    P.op("tensor", lambda e, PT=PT, src=src, q=q, b4=b4: e.transpose(out=PT[:, q, :], in_=src[:, (b4 + q) * 128:(b4 + q + 1) * 128], identity=idb[:]),
                         [srcb, idbb], [PTb])
                blk0 = t0 // 128 + b4
                P.op("scalar", lambda e, PT=PT, dst=dst, blk0=blk0, nb4=nb4: e.copy(out=dst[:, :, blk0:blk0 + nb4], in_=PT[:, 0:nb4, :].rearrange("p k c -> p c k")),
                     [PTb], [dstb])
    P.dma("gpsimd", Vt[:, :, :], VT[:], [VTb], [Vtb], VTb)
    P.dma("gpsimd", X0t[:, :, :], XT[:], [XTb], [X0tb], XTb)


def hyconv_b_phase(k, n, Hd="Hd", Vt="Vt_s", X0t="X0t_s", hyo="hyo"):
    nc, P = k.nc, k.P
    Hd, Hdb = k.d(Hd); Vt, Vtb = k.d(Vt); X0t, X0tb = k.d(X0t); hyo, hyob = k.d(hyo)
    NB = n // 128
    VT, VTb = k.sb("VT", [128, 128, NB], BF16)
    XT, XTb = k.sb("XT", [128, 128, NB], BF16)
    P.dma("sync", VT[:], Vt[:, :, :], [Vtb], [VTb], VTb)
    P.dma("sync", XT[:], X0t[:, :, :], [X0tb], [XTb], XTb)
    OUT, OUTb = k.sb("OUT", [128, 2 * NB, 64], BF16)
    ds = list(range(-(NB - 1), NB))
    GD = 64
    groups = [ds[i:i + GD] for i in range(0, len(ds), GD)]
    gz = [g for g in groups if 0 in g][0]
    groups = [gz] + [g for g in groups if g is not gz]
    GW = GD * 128 + 128
    G = [k.sb(f"G{i}", [128, GW], BF16) for i in range(4)]
    YS = [k.sb(f"YS{i}", [128, 2, NB], F32) for i in range(2)]
    py = [k.ps(f"py{i}", [128, 2, NB], F32) for i in range(2)]
    gi = 0
    for c in range(64):
        PY, PYb = py[c % 2]
        nmm = len(ds)
        done = 0
        for grp in groups:
            Gt, Gtb = G[gi % 4]; gi += 1
            u0 = n + 128 * grp[0] - 127
            width = 128 * (len(grp) - 1) + 128
            src = bass.AP(Hd.tensor, Hd.offset + c * 2 * n + u0, [[1, 128], [1, width]])
            P.dma("sync", Gt[:, 0:width], src, [Hdb], [Gtb], Gtb)
            order = ([0] + [d for d in grp if d != 0]) if 0 in grp else grp
            for d in order:
                a_lo, a_hi = max(0, d), min(NB, NB + d)
                off = 128 * (d - grp[0])
                done += 1
                P.op("tensor", lambda e, PY=PY, Gt=Gt, off=off, a_lo=a_lo, a_hi=a_hi, d=d, c=c, first=(done == 1), last=(done == nmm):
                     e.matmul(PY[:, :, a_lo:a_hi], lhsT=Gt[:, off:off + 128], rhs=VT[:, c:c + 65:64, a_lo - d:a_hi - d], start=first, stop=last),
                     [Gtb, VTb], [PYb])
        Y, Yb = YS[c % 2]
        P.op("scalar", lambda e, Y=Y, PY=PY: e.copy(out=Y[:], in_=PY[:]), [PYb], [Yb])
        eng = "vector" if c % 2 == 0 else "gpsimd"
        P.op(eng, lambda e, Y=Y, c=c: e.tensor_tensor(out=OUT[:, :, c].rearrange("p (b a) -> p b a", b=2), in0=Y[:], in1=XT[:, c:c + 65:64, :], op=ALU.mult),
             [Yb, XTb], [OUTb])
    AB_ = max(1, min(4, NB))
    for b in range(2):
        for a0 in range(0, NB, AB_):
            dst = bass.AP(hyo.tensor, hyo.offset + b * n * 64 + a0 * 128 * 64, [[64, 128], [128 * 64, AB_], [1, 64]])
            P.dma("gpsimd", dst, OUT[:, b * NB + a0:b * NB + a0 + AB_, :], [OUTb], [hyob.part((b, a0))], OUTb)


N_SEQ = 16384
N_CTX = 256


def build_A():
    pr = Program()
    pa_drams(pr, 0)
    pr.phase(cast_phase, "w32", "w_bf", D, 3584)
    pr.phase(proj_phase)
    return pr.nc


def hy_drams(pr, n, sfx):
    pr.dram("ft" + sfx, [33, 2 * n], F32, "ExternalInput")
    pr.dram("dec" + sfx, [64, 2 * n], F32, "ExternalInput")
    pr.dram("zc" + sfx, [3, 128, n + 2], BF16, "ExternalInput")
    pr.dram("Hd" + sfx, [64, 2 * n], BF16)
    pr.dram("hsum" + sfx, [64], F32)
    pr.dram("Vt_s" + sfx, [128, 128, n // 128], BF16)
    pr.dram("X0t_s" + sfx, [128, 128, n // 128], BF16)
    pr.dram("hyo" + sfx, [2, n, 64], BF16, "ExternalOutput")


def attn_filter_phase(k):
    ga = attn_gen(k, npst=3, npso=2)
    gf = filter_gen(k, N_SEQ, tag="f", nps=1)
    alive = [True, True]
    while any(alive):
        for i, (gen, steps) in enumerate(((ga, 1), (gf, 2))):
            for _ in range(steps):
                if alive[i]:
                    try:
                        next(gen)
                    except StopIteration:
                        alive[i] = False


def build_B():
    pr = Program()
    pb_drams(pr)
    pr.dram("hw1", [33, 64], F32, "ExternalInput"); pr.dram("hw2", [64, 64], F32, "ExternalInput")
    pr.dram("hw3b", [64, 64], F32, "ExternalInput"); pr.dram("hw3f", [64, 64], F32, "ExternalInput")
    for nm in ("hb1", "hf1", "hb2", "hf2", "hbias"):
        pr.dram(nm, [64], F32, "ExternalInput")
    pr.dram("hcw", [3, 3, 64], F32, "ExternalInput"); pr.dram("hcb", [3, 64], F32, "ExternalInput")
    hy_drams(pr, N_SEQ, "")
    hy_drams(pr, N_CTX, "c")
    pr.phase(attn_filter_phase)
    pr.phase(pool_phase)
    for n, s in ((N_SEQ, ""), (N_CTX, "c")):
        if n == N_CTX:
            pr.phase(filter_phase, n, ft="ft" + s, dec="dec" + s, Hd="Hd" + s, hsum="hsum" + s)
        pr.phase(hyconv_a_phase, n, zc="zc" + s, hsum="hsum" + s, Vt="Vt_s" + s, X0t="X0t_s" + s)
        pr.phase(hyconv_b_phase, n, Hd="Hd" + s, Vt="Vt_s" + s, X0t="X0t_s" + s, hyo="hyo" + s)
    return pr.nc


def build_C():
    pr = Program()
    pc_drams(pr, x1_external=False)
    pr.phase(cast_phase, "wo32", "wo_bf", D, D)
    pr.phase(cast_phase, "wu32", "wu_bf", D, HID)
    pr.phase(cast_phase, "wd32", "wd_bf", HID, D)
    pr.phase(outproj_phase)
    pr.phase(mlp_phase)
    return pr.nc


def build_CA():
    pr = Program()
    pc_drams(pr, x1_external=False)
    pr.dram("w32", [D, 3584], F32, "ExternalInput")
    pr.dram("w_bf", [D, 3584], BF16)
    for nm in ("modx_n", "modc_n"):
        pr.dram(nm, [6 * D], F32, "ExternalInput")
    pr.dram("gpre", [D], F32, "ExternalInput")
    pr.dram("ropeC", [128, NTOK], F32, "ExternalInput")
    pr.dram("ropeS", [128, NTOK], F32, "ExternalInput")
    pr.dram("prot", [128, 128], BF16, "ExternalInput")
    pr.dram("qT", [128, 8, NTOK], BF16, "ExternalOutput")
    pr.dram("kT", [128, 2, NTOK], BF16, "ExternalOutput")
    pr.dram("v", [NTOK, 256], BF16, "ExternalOutput")
    pr.dram("zT", [1536, NTOK], BF16, "ExternalOutput")
    pr.dram("pT", [512, NTOK], BF16, "ExternalOutput")
    pr.phase(cast_phase, "wo32", "wo_bf", D, D)
    pr.phase(cast_phase, "wu32", "wu_bf", D, HID)
    pr.phase(cast_phase, "wd32", "wd_bf", HID, D)
    pr.phase(cast_phase, "w32", "w_bf", D, 3584)
    pr.phase(outproj_phase)
    pr.phase(mlp_phase)
    pr.phase(proj_phase, xin="x2", modx="modx_n", modc="modc_n")
    return pr.nc


def kernel(**inputs):
    f32 = lambda name: np.ascontiguousarray(np.asarray(inputs[name], np.float32))
    x = f32("x").copy()
    ctx = f32("ctx").copy()
    c3 = np.concatenate([f32("c"), f32("c_ctx")[None]], 0)
    w_mod, b_mod = f32("w_mod"), f32("b_mod")
    cores = list(range(8))
    res = run_bass_kernel_spmd(build_pm(), [dict(c3=c3, wm=np.ascontiguousarray(w_mod[:, :, r * NCOL:(r + 1) * NCOL]),
                                                 bm=np.ascontiguousarray(b_mod[:, r * NCOL:(r + 1) * NCOL])) for r in cores], core_ids=cores)
    mod = np.concatenate([res.results[r]["mod"] for r in cores], axis=2)
    del w_mod
    ncA, ncB, ncC, ncCA = build_A(), build_B(), build_C(), build_CA()
    cols = w_in_cols()
    ident = np.eye(128, dtype=np.float32)
    identb = np.eye(128).astype(BF)
    prot = rot_perm()
    ropes = [rope_tables(j * CHUNK) for j in range(4)]
    masks = [attn_masks(j) for j in range(4)]
    invx = [pool_inv_counts(j * CHUNK, CHUNK, N_SEQ) for j in range(4)]
    invc = pool_inv_counts(0, N_CTX, N_CTX)
    tabs = [hyena_tables(N_SEQ, r) for r in cores]
    tabc = [hyena_tables(N_CTX, r) for r in cores]
    A = None
    for l in range(4):
        g = lambda name, l=l: f32(name)[l]
        xins = [np.concatenate([x[r // 4, (r % 4) * CHUNK:(r % 4 + 1) * CHUNK], ctx[r // 4]], 0) for r in cores]
        if A is None:
            w32 = np.ascontiguousarray(g("w_in")[:, cols])
            ims = [dict(xin=xins[r], w32=w32, modx=mod[l, r // 4], modc=mod[l, 2], gpre=g("g_pre_mix"), ropeC=ropes[r % 4][0], ropeS=ropes[r % 4][1],
                        prot=prot, ident=ident) for r in cores]
            A = run_bass_kernel_spmd(ncA, ims, core_ids=cores).results
            del ims
        hw = g("hy_conv_w"); hb = g("hy_conv_b"); w3 = g("hy_w3"); hbias = g("hy_bias")
        ims = []
        for r in cores:
            b, j = r // 4, r % 4
            grp = [4 * b + i for i in range(4)]
            kTh = np.concatenate([halo_cat([A[q]["kT"][:, :, :CHUNK] for q in grp], j, 2, 128, None), A[r]["kT"][:, :, CHUNK:]], 2)
            vh = np.concatenate([halo_cat([A[q]["v"][:CHUNK] for q in grp], j, 0, 128, None), A[r]["v"][CHUNK:]], 0)
            pTh = halo_cat([A[q]["pT"][:, :CHUNK] for q in grp], j, 1, 8, None)
            pTc = np.concatenate([np.zeros((512, 8), BF), A[r]["pT"][:, CHUNK:], np.zeros((512, 8), BF)], 1)
            zc = np.zeros((3, 128, N_SEQ + 2), BF)
            zcc = np.zeros((3, 128, N_CTX + 2), BF)
            for part in range(3):
                rows = slice(192 * r + part * 64, 192 * r + part * 64 + 64)
                for bb in range(2):
                    for jj in range(4):
                        zc[part, bb * 64:(bb + 1) * 64, 1 + jj * CHUNK:1 + (jj + 1) * CHUNK] = A[4 * bb + jj]["zT"][rows, :CHUNK]
                    zcc[part, bb * 64:(bb + 1) * 64, 1:1 + N_CTX] = A[4 * bb]["zT"][rows, CHUNK:]
            cw = np.stack([hw[:, part * 512 + r * 64: part * 512 + (r + 1) * 64] for part in range(3)], 1)
            cb = np.stack([hb[part * 512 + r * 64: part * 512 + (r + 1) * 64] for part in range(3)], 0)
            ims.append(dict(qT=A[r]["qT"], kTh=np.ascontiguousarray(kTh), vh=np.ascontiguousarray(vh), masks=masks[j], sink=g("attn_sink"), identb=identb,
                            pTh=np.ascontiguousarray(pTh), pTc=pTc, invx=invx[j], invc=invc, pool_w=g("pool_w"), pool_scale=g("pool_scale"),
                            hw1=g("hy_w1"), hw2=g("hy_w2"), hw3f=np.ascontiguousarray(w3[:, r * 64:(r + 1) * 64]),
                            hw3b=np.ascontiguousarray(w3[:, 512 + r * 64:512 + (r + 1) * 64]), hb1=g("hy_b1"), hf1=g("hy_freq1"), hb2=g("hy_b2"), hf2=g("hy_freq2"),
                            hbias=np.ascontiguousarray(hbias[r * 64:(r + 1) * 64]), hcw=np.ascontiguousarray(cw), hcb=np.ascontiguousarray(cb),
                            ft=tabs[r][0], dec=tabs[r][1], zc=zc, ftc=tabc[r][0], decc=tabc[r][1], zcc=zcc))
        A = None
        Bo = run_bass_kernel_spmd(ncB, ims, core_ids=cores).results
        del ims
        ims = []
        for r in cores:
            b, j = r // 4, r % 4
            hy = np.concatenate([np.concatenate([Bo[q]["hyo"][b, j * CHUNK:(j + 1) * CHUNK] for q in cores], 1),
                                 np.concatenate([Bo[q]["hyoc"][b] for q in cores], 1)], 0)
            d = dict(attn=Bo[r]["attn"], hy=np.ascontiguousarray(hy), po=Bo[r]["po"], xin=xins[r], wo32=g("w_out"), wu32=g("w_up"), wd32=g("w_down"),
                     gbr=g("g_branch"), gpost=g("g_post_mix"), gpre2=g("g_pre_mlp"), gpost2=g("g_post_mlp"), modx=mod[l, b], modc=mod[l, 2], ident=ident)
            if l < 3:
                d.update(w32=np.ascontiguousarray(f32("w_in")[l + 1][:, cols]), modx_n=mod[l + 1, b], modc_n=mod[l + 1, 2], gpre=f32("g_pre_mix")[l + 1],
                         ropeC=ropes[j][0], ropeS=ropes[j][1], prot=prot)
            ims.append(d)
        del Bo
        Co = run_bass_kernel_spmd(ncCA if l < 3 else ncC, ims, core_ids=cores).results
        del ims
        for r in cores:
            b, j = r // 4, r % 4
            x[b, j * CHUNK:(j + 1) * CHUNK] = Co[r]["x2"][:CHUNK]
            if j == 0:
                ctx[b] = Co[r]["x2"][CHUNK:]
        A = Co if l < 3 else None
    return x
```

```python
import numpy as np
import ml_dtypes
import concourse.bass as bass
import concourse.mybir as mybir
from concourse.bass_utils import run_bass_kernel_spmd

F32 = mybir.dt.float32
BF16 = mybir.dt.bfloat16
ALU = mybir.AluOpType
AF = mybir.ActivationFunctionType
AX = mybir.AxisListType

EPOCH = 20000


class Buf:
    __slots__ = ("name", "w", "r", "parent", "children", "sem", "cnt")

    def __init__(self, name, parent=None):
        self.name = name
        self.w = None
        self.r = {}
        self.parent = parent
        self.children = {}
        self.sem = None
        self.cnt = 0

    def part(self, key):
        c = self.children.get(key)
        if c is None:
            c = Buf(f"{self.name}.{key}", parent=self)
            self.children[key] = c
        return c


class Op:
    __slots__ = ("eng", "fn", "deps", "is_dma", "sem", "val", "needs_inc", "idx", "owner")

    def __init__(self, eng, fn, is_dma, owner):
        self.eng = eng
        self.fn = fn
        self.deps = []
        self.is_dma = is_dma
        self.owner = owner
        self.sem = None
        self.val = 0
        self.needs_inc = False


class Prog:
    ENGS = ("sync", "scalar", "vector", "gpsimd", "tensor")
    UID = 0

    def __init__(self, nc):
        self.nc = nc
        self.ops = {e: [] for e in self.ENGS}
        self.nops = 0
        self.all_ops = []
        self.final = []
        Prog.UID += 1
        self.uid = Prog.UID

    def buf(self, name):
        return Buf(name)

    def _nodes(self, b):
        nodes = [b]
        if b.children:
            nodes += list(b.children.values())
        if b.parent is not None:
            nodes.append(b.parent)
        return nodes

    def op(self, eng, fn, reads=(), writes=(), dma=False, owner=None):
        o = Op(eng, fn, dma, owner)
        deps = {}

        def add(d):
            if d is None:
                return
            if d.eng == eng and eng == "tensor":
                return
            deps[id(d)] = d

        for b in reads:
            for n in self._nodes(b):
                add(n.w)
        for b in writes:
            for n in self._nodes(b):
                add(n.w)
                for rd in n.r.values():
                    for x in rd:
                        add(x)
        best = {}
        out = []
        for d in deps.values():
            if d.is_dma:
                out.append(d)
            else:
                cur = best.get(d.eng)
                if cur is None or d.idx > cur.idx:
                    best[d.eng] = d
        out += list(best.values())
        for d in out:
            d.needs_inc = True
        o.deps = out
        o.idx = len(self.ops[eng])
        self.ops[eng].append(o)
        self.all_ops.append(o)
        self.nops += 1
        for b in reads:
            lst = b.r.setdefault(eng, [])
            if dma:
                lst.append(o)
            else:
                lst[:] = [o]
        for b in writes:
            b.w = o
            b.r = {}
            for c in b.children.values():
                c.w = o
                c.r = {}
        if dma:
            assert owner is not None
            o.needs_inc = True
        return o

    def dma(self, eng, out, in_, reads, writes, owner, **kw):
        return self.op(eng, lambda e: e.dma_start(out=out, in_=in_, **kw), reads, writes, dma=True, owner=owner)

    def emit(self, final_wait_bufs=()):
        nc = self.nc
        import contextlib
        stack = contextlib.ExitStack()
        eng_sems = {}
        dsems = []
        for o in self.all_ops:
            if o.is_dma:
                b = o.owner
                if b.sem is None:
                    b.sem = nc.alloc_semaphore(name=f"d{self.uid}_{len(dsems)}")
                    dsems.append(b.sem)
                b.cnt += 16
                o.sem = b.sem
                o.val = b.cnt
        for e in self.ENGS:
            cnt = 0
            for o in self.ops[e]:
                if (not o.is_dma) and o.needs_inc:
                    ep = cnt // EPOCH
                    key = (e, ep)
                    if key not in eng_sems:
                        eng_sems[key] = nc.alloc_semaphore(name=f"e{self.uid}_{e}_{ep}")
                    o.sem = eng_sems[key]
                    o.val = cnt % EPOCH + 1
                    cnt += 1
        self.nsems = len(eng_sems)
        final = []
        for o in self.all_ops:
            if o.is_dma:
                final.append((o.sem, o.val))
        with stack, nc.Block() as block:
            def replay(ename, e, extra=()):
                waited = {}
                for o in self.ops[ename]:
                    for d in o.deps:
                        k = id(d.sem)
                        if waited.get(k, 0) >= d.val:
                            continue
                        waited[k] = d.val
                        e.wait_ge(d.sem, d.val)
                    ins = o.fn(e)
                    if o.needs_inc:
                        ins.then_inc(o.sem, 16 if o.is_dma else 1)
                fin = {}
                for (s, v) in extra:
                    if fin.get(id(s), (None, 0))[1] < v:
                        fin[id(s)] = (s, v)
                for (s, v) in fin.values():
                    e.wait_ge(s, v)

            @block.sync
            def _(e):
                replay("sync", e)

            @block.scalar
            def _(e):
                replay("scalar", e)

            @block.vector
            def _(e):
                replay("vector", e)

            @block.gpsimd
            def _(e):
                replay("gpsimd", e, extra=final)

            @block.tensor
            def _(e):
                replay("tensor", e)


def bf16_np(a):
    return np.asarray(a).astype(ml_dtypes.bfloat16)


D = 2048
SEQ = 16384
CTX = 256
GRID_W = 64
NCORE = 8
CHUNK = 4096
BF = ml_dtypes.bfloat16


def rope_tables(tok0, ntok_x=CHUNK, nctx=CTX):
    t = np.arange(tok0, tok0 + ntok_x)
    row = (t // GRID_W).astype(np.float32)
    col = (t % GRID_W).astype(np.float32)
    quarter = 32
    inv_freq = (10000.0 ** (-np.arange(quarter, dtype=np.float32) / quarter)).astype(np.float32)
    C = np.ones((128, ntok_x + nctx), np.float32)
    S = np.zeros((128, ntok_x + nctx), np.float32)
    for d in range(128):
        pos = row if d < 64 else col
        ang = (pos * inv_freq[d % 32]).astype(np.float32)
        C[d, :ntok_x] = np.cos(ang)
        s = np.sin(ang)
        S[d, :ntok_x] = -s if (d % 64) < 32 else s
    return C, S


def rot_perm():
    Pm = np.zeros((128, 128), np.float32)
    for m in range(128):
        partner = m + 32 if (m % 64) < 32 else m - 32
        Pm[partner, m] = 1.0
    return Pm.astype(BF)


def hy_col_perm():
    idx = np.zeros(1536, np.int64)
    for r in range(8):
        for part in range(3):
            for c in range(64):
                idx[r * 192 + part * 64 + c] = part * 512 + r * 64 + c
    return idx


def w_in_cols():
    hp = hy_col_perm()
    return np.concatenate([np.arange(0, 1536), 1536 + hp, np.arange(3072, 3584)])


def attn_masks(chunk_j, nchunks=4):
    s = np.arange(128)[:, None]
    q = np.arange(128)[None, :]
    NEG = -30000.0
    prev = np.where(s >= q, 0.0, NEG)
    nxt = np.where(s <= q, 0.0, NEG)
    allneg = np.full((128, 128), NEG)
    m = [prev, nxt, allneg if chunk_j == 0 else prev, allneg if chunk_j == nchunks - 1 else nxt]
    return np.stack([np.tile(a, (1, 4)) for a in m]).astype(BF)


def pool_inv_counts(tok0, ntok, n):
    t = np.arange(tok0, tok0 + ntok)
    out = np.zeros((4, ntok), np.float32)
    for g, w in enumerate((2, 4, 8, 16)):
        h = w // 2
        cnt = (np.minimum(t + h, n) - np.maximum(t - h, 0)).astype(np.float32)
        out[g] = 1.0 / cnt
    return out


def halo_cat(parts, j, axis, halo, zero_like):
    own = parts[j]
    def take(a, sl):
        idx = [slice(None)] * a.ndim
        idx[axis] = sl
        return a[tuple(idx)]
    zshape = list(own.shape); zshape[axis] = halo
    z = np.zeros(zshape, own.dtype)
    left = take(parts[j - 1], slice(parts[j - 1].shape[axis] - halo, None)) if j > 0 else z
    right = take(parts[j + 1], slice(0, halo)) if j < len(parts) - 1 else z
    return np.concatenate([left, own, right], axis=axis)


def hyena_tables(n, core, width=512):
    m = np.arange(2 * n)
    tau = np.where(m < n, np.where(m == 0, 0, n - m), m - n)
    t_all = np.linspace(0.0, 1.0, n, dtype=np.float32)
    bands = 16
    omega_all = (2.0 * np.pi * np.arange(n, dtype=np.float32) / n).astype(np.float32)
    f = np.linspace(1e-4, bands - 1, bands, dtype=np.float32)[None, :]
    fo = (f * omega_all[:, None]).astype(np.float32)
    feats_all = np.concatenate([t_all[:, None], np.cos(fo), -np.sin(fo)], axis=-1).astype(np.float32)
    ft = np.ascontiguousarray(feats_all[tau].T)
    max_decay = np.log(1e-2) / 0.3
    min_decay = np.log(1e-2) / 1.5
    deltas = np.abs(np.linspace(min_decay, max_decay, width, dtype=np.float32))[core * 64:(core + 1) * 64]
    dec = np.exp(-t_all[tau][None, :] * deltas[:, None]).astype(np.float32)
    return ft, dec


import contextlib, os

D = 2048
NTX = 32
NTC = 2
NT = NTX + NTC
NTOK = NT * 128
EPS = 1e-6


class K:
    def __init__(self, nc, drams, outs):
        self.nc = nc
        self.P = Prog(nc)
        self.stack = contextlib.ExitStack()
        self.drams = drams
        self.outnames = outs

    def d(self, name):
        ap = self.drams[name]
        return ap, self.P.buf(name)

    def sb(self, name, shape, dt):
        t = self.stack.enter_context(self.nc.sbuf_tensor(f"p{self.P.uid}_{name}", list(shape), dt))
        return t, self.P.buf(name)

    def ps(self, name, shape, dt):
        t = self.stack.enter_context(self.nc.psum_tensor(f"p{self.P.uid}_{name}", list(shape), dt))
        return t, self.P.buf(name)


class Program:
    def __init__(self):
        self.nc = bass.Bass("TRN2", target_bir_lowering=False)
        self.drams = {}
        self.outs = []
        self.nphase = 0

    def dram(self, name, shape, dt, kind=None):
        if kind:
            t = self.nc.dram_tensor(name, list(shape), dt, kind=kind)
        else:
            t = self.nc.dram_tensor(name, list(shape), dt)
        self.drams[name] = t.ap()
        if kind == "ExternalOutput":
            self.outs.append(name)
        return self.drams[name]

    def phase(self, fn, *args, **kw):
        nc = self.nc
        self.nphase += 1
        with nc.cleanup_on_exit():
            k = K(nc, self.drams, self.outs)
            outbufs = fn(k, *args, **kw) or []
            with k.stack:
                k.P.emit(final_wait_bufs=outbufs)
            nc.all_engine_barrier()


def load_fm(P, dst, ap1d, off, srcb, dstb, n=D, eng="sync"):
    for kk in range(n // 128):
        src = bass.AP(ap1d.tensor, ap1d.offset + off + kk * 128, [[1, 128], [1, 1]])
        P.dma(eng, dst[:, kk:kk + 1], src, [srcb], [dstb], dstb)


def vec_bc(ap1d, off, n=D):
    return bass.AP(ap1d.tensor, ap1d.offset + off, [[0, 128], [1, n]])


def rstd_from_ss(P, ss, ssb, rstd, rstdb, width, n=1):
    P.op("vector", lambda e: e.tensor_scalar(out=rstd, in0=ss, scalar1=1.0 / width, scalar2=EPS, op0=ALU.mult, op1=ALU.add),
         [ssb], [rstdb])
    P.op("scalar", lambda e: e.activation(out=rstd, in_=rstd, func=AF.Ln), [rstdb], [rstdb])
    P.op("scalar", lambda e: e.activation(out=rstd, in_=rstd, func=AF.Exp, scale=-0.5), [rstdb], [rstdb])


def pa_drams(pr, l):
    pr.dram("xin", [NTOK, D], F32, "ExternalInput")
    pr.dram("w32", [D, 3584], F32, "ExternalInput")
    pr.dram("w_bf", [D, 3584], BF16)
    pr.dram("modx", [6 * D], F32, "ExternalInput")
    pr.dram("modc", [6 * D], F32, "ExternalInput")
    pr.dram("gpre", [D], F32, "ExternalInput")
    pr.dram("ropeC", [128, NTOK], F32, "ExternalInput")
    pr.dram("ropeS", [128, NTOK], F32, "ExternalInput")
    pr.dram("prot", [128, 128], BF16, "ExternalInput")
    pr.dram("ident", [128, 128], F32, "ExternalInput")
    pr.dram("qT", [128, 8, NTOK], BF16, "ExternalOutput")
    pr.dram("kT", [128, 2, NTOK], BF16, "ExternalOutput")
    pr.dram("v", [NTOK, 256], BF16, "ExternalOutput")
    pr.dram("zT", [1536, NTOK], BF16, "ExternalOutput")
    pr.dram("pT", [512, NTOK], BF16, "ExternalOutput")


def build_pa():
    pr = Program()
    pa_drams(pr, 0)
    pr.phase(cast_phase, "w32", "w_bf", D, 3584)
    pr.phase(proj_phase)
    return pr.nc


def cast_phase(k, src, dst, rows, cols):
    cast_ops(k, src, dst, rows, cols)


def cast_ops(k, src, dst, rows, cols):
    P = k.P
    s, sb_ = k.d(src)
    d, db_ = k.d(dst)
    n = 8
    step = rows // n
    for c in range(n):
        P.dma("gpsimd", d[c * step:(c + 1) * step, :], s[c * step:(c + 1) * step, :], [sb_], [db_.part(c)], db_.part(c))


def proj_phase(k, xin="xin", w="w_bf", modx="modx", modc="modc", gpre="gpre", ropeC="ropeC", ropeS="ropeS",
               prot="prot", ident="ident", qT="qT", kT="kT", vo="v", zT="zT", pT="pT"):
    xin, xinb = k.d(xin); w, wb_ = k.d(w); modx, modxb = k.d(modx); modc, modcb = k.d(modc); gpre, gpreb = k.d(gpre)
    ropeC, ropeCb = k.d(ropeC); ropeS, ropeSb = k.d(ropeS); prot, protb = k.d(prot); ident, identb = k.d(ident)
    qT, qTb = k.d(qT); kT, kTb = k.d(kT); vo, vob = k.d(vo); zT, zTb = k.d(zT); pT, pTb = k.d(pT)
    nc, P = k.nc, k.P
    W, Wb = k.sb("W", [128, 16, 3584], BF16)
    wv = w.rearrange("(k p) n -> p k n", p=128)
    for c in range(4):
        P.dma("sync", W[:, 4 * c:4 * c + 4, :], wv[:, 4 * c:4 * c + 4, :], [wb_], [Wb.part(c)], Wb.part(c))
    idt, idtb = k.sb("idt", [128, 128], F32)
    P.dma("sync", idt[:], ident[:, :], [identb], [idtb], idtb)
    prt, prtb = k.sb("prt", [128, 128], BF16)
    P.dma("sync", prt[:], prot[:, :], [protb], [prtb], prtb)
    AB, ABb = k.sb("AB", [128, 2, 2, 16], F32)
    tmpv, tmpvb = k.sb("tmpv", [128, 3, 16], F32)
    load_fm(P, tmpv[:, 0, :], gpre, 0, gpreb, tmpvb.part(0))
    for s, (m, mb) in enumerate(((modx, modxb), (modc, modcb))):
        load_fm(P, tmpv[:, 1 + s, :], m, D, mb, tmpvb.part(1 + s))
        load_fm(P, AB[:, s, 1, :], m, 0, mb, ABb.part(s))
    for s in range(2):
        P.op("vector", lambda e, s=s: e.scalar_tensor_tensor(out=AB[:, s, 0, :], in0=tmpv[:, 1 + s, :], scalar=1.0, in1=tmpv[:, 0, :],
                                                              op0=ALU.add, op1=ALU.mult),
             [tmpvb], [ABb.part(("A", s))])

    NXB = 2
    xt = [k.sb(f"xt{i}", [128, D], F32) for i in range(NXB)]
    ss = [k.sb(f"ss{i}", [128, 2], F32) for i in range(NXB)]
    junk, junkb = k.sb("junk", [128, D], BF16)
    hxT = [k.sb(f"hxT{i}", [128, 16, 512], BF16) for i in range(1)]
    ptr = [k.ps(f"ptr{i}", [128, 4, 128], F32) for i in range(2)]
    pacc = [k.ps(f"pacc{i}", [128, 512], F32) for i in range(3)]
    prot_ps = [k.ps(f"prot{i}", [128, 512], F32) for i in range(2)]
    pv = [k.ps(f"pv{i}", [128, 256], F32) for i in range(1)]
    qb = [k.sb(f"qb{i}", [128, 512], BF16) for i in range(2)]
    rc = [k.sb(f"rc{i}", [128, 512], F32) for i in range(2)]
    rs = [k.sb(f"rs{i}", [128, 512], F32) for i in range(2)]
    t1 = [k.sb(f"t1{i}", [128, 512], F32) for i in range(2)]
    t2 = [k.sb(f"t2{i}", [128, 512], F32) for i in range(2)]
    pf = [k.sb(f"pf{i}", [128, 512], F32) for i in range(2)]
    prf = [k.sb(f"prf{i}", [128, 512], F32) for i in range(2)]
    qo = [k.sb(f"qo{i}", [128, 512], BF16) for i in range(3)]
    zo = [k.sb(f"zo{i}", [128, 512], BF16) for i in range(3)]
    vs = [k.sb(f"vs{i}", [128, 256], BF16) for i in range(2)]

    macro = [(4 * i, 4, 0) for i in range(NTX // 4)] + [(NTX, NTC, 1)]
    import os
    if os.environ.get('NMACRO'): macro = macro[:int(os.environ['NMACRO'])]
    cnt = dict(x=0, tr=0, acc=0, rot=0, q=0, z=0, v=0)
    for mi, (t0, nt, mset) in enumerate(macro):
        ntok = nt * 128
        tok0 = t0 * 128
        H, Hb = hxT[0]
        RC, RCb = rc[mi % 2]
        RS, RSb = rs[mi % 2]
        P.dma("sync", RC[:, :ntok], ropeC[:, tok0:tok0 + ntok], [ropeCb], [RCb], RCb)
        P.dma("sync", RS[:, :ntok], ropeS[:, tok0:tok0 + ntok], [ropeSb], [RSb], RSb)
        for j in range(nt):
            X, Xb = xt[cnt["x"] % NXB]
            S, Sb = ss[cnt["x"] % NXB]
            cnt["x"] += 1
            r0 = (t0 + j) * 128
            P.dma("sync", X[:], xin[r0:r0 + 128, :], [xinb], [Xb], Xb)
            P.op("scalar", lambda e, X=X, S=S: e.activation(out=junk[:], in_=X[:], func=AF.Square, accum_out=S[:, 0:1]),
                 [Xb], [junkb, Sb])
            rstd_from_ss(P, S[:, 0:1], Sb, S[:, 1:2], Sb, D)
            P.op("scalar", lambda e, X=X, S=S: e.activation(out=X[:], in_=X[:], func=AF.Copy, scale=S[:, 1:2]),
                 [Xb, Sb], [Xb])
            for kg in range(4):
                PT, PTb = ptr[cnt["tr"] % 2]
                cnt["tr"] += 1
                for k4 in range(4):
                    kk = kg * 4 + k4
                    P.op("tensor", lambda e, PT=PT, X=X, k4=k4, kk=kk: e.transpose(out=PT[:, k4, :], in_=X[:, kk * 128:(kk + 1) * 128], identity=idt[:]),
                         [Xb, idtb], [PTb])
                for k4 in range(4):
                    kk = kg * 4 + k4
                    P.op("scalar", lambda e, ntok=ntok, PT=PT, H=H, kk=kk, k4=k4, j=j, mset=mset: e.activation(
                        out=H[:, kk, j * 128:(j + 1) * 128], in_=PT[:, k4, :], func=AF.Identity,
                        scale=AB[:, mset, 0, kk:kk + 1], bias=AB[:, mset, 1, kk:kk + 1]),
                         [PTb, ABb], [Hb])
        STAGE = int(os.environ.get('PA_STAGE', '9'))
        chunks = [("q", i, i * 128) for i in range(8)] + [("k", i, 1024 + i * 128) for i in range(2)] + \
                 [("z", i, 1536 + i * 128) for i in range(12)] + [("p", i, 3072 + i * 128) for i in range(4)]
        if STAGE == 1:
            chunks = []
            for kk in range(12):
                P.dma('gpsimd', zT[kk * 128:(kk + 1) * 128, tok0:tok0 + ntok].bitcast(BF16)[:, :ntok], H[:, kk, :ntok], [Hb], [zTb.part(kk)], Hb)
        if STAGE == 2:
            chunks = [c for c in chunks if c[0] in ('z', 'p')]
        for (kind, ci, col) in chunks:
            PA_, PAb = pacc[cnt["acc"] % 3]
            cnt["acc"] += 1
            for kk in range(16):
                P.op("tensor", lambda e, ntok=ntok, PA_=PA_, kk=kk, col=col, H=H: e.matmul(PA_[:, :ntok], lhsT=W[:, kk, col:col + 128], rhs=H[:, kk, :ntok],
                                                                            start=(kk == 0), stop=(kk == 15)),
                     [Wb, Hb], [PAb])
            if kind in ("q", "k"):
                QB, QBb = qb[cnt["rot"] % 2]
                PR, PRb = prot_ps[cnt["rot"] % 2]
                T1, T1b = t1[cnt["rot"] % 2]
                cnt["rot"] += 1
                QO, QOb = qo[cnt["q"] % 3]
                cnt["q"] += 1
                PF, PFb = pf[(cnt["rot"] - 1) % 2]
                PRF, PRFb = prf[(cnt["rot"] - 1) % 2]
                T2, T2b = t2[(cnt["rot"] - 1) % 2]
                P.op("scalar", lambda e, ntok=ntok, PF=PF, PA_=PA_: e.copy(out=PF[:, :ntok], in_=PA_[:, :ntok]), [PAb], [PFb])
                P.op("gpsimd", lambda e, ntok=ntok, QB=QB, PF=PF: e.tensor_copy(out=QB[:, :ntok], in_=PF[:, :ntok]), [PFb], [QBb])
                P.op("tensor", lambda e, ntok=ntok, PR=PR, QB=QB: e.matmul(PR[:, :ntok], lhsT=prt[:], rhs=QB[:, :ntok], start=True, stop=True),
                     [prtb, QBb], [PRb])
                P.op("scalar", lambda e, ntok=ntok, PRF=PRF, PR=PR: e.copy(out=PRF[:, :ntok], in_=PR[:, :ntok]), [PRb], [PRFb])
                P.op("vector", lambda e, ntok=ntok, T1=T1, PF=PF, RC=RC: e.tensor_tensor(out=T1[:, :ntok], in0=PF[:, :ntok], in1=RC[:, :ntok], op=ALU.mult),
                     [PFb, RCb], [T1b])
                P.op("vector", lambda e, ntok=ntok, T2=T2, PRF=PRF, RS=RS: e.tensor_tensor(out=T2[:, :ntok], in0=PRF[:, :ntok], in1=RS[:, :ntok], op=ALU.mult),
                     [PRFb, RSb], [T2b])
                P.op("vector", lambda e, ntok=ntok, QO=QO, T1=T1, T2=T2: e.tensor_tensor(out=QO[:, :ntok], in0=T1[:, :ntok], in1=T2[:, :ntok], op=ALU.add),
                     [T1b, T2b], [QOb])
                dst = (qT if kind == "q" else kT)
                dstb = (qTb if kind == "q" else kTb)
                P.dma("gpsimd", dst[:, ci, tok0:tok0 + ntok], QO[:, :ntok], [QOb], [dstb.part((mi, ci))], QOb)
            else:
                ZO, ZOb = zo[cnt["z"] % 3]
                cnt["z"] += 1
                P.op("scalar", lambda e, ntok=ntok, ZO=ZO, PA_=PA_: e.copy(out=ZO[:, :ntok], in_=PA_[:, :ntok]), [PAb], [ZOb])
                dst, dstb = (zT, zTb) if kind == "z" else (pT, pTb)
                P.dma("gpsimd", dst[ci * 128:(ci + 1) * 128, tok0:tok0 + ntok], ZO[:, :ntok], [ZOb], [dstb.part((mi, ci))], ZOb)
        for j in range(nt if STAGE >= 4 else 0):
            PV, PVb = pv[0]
            for kk in range(16):
                P.op("tensor", lambda e, PV=PV, kk=kk, j=j, H=H: e.matmul(PV[:], lhsT=H[:, kk, j * 128:(j + 1) * 128], rhs=W[:, kk, 1280:1536],
                                                                      start=(kk == 0), stop=(kk == 15)),
                     [Wb, Hb], [PVb])
            VS, VSb = vs[cnt["v"] % 2]
            cnt["v"] += 1
            P.op("scalar", lambda e, VS=VS, PV=PV: e.copy(out=VS[:], in_=PV[:]), [PVb], [VSb])
            r0 = (t0 + j) * 128
            P.dma("gpsimd", vo[r0:r0 + 128, :], VS[:], [VSb], [vob.part((mi, j))], VSb)


NCOL = 1536


def build_pm():
    pr = Program()
    pr.dram("c3", [3, D], F32, "ExternalInput")
    pr.dram("wm", [4, D, NCOL], F32, "ExternalInput")
    pr.dram("bm", [4, NCOL], F32, "ExternalInput")
    pr.dram("mod", [4, 3, NCOL], F32, "ExternalOutput")
    pr.phase(mod_phase)
    return pr.nc


def mod_phase(k, c3="c3", wm="wm", bm="bm", mo="mod"):
    c3, c3b = k.d(c3); wm, wmb = k.d(wm); bm, bmb = k.d(bm); mo, mob = k.d(mo)
    nc, P = k.nc, k.P
    cT, cTb = k.sb("cT", [128, 16, 3], F32)
    sT, sTb = k.sb("sT", [128, 16, 3], F32)
    for row in range(3):
        for kk in range(16):
            src = bass.AP(c3.tensor, c3.offset + row * D + kk * 128, [[1, 128], [1, 1]])
            P.dma("sync", cT[:, kk, row:row + 1], src, [c3b], [cTb], cTb)
    P.op("scalar", lambda e: e.activation(out=sT[:], in_=cT[:], func=AF.Silu), [cTb], [sTb])
    wt = [k.sb(f"wt{i}", [128, 16, 512], F32) for i in range(2)]
    pm = [k.ps(f"pm{i}", [3, 512], F32) for i in range(2)]
    ms = [k.sb(f"ms{i}", [3, 512], F32) for i in range(2)]
    bs = [k.sb(f"bs{i}", [3, 512], F32) for i in range(2)]
    i = 0
    for l in range(4):
        wv = wm[l].rearrange("(k p) n -> p k n", p=128)
        for n in range(NCOL // 512):
            WT, WTb = wt[i % 2]
            PM, PMb = pm[i % 2]
            MS, MSb = ms[i % 2]
            BS, BSb = bs[i % 2]
            i += 1
            for h in range(2):
                P.dma("sync", WT[:, 8 * h:8 * h + 8, :], wv[:, 8 * h:8 * h + 8, n * 512:(n + 1) * 512], [wmb], [WTb.part(h)], WTb.part(h))
            bsrc = bass.AP(bm.tensor, bm.offset + l * NCOL + n * 512, [[0, 3], [1, 512]])
            P.dma("sync", BS[:], bsrc, [bmb], [BSb], BSb)
            for kk in range(16):
                P.op("tensor", lambda e, PM=PM, WT=WT, kk=kk: e.matmul(PM[:], lhsT=sT[:, kk, :], rhs=WT[:, kk, :], start=(kk == 0), stop=(kk == 15)),
                     [sTb, WTb], [PMb])
            P.op("scalar", lambda e, MS=MS, PM=PM: e.copy(out=MS[:], in_=PM[:]), [PMb], [MSb])
            P.op("vector", lambda e, MS=MS, BS=BS: e.tensor_tensor(out=MS[:], in0=MS[:], in1=BS[:], op=ALU.add), [MSb, BSb], [MSb])
            P.dma("gpsimd", mo[l, :, n * 512:(n + 1) * 512], MS[:], [MSb], [mob.part((l, n))], MSb)


HID = 4 * D


def rstd_multi(P, S, Sb, widths):
    for g, wd in enumerate(widths):
        P.op("vector", lambda e, g=g, wd=wd: e.tensor_scalar(out=S[:, 4 + g:5 + g], in0=S[:, g:g + 1], scalar1=1.0 / wd, scalar2=EPS,
                                                              op0=ALU.mult, op1=ALU.add), [Sb], [Sb])
    n = len(widths)
    P.op("scalar", lambda e: e.activation(out=S[:, 4:4 + n], in_=S[:, 4:4 + n], func=AF.Ln), [Sb], [Sb])
    P.op("scalar", lambda e: e.activation(out=S[:, 4:4 + n], in_=S[:, 4:4 + n], func=AF.Exp, scale=-0.5), [Sb], [Sb])


def norm_T(P, X, Xb, S, Sb, groups, junk, junkb, ptr, cnt, idt, idtb, scale_fn, bias_fn, vecb, H, Hb, hcol0):
    for g, (c0, c1) in enumerate(groups):
        P.op("scalar", lambda e, g=g, c0=c0, c1=c1: e.activation(out=junk[:, c0:c1], in_=X[:, c0:c1], func=AF.Square, accum_out=S[:, g:g + 1]),
             [Xb], [junkb, Sb])
    rstd_multi(P, S, Sb, [c1 - c0 for (c0, c1) in groups])
    for g, (c0, c1) in enumerate(groups):
        P.op("scalar", lambda e, g=g, c0=c0, c1=c1: e.activation(out=X[:, c0:c1], in_=X[:, c0:c1], func=AF.Copy, scale=S[:, 4 + g:5 + g]),
             [Xb, Sb], [Xb])
    for kg in range(4):
        PT, PTb = ptr[cnt["tr"] % len(ptr)]
        cnt["tr"] += 1
        for k4 in range(4):
            kk = kg * 4 + k4
            P.op("tensor", lambda e, PT=PT, k4=k4, kk=kk: e.transpose(out=PT[:, k4, :], in_=X[:, kk * 128:(kk + 1) * 128], identity=idt[:]),
                 [Xb, idtb], [PTb])
        for k4 in range(4):
            kk = kg * 4 + k4
            if bias_fn is None:
                P.op("scalar", lambda e, PT=PT, kk=kk, k4=k4: e.activation(out=H[:, kk, hcol0:hcol0 + 128], in_=PT[:, k4, :], func=AF.Copy,
                                                                        scale=scale_fn(kk)), [PTb, vecb], [Hb])
            else:
                P.op("scalar", lambda e, PT=PT, kk=kk, k4=k4: e.activation(out=H[:, kk, hcol0:hcol0 + 128], in_=PT[:, k4, :], func=AF.Identity,
                                                                        scale=scale_fn(kk), bias=bias_fn(kk)), [PTb, vecb], [Hb])


def load_bc(P, dst, dstb, ap1d, off, srcb, n=D, eng="sync"):
    P.dma(eng, dst, vec_bc(ap1d, off, n), [srcb], [dstb], dstb)


def outproj_phase(k, attn="attn", hy="hy", po="po", xin="xin", wo="wo_bf", gbr="gbr", gpost="gpost", modx="modx", modc="modc",
                  ident="ident", x1="x1", extra_casts=()):
    nc, P = k.nc, k.P
    for (csrc, cdst, crows, ccols) in extra_casts:
        cast_ops(k, csrc, cdst, crows, ccols)
    attn, attnb = k.d(attn); hy, hyb = k.d(hy); po, pob = k.d(po); xin, xinb = k.d(xin); wo, wob = k.d(wo)
    gbr, gbrb = k.d(gbr); gpost, gpostb = k.d(gpost); modx, modxb = k.d(modx); modc, modcb = k.d(modc)
    ident, identb = k.d(ident); x1, x1b = k.d(x1)
    W, Wb = k.sb("Wo", [128, 16, D], BF16)
    wv = wo.rearrange("(k p) n -> p k n", p=128)
    for c in range(4):
        P.dma("sync", W[:, 4 * c:4 * c + 4, :], wv[:, 4 * c:4 * c + 4, :], [wob], [Wb.part(c)], Wb.part(c))
    idt, idtb = k.sb("idt", [128, 128], F32)
    P.dma("sync", idt[:], ident[:, :], [identb], [idtb], idtb)
    gb, gbb = k.sb("gb", [128, 16], F32)
    load_fm(P, gb[:, :], gbr, 0, gbrb, gbb)
    G1 = [k.sb(f"G1_{s}", [128, D], F32) for s in range(2)]
    gtmp, gtmpb = k.sb("gtmp", [128, D], F32)
    for s, (m, mb) in enumerate(((modx, modxb), (modc, modcb))):
        load_bc(P, G1[s][0][:], G1[s][1], gpost, 0, gpostb)
        load_bc(P, gtmp[:], gtmpb, m, 2 * D, mb)
        P.op("vector", lambda e, s=s: e.tensor_tensor(out=G1[s][0][:], in0=G1[s][0][:], in1=gtmp[:], op=ALU.mult), [G1[s][1], gtmpb], [G1[s][1]])
    Ms = [k.sb(f"M{i}", [128, D], F32) for i in range(2)]
    Xs = [k.sb(f"X{i}", [128, D], F32) for i in range(2)]
    Os = [k.sb(f"O{i}", [128, D], F32) for i in range(2)]
    Ss = [k.sb(f"S{i}", [128, 8], F32) for i in range(2)]
    S2s = [k.sb(f"S2{i}", [128, 8], F32) for i in range(2)]
    mT = [k.sb(f"mT{i}", [128, 16, 128], BF16) for i in range(2)]
    junk, junkb = k.sb("junk", [128, D], BF16)
    ptr = [k.ps(f"ptr{i}", [128, 4, 128], F32) for i in range(2)]
    pout = [k.ps(f"pout{i}", [128, D], F32) for i in range(1)]
    cnt = dict(tr=0)
    groups = [(0, 1024), (1024, 1536), (1536, 2048)]
    for t in range(NT):
        mset = 0 if t < NTX else 1
        r0 = t * 128
        M, Mb = Ms[t % 2]; X, Xb = Xs[t % 2]; O, Ob = Os[t % 2]; S, Sb = Ss[t % 2]; S2, S2b = S2s[t % 2]
        H, Hb = mT[t % 2]
        P.dma("gpsimd", M[:, 0:1024], attn[r0:r0 + 128, :], [attnb], [Mb.part(0)], Mb.part(0))
        P.dma("gpsimd", M[:, 1024:1536], hy[r0:r0 + 128, :], [hyb], [Mb.part(1)], Mb.part(1))
        P.dma("gpsimd", M[:, 1536:2048], po[r0:r0 + 128, :], [pob], [Mb.part(2)], Mb.part(2))
        P.dma("sync", X[:], xin[r0:r0 + 128, :], [xinb], [Xb], Xb)
        norm_T(P, M, Mb, S, Sb, groups, junk, junkb, ptr, cnt, idt, idtb, lambda kk: gb[:, kk:kk + 1], None, gbb, H, Hb, 0)
        PO, POb = pout[0]
        for cg in range(4):
            for kk in range(16):
                P.op("tensor", lambda e, PO=PO, H=H, kk=kk, cg=cg: e.matmul(PO[:, cg * 512:(cg + 1) * 512], lhsT=H[:, kk, :], rhs=W[:, kk, cg * 512:(cg + 1) * 512],
                                                                        start=(kk == 0), stop=(kk == 15)), [Hb, Wb], [POb])
        P.op("scalar", lambda e, PO=PO, S2=S2: e.activation(out=junk[:], in_=PO[:], func=AF.Square, accum_out=S2[:, 0:1]), [POb], [junkb, S2b])
        rstd_multi(P, S2, S2b, [D])
        P.op("scalar", lambda e, PO=PO, O=O, S2=S2: e.activation(out=O[:], in_=PO[:], func=AF.Copy, scale=S2[:, 4:5]), [POb, S2b], [Ob])
        P.op("vector", lambda e, O=O, mset=mset: e.tensor_tensor(out=O[:], in0=O[:], in1=G1[mset][0][:], op=ALU.mult), [Ob, G1[mset][1]], [Ob])
        P.op("gpsimd", lambda e, O=O, X=X: e.tensor_tensor(out=O[:], in0=O[:], in1=X[:], op=ALU.add), [Ob, Xb], [Ob])
        P.dma("gpsimd", x1[r0:r0 + 128, :], O[:], [Ob], [x1b.part(t)], Ob)


def mlp_phase(k, x1="x1", wu="wu_bf", wd="wd_bf", gpre="gpre2", gpost="gpost2", modx="modx", modc="modc", ident="ident", x2="x2"):
    nc, P = k.nc, k.P
    x1, x1b = k.d(x1); wu, wub = k.d(wu); wd, wdb = k.d(wd); gpre, gpreb = k.d(gpre); gpost, gpostb = k.d(gpost)
    modx, modxb = k.d(modx); modc, modcb = k.d(modc); ident, identb = k.d(ident); x2, x2b = k.d(x2)
    idt, idtb = k.sb("idt", [128, 128], F32)
    P.dma("sync", idt[:], ident[:, :], [identb], [idtb], idtb)
    AB, ABb = k.sb("AB", [128, 2, 2, 16], F32)
    tmpv, tmpvb = k.sb("tmpv", [128, 3, 16], F32)
    load_fm(P, tmpv[:, 0, :], gpre, 0, gpreb, tmpvb.part(0))
    for s, (m, mb) in enumerate(((modx, modxb), (modc, modcb))):
        load_fm(P, tmpv[:, 1 + s, :], m, 4 * D, mb, tmpvb.part(1 + s))
        load_fm(P, AB[:, s, 1, :], m, 3 * D, mb, ABb.part(s))
    for s in range(2):
        P.op("vector", lambda e, s=s: e.scalar_tensor_tensor(out=AB[:, s, 0, :], in0=tmpv[:, 1 + s, :], scalar=1.0, in1=tmpv[:, 0, :],
                                                              op0=ALU.add, op1=ALU.mult), [tmpvb], [ABb.part(("A", s))])
    G2, G2b = k.sb("G2", [128, D], F32)
    Xs = [k.sb(f"X{i}", [128, D], F32) for i in range(2)]
    gtmp, gtmpb = Xs[1]
    Ss = [k.sb(f"S{i}", [128, 8], F32) for i in range(2)]
    S2, S2b = k.sb("S2", [128, 4, 8], F32)
    O2, O2b = k.sb("O2", [128, 4, D], F32)
    h2T, h2Tb = k.sb("h2T", [128, 16, 512], BF16)
    hidT, hidTb = k.sb("hidT", [128, 64, 512], BF16)
    ws = [k.sb(f"ws{i}", [128, 16, 512], BF16) for i in range(2)]
    rl = [k.sb(f"rl{i}", [128, 512], F32) for i in range(2)]
    junk2, junk2b = k.sb("junk2", [128, 512], BF16)
    ptr = [k.ps(f"ptr{i}", [128, 4, 128], F32) for i in range(2)]
    pup = [k.ps(f"pup{i}", [128, 512], F32) for i in range(2)]
    pdn = [k.ps(f"pdn{i}", [128, 512], F32) for i in range(4)]
    wuv = wu.rearrange("(k p) n -> p k n", p=128)
    wdv = wd.rearrange("(k p) n -> p k n", p=128)
    macro = [(4 * i, 4, 0) for i in range(NTX // 4)] + [(NTX, NTC, 1)]
    cnt = dict(tr=0, x=0, w=0, up=0)
    if os.environ.get('MLP_NMACRO'): macro = macro[:int(os.environ['MLP_NMACRO'])]
    h2Ts = [(h2T, h2Tb), k.sb("h2Tb", [128, 16, 512], BF16)]
    junkx, junkxb = k.sb("junkx", [128, D], BF16)
    cur_set = [None]

    def do_norm(mi):
        t0, nt, mset = macro[mi]
        H2, H2b = h2Ts[mi % 2]
        for j in range(nt):
            X, Xb = Xs[cnt["x"] % 2]; S, Sb = Ss[cnt["x"] % 2]
            cnt["x"] += 1
            r0 = (t0 + j) * 128
            P.dma("sync", X[:], x1[r0:r0 + 128, :], [x1b], [Xb], Xb)
            norm_T(P, X, Xb, S, Sb, [(0, D)], junkx, junkxb, ptr, cnt, idt, idtb,
                   lambda kk, mset=mset: AB[:, mset, 0, kk:kk + 1], lambda kk, mset=mset: AB[:, mset, 1, kk:kk + 1], ABb, H2, H2b, j * 128)

    do_norm(0)
    for mi, (t0, nt, mset) in enumerate(macro):
        ntok = nt * 128
        h2T, h2Tb = h2Ts[mi % 2]
        for sl in range(16):
            WS, WSb = ws[cnt["w"] % 2]
            cnt["w"] += 1
            for h in range(2):
                P.dma("sync", WS[:, 8 * h:8 * h + 8, :], wuv[:, 8 * h:8 * h + 8, sl * 512:(sl + 1) * 512], [wub], [WSb.part(h)], WSb.part(h))
            for c4 in range(4):
                hc = sl * 4 + c4
                PU, PUb = pup[cnt["up"] % 2]; RL, RLb = rl[cnt["up"] % 2]
                cnt["up"] += 1
                for kk in range(16):
                    P.op("tensor", lambda e, ntok=ntok, PU=PU, WS=WS, kk=kk, c4=c4, h2T=h2T: e.matmul(PU[:, :ntok], lhsT=WS[:, kk, c4 * 128:(c4 + 1) * 128], rhs=h2T[:, kk, :ntok],
                                                                              start=(kk == 0), stop=(kk == 15)), [WSb, h2Tb], [PUb])
                P.op("scalar", lambda e, ntok=ntok, PU=PU, RL=RL: e.activation(out=RL[:, :ntok], in_=PU[:, :ntok], func=AF.Relu), [PUb], [RLb])
                eng = "vector" if hc % 2 == 0 else "gpsimd"
                P.op(eng, lambda e, ntok=ntok, RL=RL, hc=hc: e.tensor_tensor(out=hidT[:, hc, :ntok], in0=RL[:, :ntok], in1=RL[:, :ntok], op=ALU.mult),
                     [RLb], [hidTb.part(hc)])
        if mi + 1 < len(macro):
            do_norm(mi + 1)
        for cg in range(4):
            for q in range(4):
                WS, WSb = ws[cnt["w"] % 2]
                cnt["w"] += 1
                for h in range(2):
                        P.dma("sync", WS[:, 8 * h:8 * h + 8, :], wdv[:, q * 16 + 8 * h:q * 16 + 8 * h + 8, cg * 512:(cg + 1) * 512], [wdb], [WSb.part(h)], WSb.part(h))
                for j in range(nt):
                    PD, PDb = pdn[j]
                    for c in range(16):
                        hc = q * 16 + c
                        P.op("tensor", lambda e, PD=PD, WS=WS, c=c, hc=hc, j=j: e.matmul(PD[:], lhsT=hidT[:, hc, j * 128:(j + 1) * 128], rhs=WS[:, c, :],
                                                                                   start=(hc == 0), stop=(hc == 63)), [hidTb, WSb], [PDb])
            for j in range(nt):
                PD, PDb = pdn[j]
                P.op("scalar", lambda e, PD=PD, j=j, cg=cg: e.activation(out=junk2[:], in_=PD[:], func=AF.Square, accum_out=S2[:, j, cg:cg + 1]),
                     [PDb], [junk2b, S2b.part(j)])
                P.op("scalar", lambda e, PD=PD, j=j, cg=cg: e.copy(out=O2[:, j, cg * 512:(cg + 1) * 512], in_=PD[:]), [PDb], [O2b.part(j)])
        if mset != cur_set[0]:
            cur_set[0] = mset
            m, mb = (modx, modxb) if mset == 0 else (modc, modcb)
            load_bc(P, G2[:], G2b, gpost, 0, gpostb)
            load_bc(P, gtmp[:], gtmpb, m, 5 * D, mb)
            P.op("vector", lambda e: e.tensor_tensor(out=G2[:], in0=G2[:], in1=gtmp[:], op=ALU.mult), [G2b, gtmpb], [G2b])
        for j in range(nt):
            X, Xb = Xs[cnt["x"] % 2]
            cnt["x"] += 1
            r0 = (t0 + j) * 128
            P.dma("sync", X[:], x1[r0:r0 + 128, :], [x1b], [Xb], Xb)
            S2j = S2[:, j, :]
            P.op("vector", lambda e, S2j=S2j: e.tensor_tensor(out=S2j[:, 0:2], in0=S2j[:, 0:2], in1=S2j[:, 2:4], op=ALU.add), [S2b.part(j)], [S2b.part(j)])
            P.op("vector", lambda e, S2j=S2j: e.tensor_tensor(out=S2j[:, 0:1], in0=S2j[:, 0:1], in1=S2j[:, 1:2], op=ALU.add), [S2b.part(j)], [S2b.part(j)])
            rstd_multi(P, S2j, S2b.part(j), [D])
            P.op("vector", lambda e, j=j, S2j=S2j: e.scalar_tensor_tensor(out=O2[:, j, :], in0=O2[:, j, :], scalar=S2j[:, 4:5], in1=G2[:],
                                                                     op0=ALU.mult, op1=ALU.mult), [O2b.part(j), S2b.part(j), G2b], [O2b.part(j)])
            P.op("gpsimd", lambda e, j=j, X=X: e.tensor_tensor(out=O2[:, j, :], in0=O2[:, j, :], in1=X[:], op=ALU.add), [O2b.part(j), Xb], [O2b.part(j)])
            P.dma("gpsimd", x2[r0:r0 + 128, :], O2[:, j, :], [O2b.part(j)], [x2b.part((mi, j))], O2b.part(j))


def pc_drams(pr, x1_external=True):
    pr.dram("attn", [NTOK, 1024], BF16, "ExternalInput")
    pr.dram("hy", [NTOK, 512], BF16, "ExternalInput")
    pr.dram("po", [NTOK, 512], BF16, "ExternalInput")
    pr.dram("xin", [NTOK, D], F32, "ExternalInput")
    pr.dram("wo32", [D, D], F32, "ExternalInput")
    pr.dram("wu32", [D, HID], F32, "ExternalInput")
    pr.dram("wd32", [HID, D], F32, "ExternalInput")
    pr.dram("wo_bf", [D, D], BF16)
    pr.dram("wu_bf", [D, HID], BF16)
    pr.dram("wd_bf", [HID, D], BF16)
    for n in ("gbr", "gpost", "gpre2", "gpost2"):
        pr.dram(n, [D], F32, "ExternalInput")
    pr.dram("modx", [6 * D], F32, "ExternalInput")
    pr.dram("modc", [6 * D], F32, "ExternalInput")
    pr.dram("ident", [128, 128], F32, "ExternalInput")
    pr.dram("x1", [NTOK, D], F32, "ExternalOutput") if x1_external else pr.dram("x1", [NTOK, D], F32)
    pr.dram("x2", [NTOK, D], F32, "ExternalOutput")


def build_pc():
    pr = Program()
    pc_drams(pr)
    pr.phase(cast_phase, "wo32", "wo_bf", D, D)
    pr.phase(outproj_phase, extra_casts=[("wu32", "wu_bf", D, HID), ("wd32", "wd_bf", HID, D)])
    pr.phase(mlp_phase)
    return pr.nc


NKX = NTX + 2
NKB = NKX + NTC
SCALE = 128 ** -0.5


def attn_phase(k, **kw):
    for _ in attn_gen(k, **kw):
        pass


def attn_gen(k, qT="qT", kTh="kTh", vh="vh", masks="masks", sink="sink", identb="identb", attn="attn", npst=4, npso=3):
    nc, P = k.nc, k.P
    qT, qTb = k.d(qT); kTh, kThb = k.d(kTh); vh, vhb = k.d(vh); masks, masksb = k.d(masks); sink, sinkb = k.d(sink)
    identd, identdb = k.d(identb); attn, attnb = k.d(attn)
    KT, KTb = k.sb("KT", [128, 2, NKB * 128], BF16)
    for g in range(2):
        P.dma("sync", KT[:, g, :], kTh[:, g, :], [kThb], [KTb.part(g)], KTb.part(g))
    VA, VAb = k.sb("VA", [128, NKB, 2, 130], BF16)
    P.op("vector", lambda e: e.memset(VA[:], 1.0), [], [VAb])
    vhv = vh.rearrange("(b p) (g d) -> p b g d", p=128, g=2)
    for g in range(2):
        for h in range(4):
            b0, b1 = h * 9, min(NKB, (h + 1) * 9)
            P.dma("sync", VA[:, b0:b1, g, 0:128], vhv[:, b0:b1, g, :], [vhb], [VAb], VAb)
    MK, MKb = k.sb("MK", [128, 4, 512], BF16)
    P.dma("sync", MK[:], masks.rearrange("m p n -> p m n"), [masksb], [MKb], MKb)
    idb, idbb = k.sb("idb", [128, 128], BF16)
    P.dma("sync", idb[:], identd[:, :], [identdb], [idbb], idbb)
    es, esb = k.sb("es", [128, 8], F32)
    P.dma("sync", es[:], bass.AP(sink.tensor, sink.offset, [[0, 128], [1, 8]]), [sinkb], [esb], esb)
    P.op("scalar", lambda e: e.activation(out=es[:], in_=es[:], func=AF.Exp), [esb], [esb])
    Qs = [k.sb(f"Q{i}", [128, 8, 128], BF16) for i in range(2)]
    PTs = [k.sb(f"PT{i}", [128, 512], BF16) for i in range(10)]
    Ofs = [k.sb(f"Of{i}", [128, 132], F32) for i in range(3)]
    dn = [k.sb(f"dn{i}", [128, 2], F32) for i in range(3)]
    AT = [k.sb(f"AT{i}", [128, 1024], BF16) for i in range(2)]
    pst = [k.ps(f"pst{i}", [128, 512], F32) for i in range(npst)]
    pso = [k.ps(f"pso{i}", [128, 512], F32) for i in range(npso)]
    c = dict(st=0, pt=0, o=0)
    for t in range(NT):
        Q, Qb = Qs[t % 2]
        A, Ab = AT[t % 2]
        P.dma("sync", Q[:], qT[:, :, t * 128:(t + 1) * 128], [qTb], [Qb], Qb)
        if t < NTX:
            kbs = [(t, 0 if t > 0 else 2), (t + 1, None), (t + 2, 1 if t < NTX - 1 else 3), (NKX, None), (NKX + 1, None)]
        else:
            kbs = [(NKX, None), (NKX + 1, None)]
        for g in range(2):
            pts = []
            for (kb, mk) in kbs:
                ST, STb = pst[c["st"] % npst]; c["st"] += 1
                PT, PTb = PTs[c["pt"] % 10]; c["pt"] += 1
                P.op("tensor", lambda e, ST=ST, kb=kb, g=g, Q=Q, mk=mk: e.matmul(ST[:], lhsT=KT[:, g, kb * 128:(kb + 1) * 128], rhs=Q[:, 4 * g:4 * g + 4, :],
                                                                            start=True, stop=(mk is None)), [KTb, Qb], [STb])
                if mk is not None:
                    P.op("tensor", lambda e, ST=ST, mk=mk: e.matmul(ST[:], lhsT=idb[:], rhs=MK[:, mk, :], start=False, stop=True), [idbb, MKb], [STb])
                P.op("scalar", lambda e, ST=ST, PT=PT: e.activation(out=PT[:], in_=ST[:], func=AF.Exp, scale=SCALE), [STb], [PTb])
                pts.append((PT, PTb, kb))
            for h in range(4):
                head = 4 * g + h
                PO, POb = pso[c["o"] % npso]; OF, OFb = Ofs[c["o"] % 3]; DN, DNb = dn[c["o"] % 3]; c["o"] += 1
                for i, (PT, PTb, kb) in enumerate(pts):
                    P.op("tensor", lambda e, PO=PO, PT=PT, kb=kb, g=g, h=h, i=i, n=len(pts): e.matmul(PO[:, 0:129], lhsT=PT[:, h * 128:(h + 1) * 128], rhs=VA[:, kb, g, 0:129],
                                                                                              start=(i == 0), stop=(i == n - 1)), [PTb, VAb], [POb])
                P.op("scalar", lambda e, PO=PO, OF=OF: e.copy(out=OF[:, 0:129], in_=PO[:, 0:129]), [POb], [OFb])
                P.op("vector", lambda e, OF=OF, DN=DN, head=head: e.tensor_tensor(out=DN[:, 0:1], in0=OF[:, 128:129], in1=es[:, head:head + 1], op=ALU.add),
                     [OFb, esb], [DNb])
                P.op("vector", lambda e, DN=DN: e.reciprocal(out=DN[:, 1:2], in_=DN[:, 0:1]), [DNb], [DNb])
                P.op("vector", lambda e, OF=OF, DN=DN, A=A, head=head: e.tensor_scalar(out=A[:, head * 128:(head + 1) * 128], in0=OF[:, 0:128], scalar1=DN[:, 1:2],
                                                                                   scalar2=None, op0=ALU.mult), [OFb, DNb], [Ab])
        P.dma("gpsimd", attn[t * 128:(t + 1) * 128, :], A[:], [Ab], [attnb.part(t)], Ab)
        yield


POOLW = (2, 4, 8, 16)


def pool_phase(k, pTh="pTh", pTc="pTc", invx="invx", invc="invc", pw="pool_w", pscale="pool_scale", po="po"):
    nc, P = k.nc, k.P
    pTh, pThb = k.d(pTh); pTc, pTcb = k.d(pTc); invx, invxb = k.d(invx); invc, invcb = k.d(invc)
    pw, pwb = k.d(pw); pscale, pscaleb = k.d(pscale); po, pob = k.d(po)
    W32, W32b = k.sb("W32", [128, 4, 128], F32)
    P.dma("sync", W32[:], pw.rearrange("g c d -> c g d"), [pwb], [W32b], W32b)
    Wp, Wpb = k.sb("Wp", [128, 4, 128], BF16)
    P.op("vector", lambda e: e.tensor_copy(out=Wp[:], in_=W32[:]), [W32b], [Wpb])
    PS, PSb = k.sb("PS", [128, 512], F32)
    load_bc(P, PS[:], PSb, pscale, 0, pscaleb, n=512)
    Xp = [k.sb(f"Xp{i}", [128, 528], F32) for i in range(3)]
    Wa = [k.sb(f"Wa{i}", [128, 528], F32) for i in range(3)]
    Wb2 = [k.sb(f"Wb{i}", [128, 528], F32) for i in range(3)]
    IV = [k.sb(f"IV{i}", [128, 512], F32) for i in range(3)]
    Y = [k.sb(f"Y{i}", [128, 4, 512], BF16) for i in range(2)]
    OS = [k.sb(f"OS{i}", [128, 512], F32) for i in range(3)]
    OB = [k.sb(f"OB{i}", [128, 512], BF16) for i in range(3)]
    pp = [k.ps(f"pp{i}", [128, 512], F32) for i in range(2)]
    c = dict(x=0, o=0)
    chunks = [(pTh, pThb, invx, invxb, i * 512, 512, i * 512) for i in range(NTX // 4)] + [(pTc, pTcb, invc, invcb, 0, 256, NTX * 128)]
    for ci, (src, srcb, inv, invb, c0, n, orow) in enumerate(chunks):
        YT, YTb = Y[ci % 2]
        for g, w in enumerate(POOLW):
            h = w // 2
            X, Xb = Xp[c["x"] % 3]; A, Ab = Wa[c["x"] % 3]; B, Bb = Wb2[c["x"] % 3]; I, Ib = IV[c["x"] % 3]
            eng = "vector" if c["x"] % 2 == 0 else "gpsimd"
            c["x"] += 1
            P.dma("gpsimd", X[:, 0:n + 16], src[g * 128:(g + 1) * 128, c0:c0 + n + 16], [srcb], [Xb], Xb)
            P.dma("sync", I[:, 0:n], bass.AP(inv.tensor, inv.offset + g * inv.shape[1] + c0, [[0, 128], [1, n]]), [invb], [Ib], Ib)
            L = n + 16
            cur, curb = X, Xb
            step = 1
            bufs = [(A, Ab), (B, Bb)]
            bi = 0
            while step < w:
                L2 = L - step
                dst, dstb = bufs[bi]; bi ^= 1
                P.op(eng, lambda e, dst=dst, cur=cur, L2=L2, step=step: e.tensor_tensor(out=dst[:, 0:L2], in0=cur[:, 0:L2], in1=cur[:, step:step + L2], op=ALU.add),
                     [curb], [dstb])
                cur, curb = dst, dstb
                L = L2
                step *= 2
            dst, dstb = bufs[bi]
            P.op(eng, lambda e, dst=dst, cur=cur, I=I, h=h, n=n: e.tensor_tensor(out=dst[:, 0:n], in0=cur[:, 8 - h:8 - h + n], in1=I[:, 0:n], op=ALU.mult),
                 [curb, Ib], [dstb])
            P.op(eng, lambda e, dst=dst, X=X, YT=YT, g=g, n=n: e.tensor_tensor(out=YT[:, g, 0:n], in0=dst[:, 0:n], in1=X[:, 8:8 + n], op=ALU.subtract),
                 [dstb, Xb], [YTb.part(g)])
        for j in range(n // 128):
            PP, PPb = pp[c["o"] % 2]; O, Ob = OS[c["o"] % 3]; c["o"] += 1
            for g in range(4):
                P.op("tensor", lambda e, PP=PP, YT=YT, g=g, j=j: e.matmul(PP[:, g * 128:(g + 1) * 128], lhsT=YT[:, g, j * 128:(j + 1) * 128], rhs=Wp[:, g, :],
                                                                      start=True, stop=True), [YTb, Wpb], [PPb])
            P.op("scalar", lambda e, PP=PP, O=O: e.copy(out=O[:], in_=PP[:]), [PPb], [Ob])
            OBt, OBb = OB[(c["o"] - 1) % 3]
            P.op("vector", lambda e, O=O, OBt=OBt: e.tensor_tensor(out=OBt[:], in0=O[:], in1=PS[:], op=ALU.mult), [Ob, PSb], [OBb])
            r0 = orow + j * 128
            P.dma("gpsimd", po[r0:r0 + 128, :], OBt[:], [OBb], [pob.part((ci, j))], OBb)


def pb_drams(pr):
    pr.dram("qT", [128, 8, NTOK], BF16, "ExternalInput")
    pr.dram("kTh", [128, 2, NKB * 128], BF16, "ExternalInput")
    pr.dram("vh", [NKB * 128, 256], BF16, "ExternalInput")
    pr.dram("masks", [4, 128, 512], BF16, "ExternalInput")
    pr.dram("sink", [8], F32, "ExternalInput")
    pr.dram("identb", [128, 128], BF16, "ExternalInput")
    pr.dram("attn", [NTOK, 1024], BF16, "ExternalOutput")
    pr.dram("pTh", [512, 8 + NTX * 128 + 8], BF16, "ExternalInput")
    pr.dram("pTc", [512, 8 + NTC * 128 + 8], BF16, "ExternalInput")
    pr.dram("invx", [4, NTX * 128], F32, "ExternalInput")
    pr.dram("invc", [4, NTC * 128], F32, "ExternalInput")
    pr.dram("pool_w", [4, 128, 128], F32, "ExternalInput")
    pr.dram("pool_scale", [512], F32, "ExternalInput")
    pr.dram("po", [NTOK, 512], BF16, "ExternalOutput")


def build_pb():
    pr = Program()
    pb_drams(pr)
    PH = os.environ.get('PBPH', 'ap')
    if 'a' in PH: pr.phase(attn_phase)
    if 'p' in PH: pr.phase(pool_phase)
    return pr.nc


I32 = mybir.dt.int32
TWO_PI = float(2 * np.pi)


def col_ap(ap1d, off, n):
    return bass.AP(ap1d.tensor, ap1d.offset + off, [[1, n], [1, 1]])


def sin_layer(P, ps, psb, fs, fbs, vb, a, ab, tf, tfb, ti, tib, h, hb, N):
    P.op("scalar", lambda e: e.activation(out=a[:, :N], in_=ps[:, :N], func=AF.Identity, scale=fs, bias=fbs), [psb, vb], [ab])
    P.op("vector", lambda e: e.tensor_copy(out=ti[:, :N], in_=a[:, :N]), [ab], [tib])
    P.op("vector", lambda e: e.tensor_copy(out=tf[:, :N], in_=ti[:, :N]), [tib], [tfb])
    P.op("vector", lambda e: e.tensor_tensor(out=a[:, :N], in0=a[:, :N], in1=tf[:, :N], op=ALU.subtract), [ab, tfb], [ab])
    P.op("scalar", lambda e: e.activation(out=h[:, :N], in_=a[:, :N], func=AF.Sin, scale=TWO_PI), [ab], [hb])


def filter_phase(k, n, **kw):
    for _ in filter_gen(k, n, **kw):
        pass


def filter_gen(k, n, ft="ft", dec="dec", w1="hw1", b1="hb1", f1="hf1", w2="hw2", b2="hb2", f2="hf2", w3b="hw3b", w3f="hw3f",
               bias="hbias", Hd="Hd", hsum="hsum", tag="", nps=2):
    nc, P = k.nc, k.P
    ft, ftb = k.d(ft); dec, decb = k.d(dec); w1, w1b = k.d(w1); b1, b1b = k.d(b1); f1, f1b = k.d(f1); w2, w2b = k.d(w2)
    b2, b2b = k.d(b2); f2, f2b = k.d(f2); w3b, w3bb = k.d(w3b); w3f, w3fb = k.d(w3f); bias, biasb = k.d(bias); Hd, Hdb = k.d(Hd); hsum, hsumb = k.d(hsum)
    W1, W1b = k.sb("W1" + tag, [33, 64], F32); W2, W2b = k.sb("W2" + tag, [64, 64], F32); W3, W3b = k.sb("W3" + tag, [64, 2, 64], F32)
    P.dma("sync", W1[:], w1[:, :], [w1b], [W1b], W1b)
    P.dma("sync", W2[:], w2[:, :], [w2b], [W2b], W2b)
    P.dma("sync", W3[:, 0, :], w3b[:, :], [w3bb], [W3b.part(0)], W3b.part(0))
    P.dma("sync", W3[:, 1, :], w3f[:, :], [w3fb], [W3b.part(1)], W3b.part(1))
    V, Vb = k.sb("V" + tag, [64, 10], F32)
    for i, (a, ab) in enumerate(((f1, f1b), (b1, b1b), (f2, f2b), (b2, b2b), (bias, biasb))):
        P.dma("sync", V[:, i:i + 1], col_ap(a, 0, 64), [ab], [Vb.part(i)], Vb.part(i))
    P.op("vector", lambda e: e.tensor_tensor(out=V[:, 5:6], in0=V[:, 0:1], in1=V[:, 1:2], op=ALU.mult), [Vb], [Vb])
    P.op("vector", lambda e: e.tensor_tensor(out=V[:, 6:7], in0=V[:, 2:3], in1=V[:, 3:4], op=ALU.mult), [Vb], [Vb])
    P.op("vector", lambda e: e.tensor_scalar(out=V[:, 5:7], in0=V[:, 5:7], scalar1=1.0 / TWO_PI, scalar2=None, op0=ALU.mult), [Vb], [Vb])
    P.op("vector", lambda e: e.tensor_scalar(out=V[:, 7:8], in0=V[:, 0:1], scalar1=1.0 / TWO_PI, scalar2=None, op0=ALU.mult), [Vb], [Vb])
    P.op("vector", lambda e: e.tensor_scalar(out=V[:, 8:9], in0=V[:, 2:3], scalar1=1.0 / TWO_PI, scalar2=None, op0=ALU.mult), [Vb], [Vb])
    nch = (2 * n + 511) // 512
    SA, SAb = k.sb("SA" + tag, [64, nch + 4], F32)
    T0, T0b = k.sb("T0" + tag, [64, 4], F32)
    FT = [k.sb(f"FT{i}" + tag, [33, 512], F32) for i in range(2)]
    DC = [k.sb(f"DC{i}" + tag, [64, 512], F32) for i in range(2)]
    A = [k.sb(f"A{i}" + tag, [64, 512], F32) for i in range(2)]
    TF = [k.sb(f"TF{i}" + tag, [64, 512], F32) for i in range(2)]
    TI = [k.sb(f"TI{i}" + tag, [64, 512], I32) for i in range(2)]
    A2 = [k.sb(f"A2{i}" + tag, [64, 512], F32) for i in range(2)]
    TF2 = [k.sb(f"TF2{i}" + tag, [64, 512], F32) for i in range(2)]
    TI2 = [k.sb(f"TI2{i}" + tag, [64, 512], I32) for i in range(2)]
    H1 = [k.sb(f"H1{i}" + tag, [64, 512], F32) for i in range(2)]
    H2 = [k.sb(f"H2{i}" + tag, [64, 512], F32) for i in range(2)]
    R = [k.sb(f"R{i}" + tag, [64, 512], F32) for i in range(2)]
    J = [k.sb(f"J{i}" + tag, [64, 512], F32) for i in range(2)]
    HB = [k.sb(f"HB{i}" + tag, [64, 512], BF16) for i in range(2)]
    p1 = [k.ps(f"p1{i}" + tag, [64, 512], F32) for i in range(nps)]
    p2 = [k.ps(f"p2{i}" + tag, [64, 512], F32) for i in range(nps)]
    p3 = [k.ps(f"p3{i}" + tag, [64, 512], F32) for i in range(nps)]
    def stage1(ch):
        c0 = ch * 512; N = min(512, 2 * n - c0); i = ch % 2; ip = ch % nps
        F_, Fb_ = FT[i]; D_, Db_ = DC[i]
        P.dma("sync", F_[:, :N], ft[:, c0:c0 + N], [ftb], [Fb_], Fb_)
        P.dma("sync", D_[:, :N], dec[:, c0:c0 + N], [decb], [Db_], Db_)
        P.op("tensor", lambda e, ip=ip, N=N, F_=F_: e.matmul(p1[ip][0][:, :N], lhsT=W1[:], rhs=F_[:, :N], start=True, stop=True), [W1b, Fb_], [p1[ip][1]])
        sin_layer(P, p1[ip][0], p1[ip][1], V[:, 7:8], V[:, 5:6], Vb, A[i][0], A[i][1], TF[i][0], TF[i][1], TI[i][0], TI[i][1], H1[i][0], H1[i][1], N)

    def stage2(ch):
        c0 = ch * 512; N = min(512, 2 * n - c0); i = ch % 2; ip = ch % nps
        P.op("tensor", lambda e, i=i, ip=ip, N=N: e.matmul(p2[ip][0][:, :N], lhsT=W2[:], rhs=H1[i][0][:, :N], start=True, stop=True), [W2b, H1[i][1]], [p2[ip][1]])
        sin_layer(P, p2[ip][0], p2[ip][1], V[:, 8:9], V[:, 6:7], Vb, A2[i][0], A2[i][1], TF2[i][0], TF2[i][1], TI2[i][0], TI2[i][1], H2[i][0], H2[i][1], N)

    def stage3(ch):
        c0 = ch * 512; N = min(512, 2 * n - c0); i = ch % 2; ip = ch % nps
        D_, Db_ = DC[i]
        nb = max(0, min(N, n - c0))
        if nb > 0:
            P.op("tensor", lambda e, i=i, ip=ip, nb=nb: e.matmul(p3[ip][0][:, 0:nb], lhsT=W3[:, 0, :], rhs=H2[i][0][:, 0:nb], start=True, stop=True), [W3b, H2[i][1]], [p3[ip][1]])
        if nb < N:
            P.op("tensor", lambda e, i=i, ip=ip, nb=nb, N=N: e.matmul(p3[ip][0][:, nb:N], lhsT=W3[:, 1, :], rhs=H2[i][0][:, nb:N], start=True, stop=True), [W3b, H2[i][1]], [p3[ip][1]])
        R_, Rb_ = R[i]
        P.op("scalar", lambda e, ip=ip, N=N, R_=R_: e.copy(out=R_[:, :N], in_=p3[ip][0][:, :N]), [p3[ip][1]], [Rb_])
        P.op("vector", lambda e, N=N, R_=R_, D_=D_: e.tensor_tensor(out=R_[:, :N], in0=R_[:, :N], in1=D_[:, :N], op=ALU.mult), [Rb_, Db_], [Rb_])
        P.op("scalar", lambda e, i=i, N=N, R_=R_, ch=ch: e.activation(out=J[i][0][:, :N], in_=R_[:, :N], func=AF.Abs, accum_out=SA[:, ch:ch + 1]), [Rb_], [J[i][1], SAb.part(ch)])
        P.op("vector", lambda e, i=i, N=N, R_=R_: e.tensor_copy(out=HB[i][0][:, :N], in_=R_[:, :N]), [Rb_], [HB[i][1]])
        if c0 == 0:
            P.op("vector", lambda e, R_=R_: e.tensor_copy(out=T0[:, 0:1], in_=R_[:, 0:1]), [Rb_], [T0b.part(0)])
        if c0 <= n < c0 + N:
            P.op("vector", lambda e, R_=R_, o=n - c0: e.tensor_copy(out=T0[:, 1:2], in_=R_[:, o:o + 1]), [Rb_], [T0b.part(1)])
        P.dma("gpsimd", Hd[:, c0:c0 + N], HB[i][0][:, :N], [HB[i][1]], [Hdb.part(ch)], HB[i][1])

    for s_ in range(nch + 2):
        if 0 <= s_ - 2 < nch:
            stage3(s_ - 2)
        if 0 <= s_ - 1 < nch:
            stage2(s_ - 1)
        if s_ < nch:
            stage1(s_)
        yield
    P.op("vector", lambda e: e.tensor_reduce(out=SA[:, nch:nch + 1], in_=SA[:, 0:nch], axis=AX.X, op=ALU.add), [SAb], [SAb])
    P.op("vector", lambda e: e.tensor_tensor(out=T0[:, 2:3], in0=T0[:, 0:1], in1=T0[:, 1:2], op=ALU.add), [T0b], [T0b])
    P.op("vector", lambda e: e.scalar_tensor_tensor(out=T0[:, 2:3], in0=SA[:, nch:nch + 1], scalar=V[:, 4:5], in1=T0[:, 2:3], op0=ALU.mult, op1=ALU.add),
         [SAb, Vb, T0b], [T0b])
    tb, tbb = k.sb("tb" + tag, [64, 2], BF16)
    P.op("vector", lambda e: e.tensor_copy(out=tb[:, 0:1], in_=T0[:, 2:3]), [T0b], [tbb])
    P.dma("gpsimd", bass.AP(Hd.tensor, Hd.offset + n, [[2 * n, 64], [1, 1]]), tb[:, 0:1], [tbb], [Hdb], tbb, allow_slow_non_contiguous=True)
    P.dma("gpsimd", col_ap(hsum, 0, 64), SA[:, nch:nch + 1], [SAb], [hsumb], SAb)


def hyconv_a_phase(k, n, zc="zc", cw="hcw", cb="hcb", hsum="hsum", identf="ident", identb="identb", Vt="Vt_s", X0t="X0t_s"):
    nc, P = k.nc, k.P
    zc, zcb = k.d(zc); cw, cwb = k.d(cw); cb, cbb = k.d(cb); hsum, hsumb = k.d(hsum); identb, identbb = k.d(identb); Vt, Vtb = k.d(Vt); X0t, X0tb = k.d(X0t)
    NB = n // 128
    CW, CWb = k.sb("CW", [128, 3, 4], F32)
    for b in range(2):
        for part in range(3):
            for tap in range(3):
                P.dma("sync", CW[b * 64:(b + 1) * 64, part, tap:tap + 1], col_ap(cw, (tap * 3 + part) * 64, 64), [cwb], [CWb], CWb)
            P.dma("sync", CW[b * 64:(b + 1) * 64, part, 3:4], col_ap(cb, part * 64, 64), [cbb], [CWb], CWb)
    rS, rSb = k.sb("rS", [128, 2], F32)
    for b in range(2):
        P.dma("sync", rS[b * 64:(b + 1) * 64, 0:1], col_ap(hsum, 0, 64), [hsumb], [rSb], rSb)
    P.op("vector", lambda e: e.reciprocal(out=rS[:, 1:2], in_=rS[:, 0:1]), [rSb], [rSb])
    P.op("vector", lambda e: e.tensor_scalar(out=CW[:, 0, :], in0=CW[:, 0, :], scalar1=rS[:, 1:2], scalar2=None, op0=ALU.mult), [CWb, rSb], [CWb])
    idb, idbb = k.sb("idb", [128, 128], BF16)
    P.dma("sync", idb[:], identb[:, :], [identbb], [idbb], idbb)
    TC = min(1024, n)
    Z = [[k.sb(f"Z{i}_{p}", [128, TC + 2], F32) for p in range(3)] for i in range(2)]
    C = [k.sb(f"C{p}", [128, TC], F32) for p in range(3)]
    VX, VXb = k.sb("VX", [128, TC], BF16)
    X0, X0b = k.sb("X0", [128, TC], BF16)
    VT, VTb = k.sb("VT", [128, 128, NB], BF16)
    XT, XTb = k.sb("XT", [128, 128, NB], BF16)
    pt = [k.ps(f"pt{i}", [128, 4, 128], BF16) for i in range(4)]
    cnt = 0
    for ci in range(n // TC):
        t0 = ci * TC
        for p in range(3):
            Zp, Zpb = Z[ci % 2][p]
            P.dma("gpsimd", Zp[:], zc[p, :, t0:t0 + TC + 2], [zcb], [Zpb], Zpb)
            Cp, Cpb = C[p]
            eng = "vector"
            P.op(eng, lambda e, Cp=Cp, Zp=Zp, p=p: e.tensor_scalar(out=Cp[:], in0=Zp[:, 1:TC + 1], scalar1=CW[:, p, 1:2], scalar2=CW[:, p, 3:4], op0=ALU.mult, op1=ALU.add),
                 [Zpb, CWb], [Cpb])
            P.op(eng, lambda e, Cp=Cp, Zp=Zp, p=p: e.scalar_tensor_tensor(out=Cp[:], in0=Zp[:, 0:TC], scalar=CW[:, p, 0:1], in1=Cp[:], op0=ALU.mult, op1=ALU.add),
                 [Zpb, CWb, Cpb], [Cpb])
            P.op(eng, lambda e, Cp=Cp, Zp=Zp, p=p: e.scalar_tensor_tensor(out=Cp[:], in0=Zp[:, 2:TC + 2], scalar=CW[:, p, 2:3], in1=Cp[:], op0=ALU.mult, op1=ALU.add),
                 [Zpb, CWb, Cpb], [Cpb])
        nbk = TC // 128
        P.op("vector", lambda e, nbk=nbk: e.tensor_tensor(out=VX[:].rearrange("p (k j) -> p k j", k=nbk)[:, :, ::-1], in0=C[2][0][:].rearrange("p (k j) -> p k j", k=nbk),
                                                        in1=C[1][0][:].rearrange("p (k j) -> p k j", k=nbk), op=ALU.mult), [C[2][1], C[1][1]], [VXb])
        P.op("gpsimd", lambda e: e.tensor_copy(out=X0[:], in_=C[0][0][:]), [C[0][1]], [X0b])
        for (src, srcb, dst, dstb) in ((VX, VXb, VT, VTb), (X0, X0b, XT, XTb)):
            for b4 in range(0, nbk, 4):
                nb4 = min(4, nbk - b4)
                PT, PTb = pt[cnt % 4]; cnt += 1
                for q in range(nb4):
                    P.op("tensor", lambda e, PT=PT, src=src, q=q, b4=b4: e.transpose(out=PT[:, q, :], in_=src[:, (b4 + q) * 128:(b4 + q + 1) * 128], identity=idb[:]),
                         [srcb, idbb], [PTb])
                blk0 = t0 // 128 + b4
                P.op("scalar", lambda e, PT=PT, dst=dst, blk0=blk0, nb4=nb4: e.copy(out=dst[:, :, blk0:blk0 + nb4], in_=PT[:, 0:nb4, :].rearrange("p k c -> p c k")),
                     [PTb], [dstb])
    P.dma("gpsimd", Vt[:, :, :], VT[:], [VTb], [Vtb], VTb)
    P.dma("gpsimd", X0t[:, :, :], XT[:], [XTb], [X0tb], XTb)


def hyconv_b_phase(k, n, Hd="Hd", Vt="Vt_s", X0t="X0t_s", hyo="hyo"):
    nc, P = k.nc, k.P
    Hd, Hdb = k.d(Hd); Vt, Vtb = k.d(Vt); X0t, X0tb = k.d(X0t); hyo, hyob = k.d(hyo)
    NB = n // 128
    VT, VTb = k.sb("VT", [128, 128, NB], BF16)
    XT, XTb = k.sb("XT", [128, 128, NB], BF16)
    P.dma("sync", VT[:], Vt[:, :, :], [Vtb], [VTb], VTb)
    P.dma("sync", XT[:], X0t[:, :, :], [X0tb], [XTb], XTb)
    OUT, OUTb = k.sb("OUT", [128, 2 * NB, 64], BF16)
    ds = list(range(-(NB - 1), NB))
    GD = 64
    groups = [ds[i:i + GD] for i in range(0, len(ds), GD)]
    gz = [g for g in groups if 0 in g][0]
    groups = [gz] + [g for g in groups if g is not gz]
    GW = GD * 128 + 128
    G = [k.sb(f"G{i}", [128, GW], BF16) for i in range(4)]
    YS = [k.sb(f"YS{i}", [128, 2, NB], F32) for i in range(2)]
    py = [k.ps(f"py{i}", [128, 2, NB], F32) for i in range(2)]
    gi = 0
    for c in range(64):
        PY, PYb = py[c % 2]
        nmm = len(ds)
        done = 0
        for grp in groups:
            Gt, Gtb = G[gi % 4]; gi += 1
            u0 = n + 128 * grp[0] - 127
            width = 128 * (len(grp) - 1) + 128
            src = bass.AP(Hd.tensor, Hd.offset + c * 2 * n + u0, [[1, 128], [1, width]])
            P.dma("sync", Gt[:, 0:width], src, [Hdb], [Gtb], Gtb)
            order = ([0] + [d for d in grp if d != 0]) if 0 in grp else grp
            for d in order:
                a_lo, a_hi = max(0, d), min(NB, NB + d)
                off = 128 * (d - grp[0])
                done += 1
                P.op("tensor", lambda e, PY=PY, Gt=Gt, off=off, a_lo=a_lo, a_hi=a_hi, d=d, c=c, first=(done == 1), last=(done == nmm):
                     e.matmul(PY[:, :, a_lo:a_hi], lhsT=Gt[:, off:off + 128], rhs=VT[:, c:c + 65:64, a_lo - d:a_hi - d], start=first, stop=last),
                     [Gtb, VTb], [PYb])
        Y, Yb = YS[c % 2]
        P.op("scalar", lambda e, Y=Y, PY=PY: e.copy(out=Y[:], in_=PY[:]), [PYb], [Yb])
        eng = "vector" if c % 2 == 0 else "gpsimd"
        P.op(eng, lambda e, Y=Y, c=c: e.tensor_tensor(out=OUT[:, :, c].rearrange("p (b a) -> p b a", b=2), in0=Y[:], in1=XT[:, c:c + 65:64, :], op=ALU.mult),
             [Yb, XTb], [OUTb])
    AB_ = max(1, min(4, NB))
    for b in range(2):
        for a0 in range(0, NB, AB_):
            dst = bass.AP(hyo.tensor, hyo.offset + b * n * 64 + a0 * 128 * 64, [[64, 128], [128 * 64, AB_], [1, 64]])
            P.dma("gpsimd", dst, OUT[:, b * NB + a0:b * NB + a0 + AB_, :], [OUTb], [hyob.part((b, a0))], OUTb)


N_SEQ = 16384
N_CTX = 256


def build_A():
    pr = Program()
    pa_drams(pr, 0)
    pr.phase(cast_phase, "w32", "w_bf", D, 3584)
    pr.phase(proj_phase)
    return pr.nc


def hy_drams(pr, n, sfx):
    pr.dram("ft" + sfx, [33, 2 * n], F32, "ExternalInput")
    pr.dram("dec" + sfx, [64, 2 * n], F32, "ExternalInput")
    pr.dram("zc" + sfx, [3, 128, n + 2], BF16, "ExternalInput")
    pr.dram("Hd" + sfx, [64, 2 * n], BF16)
    pr.dram("hsum" + sfx, [64], F32)
    pr.dram("Vt_s" + sfx, [128, 128, n // 128], BF16)
    pr.dram("X0t_s" + sfx, [128, 128, n // 128], BF16)
    pr.dram("hyo" + sfx, [2, n, 64], BF16, "ExternalOutput")


def attn_filter_phase(k):
    ga = attn_gen(k, npst=3, npso=2)
    gf = filter_gen(k, N_SEQ, tag="f", nps=1)
    alive = [True, True]
    while any(alive):
        for i, (gen, steps) in enumerate(((ga, 1), (gf, 2))):
            for _ in range(steps):
                if alive[i]:
                    try:
                        next(gen)
                    except StopIteration:
                        alive[i] = False


def build_B():
    pr = Program()
    pb_drams(pr)
    pr.dram("hw1", [33, 64], F32, "ExternalInput"); pr.dram("hw2", [64, 64], F32, "ExternalInput")
    pr.dram("hw3b", [64, 64], F32, "ExternalInput"); pr.dram("hw3f", [64, 64], F32, "ExternalInput")
    for nm in ("hb1", "hf1", "hb2", "hf2", "hbias"):
        pr.dram(nm, [64], F32, "ExternalInput")
    pr.dram("hcw", [3, 3, 64], F32, "ExternalInput"); pr.dram("hcb", [3, 64], F32, "ExternalInput")
    hy_drams(pr, N_SEQ, "")
    hy_drams(pr, N_CTX, "c")
    pr.phase(attn_filter_phase)
    pr.phase(pool_phase)
    for n, s in ((N_SEQ, ""), (N_CTX, "c")):
        if n == N_CTX:
            pr.phase(filter_phase, n, ft="ft" + s, dec="dec" + s, Hd="Hd" + s, hsum="hsum" + s)
        pr.phase(hyconv_a_phase, n, zc="zc" + s, hsum="hsum" + s, Vt="Vt_s" + s, X0t="X0t_s" + s)
        pr.phase(hyconv_b_phase, n, Hd="Hd" + s, Vt="Vt_s" + s, X0t="X0t_s" + s, hyo="hyo" + s)
    return pr.nc


def build_C():
    pr = Program()
    pc_drams(pr, x1_external=False)
    pr.phase(cast_phase, "wo32", "wo_bf", D, D)
    pr.phase(outproj_phase, extra_casts=[("wu32", "wu_bf", D, HID), ("wd32", "wd_bf", HID, D)])
    pr.phase(mlp_phase)
    return pr.nc


def build_CA():
    pr = Program()
    pc_drams(pr, x1_external=False)
    pr.dram("w32", [D, 3584], F32, "ExternalInput")
    pr.dram("w_bf", [D, 3584], BF16)
    for nm in ("modx_n", "modc_n"):
        pr.dram(nm, [6 * D], F32, "ExternalInput")
    pr.dram("gpre", [D], F32, "ExternalInput")
    pr.dram("ropeC", [128, NTOK], F32, "ExternalInput")
    pr.dram("ropeS", [128, NTOK], F32, "ExternalInput")
    pr.dram("prot", [128, 128], BF16, "ExternalInput")
    pr.dram("qT", [128, 8, NTOK], BF16, "ExternalOutput")
    pr.dram("kT", [128, 2, NTOK], BF16, "ExternalOutput")
    pr.dram("v", [NTOK, 256], BF16, "ExternalOutput")
    pr.dram("zT", [1536, NTOK], BF16, "ExternalOutput")
    pr.dram("pT", [512, NTOK], BF16, "ExternalOutput")
    pr.phase(cast_phase, "wo32", "wo_bf", D, D)
    pr.phase(outproj_phase, extra_casts=[("wu32", "wu_bf", D, HID), ("wd32", "wd_bf", HID, D), ("w32", "w_bf", D, 3584)])
    pr.phase(mlp_phase)
    pr.phase(proj_phase, xin="x2", modx="modx_n", modc="modc_n")
    return pr.nc


def kernel(**inputs):
    f32 = lambda name: np.ascontiguousarray(np.asarray(inputs[name], np.float32))
    x = f32("x").copy()
    ctx = f32("ctx").copy()
    c3 = np.concatenate([f32("c"), f32("c_ctx")[None]], 0)
    w_mod, b_mod = f32("w_mod"), f32("b_mod")
    cores = list(range(8))
    res = run_bass_kernel_spmd(build_pm(), [dict(c3=c3, wm=np.ascontiguousarray(w_mod[:, :, r * NCOL:(r + 1) * NCOL]),
                                                 bm=np.ascontiguousarray(b_mod[:, r * NCOL:(r + 1) * NCOL])) for r in cores], core_ids=cores)
    mod = np.concatenate([res.results[r]["mod"] for r in cores], axis=2)
    del w_mod
    ncA, ncB, ncC, ncCA = build_A(), build_B(), build_C(), build_CA()
    cols = w_in_cols()
    ident = np.eye(128, dtype=np.float32)
    identb = np.eye(128).astype(BF)
    prot = rot_perm()
    ropes = [rope_tables(j * CHUNK) for j in range(4)]
    masks = [attn_masks(j) for j in range(4)]
    invx = [pool_inv_counts(j * CHUNK, CHUNK, N_SEQ) for j in range(4)]
    invc = pool_inv_counts(0, N_CTX, N_CTX)
    tabs = [hyena_tables(N_SEQ, r) for r in cores]
    tabc = [hyena_tables(N_CTX, r) for r in cores]
    A = None
    for l in range(4):
        g = lambda name, l=l: f32(name)[l]
        xins = [np.concatenate([x[r // 4, (r % 4) * CHUNK:(r % 4 + 1) * CHUNK], ctx[r // 4]], 0) for r in cores]
        if A is None:
            w32 = np.ascontiguousarray(g("w_in")[:, cols])
            ims = [dict(xin=xins[r], w32=w32, modx=mod[l, r // 4], modc=mod[l, 2], gpre=g("g_pre_mix"), ropeC=ropes[r % 4][0], ropeS=ropes[r % 4][1],
                        prot=prot, ident=ident) for r in cores]
            A = run_bass_kernel_spmd(ncA, ims, core_ids=cores).results
            del ims
        hw = g("hy_conv_w"); hb = g("hy_conv_b"); w3 = g("hy_w3"); hbias = g("hy_bias")
        ims = []
        for r in cores:
            b, j = r // 4, r % 4
            grp = [4 * b + i for i in range(4)]
            kTh = np.concatenate([halo_cat([A[q]["kT"][:, :, :CHUNK] for q in grp], j, 2, 128, None), A[r]["kT"][:, :, CHUNK:]], 2)
            vh = np.concatenate([halo_cat([A[q]["v"][:CHUNK] for q in grp], j, 0, 128, None), A[r]["v"][CHUNK:]], 0)
            pTh = halo_cat([A[q]["pT"][:, :CHUNK] for q in grp], j, 1, 8, None)
            pTc = np.concatenate([np.zeros((512, 8), BF), A[r]["pT"][:, CHUNK:], np.zeros((512, 8), BF)], 1)
            zc = np.zeros((3, 128, N_SEQ + 2), BF)
            zcc = np.zeros((3, 128, N_CTX + 2), BF)
            for part in range(3):
                rows = slice(192 * r + part * 64, 192 * r + part * 64 + 64)
                for bb in range(2):
                    for jj in range(4):
                        zc[part, bb * 64:(bb + 1) * 64, 1 + jj * CHUNK:1 + (jj + 1) * CHUNK] = A[4 * bb + jj]["zT"][rows, :CHUNK]
                    zcc[part, bb * 64:(bb + 1) * 64, 1:1 + N_CTX] = A[4 * bb]["zT"][rows, CHUNK:]
            cw = np.stack([hw[:, part * 512 + r * 64: part * 512 + (r + 1) * 64] for part in range(3)], 1)
            cb = np.stack([hb[part * 512 + r * 64: part * 512 + (r + 1) * 64] for part in range(3)], 0)
            ims.append(dict(qT=A[r]["qT"], kTh=np.ascontiguousarray(kTh), vh=np.ascontiguousarray(vh), masks=masks[j], sink=g("attn_sink"), identb=identb,
                            pTh=np.ascontiguousarray(pTh), pTc=pTc, invx=invx[j], invc=invc, pool_w=g("pool_w"), pool_scale=g("pool_scale"),
                            hw1=g("hy_w1"), hw2=g("hy_w2"), hw3f=np.ascontiguousarray(w3[:, r * 64:(r + 1) * 64]),
                            hw3b=np.ascontiguousarray(w3[:, 512 + r * 64:512 + (r + 1) * 64]), hb1=g("hy_b1"), hf1=g("hy_freq1"), hb2=g("hy_b2"), hf2=g("hy_freq2"),
                            hbias=np.ascontiguousarray(hbias[r * 64:(r + 1) * 64]), hcw=np.ascontiguousarray(cw), hcb=np.ascontiguousarray(cb),
                            ft=tabs[r][0], dec=tabs[r][1], zc=zc, ftc=tabc[r][0], decc=tabc[r][1], zcc=zcc))
        A = None
        Bo = run_bass_kernel_spmd(ncB, ims, core_ids=cores).results
        del ims
        ims = []
        for r in cores:
            b, j = r // 4, r % 4
            hy = np.concatenate([np.concatenate([Bo[q]["hyo"][b, j * CHUNK:(j + 1) * CHUNK] for q in cores], 1),
                                 np.concatenate([Bo[q]["hyoc"][b] for q in cores], 1)], 0)
            d = dict(attn=Bo[r]["attn"], hy=np.ascontiguousarray(hy), po=Bo[r]["po"], xin=xins[r], wo32=g("w_out"), wu32=g("w_up"), wd32=g("w_down"),
                     gbr=g("g_branch"), gpost=g("g_post_mix"), gpre2=g("g_pre_mlp"), gpost2=g("g_post_mlp"), modx=mod[l, b], modc=mod[l, 2], ident=ident)
            if l < 3:
                d.update(w32=np.ascontiguousarray(f32("w_in")[l + 1][:, cols]), modx_n=mod[l + 1, b], modc_n=mod[l + 1, 2], gpre=f32("g_pre_mix")[l + 1],
                         ropeC=ropes[j][0], ropeS=ropes[j][1], prot=prot)
            ims.append(d)
        del Bo
        Co = run_bass_kernel_spmd(ncCA if l < 3 else ncC, ims, core_ids=cores).results
        del ims
        for r in cores:
            b, j = r // 4, r % 4
            x[b, j * CHUNK:(j + 1) * CHUNK] = Co[r]["x2"][:CHUNK]
            if j == 0:
                ctx[b] = Co[r]["x2"][CHUNK:]
        A = Co if l < 3 else None
    return x
```
